# Optimizing a Trainium2 kernel written in Bass

```python
import math
import jax, jax.numpy as jnp
from jax import lax
import numpy as np

D_MODEL = 1024
BATCH = 4
SEQ = 8192
DEPTH = 4

N_A_LAYERS = DEPTH // 2
N_B_LAYERS = DEPTH - N_A_LAYERS
D_FF = 2816
A_HEADS = 8
A_DK = 128
A_DV = 128
CONV_K = 4
CHUNK = 64
A_QKV = A_HEADS * (2 * A_DK + A_DV)
A_PROJ = A_QKV + A_HEADS * A_DV + 2 * A_HEADS
B_Q_HEADS = 16
B_KV_HEADS = 4
B_GROUP = B_Q_HEADS // B_KV_HEADS
B_DH = 64
WINDOW = 128
BLK = WINDOW
EPS = 1e-6

kernel_name = "yoco_deltanet_swa_sink_macaron"


def rmsnorm(x, g):
    x32 = x.astype(jnp.float32)
    y = x32 * lax.rsqrt(jnp.mean(x32 * x32, axis=-1, keepdims=True) + EPS)
    return (y * g.astype(jnp.float32)).astype(x.dtype)


def swiglu(h, w_gate, w_up, w_down):
    return (jax.nn.silu(h @ w_gate) * (h @ w_up)) @ w_down


def l2norm(x):
    return x * lax.rsqrt(jnp.sum(x * x, axis=-1, keepdims=True) + EPS)


def causal_conv_silu(x, w):
    k = w.shape[0]
    y = lax.conv_general_dilated(x, w[:, None, :].astype(x.dtype), window_strides=(1,),
                                 padding=[(k - 1, 0)], dimension_numbers=('NWC', 'WIO', 'NWC'),
                                 feature_group_count=x.shape[-1])
    return jax.nn.silu(y)


def gated_delta_rule(q, k, v, g, beta):
    b, l, h, dk = q.shape
    dv = v.shape[-1]
    n = l // CHUNK

    def to_chunks(t):
        t = t.reshape((b, n, CHUNK, h) + t.shape[3:])
        return jnp.moveaxis(t, 3, 2)

    q = to_chunks(l2norm(q) * dk ** -0.5)
    k = to_chunks(l2norm(k))
    v = to_chunks(v)
    beta = to_chunks(beta)
    g = jnp.cumsum(to_chunks(g), axis=-1)
    idx = jnp.arange(CHUNK)
    lower = idx[:, None] >= idx[None, :]
    strict = idx[:, None] > idx[None, :]
    decay = jnp.exp(jnp.where(lower, g[..., :, None] - g[..., None, :], -jnp.inf))
    k_beta = k * beta[..., None]
    a_mat = jnp.where(strict, jnp.einsum('bnhik,bnhjk->bnhij', k_beta, k) * decay, 0.0)
    rhs = jnp.concatenate([v * beta[..., None], k_beta * jnp.exp(g)[..., None]], axis=-1)
    sol = lax.linalg.triangular_solve(a_mat + jnp.eye(CHUNK, dtype=a_mat.dtype), rhs,
                                      left_side=True, lower=True, unit_diagonal=True)
    u, w = sol[..., :dv], sol[..., dv:]
    qk = jnp.einsum('bnhik,bnhjk->bnhij', q, k) * decay
    q_dec = q * jnp.exp(g)[..., None]
    k_dec = k * jnp.exp(g[..., -1:] - g)[..., None]
    g_last = jnp.exp(g[..., -1])

    def step(state, inp):
        q_i, k_i, u_i, w_i, qk_i, gl_i = inp
        v_new = u_i - jnp.einsum('bhck,bhkv->bhcv', w_i, state)
        o_i = jnp.einsum('bhck,bhkv->bhcv', q_i, state) + jnp.einsum('bhij,bhjv->bhiv', qk_i, v_new)
        state = state * gl_i[..., None, None] + jnp.einsum('bhck,bhcv->bhkv', k_i, v_new)
        return state, o_i

    xs = tuple(jnp.moveaxis(t, 1, 0) for t in (q_dec, k_dec, u, w, qk, g_last))
    s0 = jnp.zeros((b, h, dk, dv), jnp.float32)
    _, o = lax.scan(step, s0, xs)
    return o.transpose(1, 0, 3, 2, 4).reshape(b, l, h, dv)


def deltanet_mixer(h, w_in, conv_w, a_log, dt_bias, o_norm, w_out):
    b, l, _ = h.shape
    f32 = jnp.float32
    proj = h @ w_in
    qkv = causal_conv_silu(proj[..., :A_QKV], conv_w)
    q, k, v = jnp.split(qkv, [A_HEADS * A_DK, 2 * A_HEADS * A_DK], axis=-1)
    gate = proj[..., A_QKV:A_QKV + A_HEADS * A_DV].reshape(b, l, A_HEADS, A_DV)
    b_raw = proj[..., A_QKV + A_HEADS * A_DV:A_QKV + A_HEADS * A_DV + A_HEADS]
    a_raw = proj[..., A_PROJ - A_HEADS:]
    beta = jax.nn.sigmoid(b_raw.astype(f32))
    g = -jnp.exp(a_log.astype(f32)) * jax.nn.softplus(a_raw.astype(f32) + dt_bias.astype(f32))
    o = gated_delta_rule(q.reshape(b, l, A_HEADS, A_DK).astype(f32),
                         k.reshape(b, l, A_HEADS, A_DK).astype(f32),
                         v.reshape(b, l, A_HEADS, A_DV).astype(f32), g, beta)
    o = rmsnorm(o, o_norm) * jax.nn.silu(gate.astype(f32))
    return o.reshape(b, l, A_HEADS * A_DV).astype(h.dtype) @ w_out


def shared_kv_blocks(x, kv_norm, kv_w, kv_b):
    b, l, _ = x.shape
    nb = l // BLK
    kv = rmsnorm(x, kv_norm) @ kv_w + kv_b
    k, v = jnp.split(kv, 2, axis=-1)

    def band(t):
        t = t.reshape(b, nb, BLK, B_KV_HEADS, B_DH)
        prev = jnp.concatenate([jnp.zeros_like(t[:, :1]), t[:, :-1]], axis=1)
        return jnp.concatenate([prev, t], axis=2)

    return band(k), band(v)


def window_mask(nb):
    qi = jnp.arange(BLK)[:, None]
    ki = jnp.arange(2 * BLK)[None, :]
    in_win = (ki > qi) & (ki <= qi + WINDOW)
    real = (jnp.arange(nb)[:, None, None] > 0) | (ki >= BLK)[None]
    return in_win[None] & real


def swa_sink_mixer(h, w_q, b_q, sinks, w_out, k_blk, v_blk):
    b, l, _ = h.shape
    nb = l // BLK
    q = (h @ w_q + b_q).reshape(b, nb, BLK, B_KV_HEADS, B_GROUP, B_DH)
    s = jnp.einsum('bnqhgd,bnkhd->bnhgqk', q, k_blk).astype(jnp.float32) * B_DH ** -0.5
    s = jnp.where(window_mask(nb)[None, :, None, None], s, -jnp.inf)
    sink = sinks.astype(jnp.float32).reshape(1, 1, B_KV_HEADS, B_GROUP, 1, 1)
    m = jnp.maximum(jnp.max(s, axis=-1, keepdims=True), sink)
    p = jnp.exp(s - m)
    p = p / (jnp.sum(p, axis=-1, keepdims=True) + jnp.exp(sink - m))
    o = jnp.einsum('bnhgqk,bnkhd->bnqhgd', p.astype(v_blk.dtype), v_blk)
    return o.reshape(b, l, B_Q_HEADS * B_DH) @ w_out


def setup_inputs(seed: int = 0) -> dict:
    key = jax.random.key(seed)
    ks = jax.random.split(key, 20)
    f32 = jnp.float32

    def nrm(k, shape, fan_in):
        return jax.random.normal(k, shape, f32) * fan_in ** -0.5

    def gain(k, shape):
        return 1.0 + 0.02 * jax.random.normal(k, shape, f32)

    x = jax.random.normal(ks[0], (BATCH, SEQ, D_MODEL), f32)
    ffn_norm = gain(ks[1], (DEPTH, 2, D_MODEL))
    ffn_w_gate = nrm(ks[2], (DEPTH, 2, D_MODEL, D_FF), D_MODEL)
    ffn_w_up = nrm(ks[3], (DEPTH, 2, D_MODEL, D_FF), D_MODEL)
    ffn_w_down = nrm(ks[4], (DEPTH, 2, D_FF, D_MODEL), D_FF)
    mix_norm = gain(ks[5], (DEPTH, D_MODEL))
    a_w_in = nrm(ks[6], (N_A_LAYERS, D_MODEL, A_PROJ), D_MODEL)
    a_conv = nrm(ks[7], (N_A_LAYERS, CONV_K, A_QKV), CONV_K)
    a_A_log = jnp.log(jax.random.uniform(ks[8], (N_A_LAYERS, A_HEADS), f32, 1.0, 16.0))
    dt = jnp.exp(jax.random.uniform(ks[9], (N_A_LAYERS, A_HEADS), f32, math.log(1e-3), math.log(1e-1)))
    a_dt_bias = dt + jnp.log(-jnp.expm1(-dt))
    a_o_norm = gain(ks[10], (N_A_LAYERS, A_DV))
    a_w_out = nrm(ks[11], (N_A_LAYERS, A_HEADS * A_DV, D_MODEL), A_HEADS * A_DV)
    kv_norm = gain(ks[12], (D_MODEL,))
    kv_w = nrm(ks[13], (D_MODEL, 2 * B_KV_HEADS * B_DH), D_MODEL)
    kv_b = 0.02 * jax.random.normal(ks[14], (2 * B_KV_HEADS * B_DH,), f32)
    b_w_q = nrm(ks[15], (N_B_LAYERS, D_MODEL, B_Q_HEADS * B_DH), D_MODEL)
    b_b_q = 0.02 * jax.random.normal(ks[16], (N_B_LAYERS, B_Q_HEADS * B_DH), f32)
    b_sinks = 0.5 * jax.random.normal(ks[17], (N_B_LAYERS, B_Q_HEADS), f32)
    b_w_out = nrm(ks[18], (N_B_LAYERS, B_Q_HEADS * B_DH, D_MODEL), B_Q_HEADS * B_DH)
    final_norm = gain(ks[19], (D_MODEL,))
    return {"x": x, "ffn_norm": ffn_norm, "ffn_w_gate": ffn_w_gate, "ffn_w_up": ffn_w_up,
            "ffn_w_down": ffn_w_down, "mix_norm": mix_norm, "a_w_in": a_w_in, "a_conv": a_conv,
            "a_A_log": a_A_log, "a_dt_bias": a_dt_bias, "a_o_norm": a_o_norm, "a_w_out": a_w_out,
            "kv_norm": kv_norm, "kv_w": kv_w, "kv_b": kv_b, "b_w_q": b_w_q, "b_b_q": b_b_q,
            "b_sinks": b_sinks, "b_w_out": b_w_out, "final_norm": final_norm}


def reference(x, ffn_norm, ffn_w_gate, ffn_w_up, ffn_w_down, mix_norm, a_w_in, a_conv, a_A_log,
              a_dt_bias, a_o_norm, a_w_out, kv_norm, kv_w, kv_b, b_w_q, b_b_q, b_sinks, b_w_out,
              final_norm):
    k_blk = v_blk = None
    for i in range(DEPTH):
        x = x + 0.5 * swiglu(rmsnorm(x, ffn_norm[i, 0]), ffn_w_gate[i, 0], ffn_w_up[i, 0], ffn_w_down[i, 0])
        h = rmsnorm(x, mix_norm[i])
        if i < N_A_LAYERS:
            x = x + deltanet_mixer(h, a_w_in[i], a_conv[i], a_A_log[i], a_dt_bias[i], a_o_norm[i], a_w_out[i])
        else:
            j = i - N_A_LAYERS
            x = x + swa_sink_mixer(h, b_w_q[j], b_b_q[j], b_sinks[j], b_w_out[j], k_blk, v_blk)
        x = x + 0.5 * swiglu(rmsnorm(x, ffn_norm[i, 1]), ffn_w_gate[i, 1], ffn_w_up[i, 1], ffn_w_down[i, 1])
        if i == N_A_LAYERS - 1:
            k_blk, v_blk = shared_kv_blocks(x, kv_norm, kv_w, kv_b)
    return rmsnorm(x, final_norm)
```

```python
import contextlib
import numpy as np
import concourse.bass as bass
import concourse.mybir as mybir
from concourse.bass_utils import run_bass_kernel_spmd

F32 = mybir.dt.float32
BF16 = mybir.dt.bfloat16
AF = mybir.ActivationFunctionType
ALU = mybir.AluOpType
AX = mybir.AxisListType

D = 1024
DFF = 2816
NJ = DFF // 128
TOK = 4096
EPS = 1e-6
NCORES = 8


class Tok:
    __slots__ = ("sem", "val", "eng", "sem_key")

    def __init__(self, sem, val, eng):
        self.sem, self.val, self.eng = sem, val, eng


class BufState:
    def __init__(self):
        self.last_w = None
        self.reads = {}


class Prog:
    def __init__(self, nc):
        self.nc = nc
        self.es = contextlib.ExitStack()
        self.eng = {"pe": nc.tensor, "act": nc.scalar, "dve": nc.vector, "pool": nc.gpsimd, "sp": nc.sync}
        self.sem = {}
        self.cnt = {}
        for e in self.eng:
            self.sem[e] = self.es.enter_context(nc.semaphore("sem_" + e))
            self.cnt[e] = 0
        self.waited = {e: {} for e in self.eng}
        self.bufs = {}
        self.dsems = {}
        self.n_ins = 0
        self._uid = 0
        self.stack = [self.es]
        self.semkey_of = {}

    def push(self):
        es = contextlib.ExitStack()
        self.stack.append(es)
        return es

    def pop(self):
        self.barrier()
        self.stack.pop().close()

    def barrier(self):
        for e in self.eng:
            for o in self.eng:
                if o != e and self.cnt[o] > 0:
                    self._wait(e, Tok(self.sem[o], self.cnt[o], o))
            for k, d in self.dsems.items():
                if d[1] > 0:
                    t = Tok(d[0], d[1], "dma")
                    t.sem_key = k
                    self._wait(e, t)

    def sb(self, name, shape, dt):
        self._uid += 1
        t = self.stack[-1].enter_context(self.nc.sbuf_tensor(f"{name}_u{self._uid}", list(shape), dt))
        self.semkey_of[t.name] = name
        return t

    def ps(self, name, shape, dt):
        return self.stack[-1].enter_context(self.nc.psum_tensor(name, list(shape), dt))

    def key(self, x):
        if isinstance(x, str):
            return x
        if isinstance(x, tuple):
            return x[1]
        return getattr(x, 'tensor', x).name

    def st(self, k):
        s = self.bufs.get(k)
        if s is None:
            s = self.bufs[k] = BufState()
        return s

    def _wait(self, e, tok, war=False):
        if tok is None:
            return
        if tok.eng == e and (e == "pe" or war or e == "sp"):
            return
        sid = id(tok.sem)
        val = tok.val
        if tok.eng == "dma":
            val = max(val, self.dsems[tok.sem_key][1])
        if self.waited[e].get(sid, 0) >= val:
            return
        self.eng[e].wait_ge(tok.sem, val)
        self.waited[e][sid] = val

    def _deps(self, e, reads, writes):
        for r in reads:
            s = self.st(self.key(r))
            self._wait(e, s.last_w)
        for w in writes:
            s = self.st(self.key(w))
            self._wait(e, s.last_w)
            for t in s.reads.values():
                self._wait(e, t, war=True)

    def _commit(self, tok, reads, writes):
        for r in reads:
            s = self.st(self.key(r))
            s.reads[tok.eng if tok.eng != "dma" else id(tok.sem)] = tok
        for w in writes:
            s = self.st(self.key(w))
            s.last_w = tok
            s.reads = {}

    def op(self, e, fn, reads, writes):
        reads = [r for r in reads if r is not None and not isinstance(r, (int, float))]
        self._deps(e, reads, writes)
        ins = fn(self.eng[e])
        self.cnt[e] += 1
        ins.then_inc(self.sem[e], 1)
        tok = Tok(self.sem[e], self.cnt[e], e)
        self._commit(tok, reads, writes)
        self.n_ins += 1
        return ins

    def dma(self, q, out, in_, out_key=None, in_key=None, sem_key=None):
        ok = out_key or self.key(out)
        ik = in_key or self.key(in_)
        sb_side = out.tensor.name if out.tensor.name not in self.dram_names else in_.tensor.name
        sk = sem_key or self.semkey_of.get(sb_side, sb_side)
        if sk not in self.dsems:
            self.dsems[sk] = [self.es.enter_context(self.nc.semaphore("d_" + sk.replace(":", "_"))), 0]
        d = self.dsems[sk]
        self._deps(q, [ik], [ok])
        ins = self.eng[q].dma_start(out=out, in_=in_)
        d[1] += 16
        ins.then_inc(d[0], 16)
        tok = Tok(d[0], d[1], "dma")
        tok_key = sk
        tok.sem_key = tok_key
        self._commit(tok, [ik], [ok])
        self.n_ins += 1
        return ins

    dram_names = set()

    def dram(self, name, shape, dt, kind):
        self.dram_names.add(name)
        return self.nc.dram_tensor(name, list(shape), dt, kind=kind).ap()

    def mm(self, out, lhsT, rhs, start=True, stop=True, **kw):
        return self.op("pe", lambda e: e.matmul(out, lhsT, rhs, start=start, stop=stop, **kw), [lhsT, rhs], [out])

    def tr(self, out, in_, ident):
        return self.op("pe", lambda e: e.transpose(out, in_, ident), [in_, ident], [out])

    def act(self, out, in_, func, bias=None, scale=None, accum_out=None):
        kw = {}
        if bias is not None:
            kw["bias"] = bias
        if scale is not None:
            kw["scale"] = scale
        if accum_out is not None:
            kw["accum_out"] = accum_out
        wr = [out] + ([accum_out] if accum_out is not None else [])
        return self.op("act", lambda e: e.activation(out=out, in_=in_, func=func, **kw), [in_, bias, scale], wr)

    def tt(self, e, out, in0, in1, op):
        return self.op(e, lambda g: g.tensor_tensor(out=out, in0=in0, in1=in1, op=op), [in0, in1], [out])

    def ts(self, e, out, in0, s1, op0, s2=None, op1=None, accum_out=None):
        kw = {}
        if op1 is not None:
            kw["op1"] = op1
        if accum_out is not None:
            kw["accum_out"] = accum_out
        wr = [out] + ([accum_out] if accum_out is not None else [])
        return self.op(e, lambda g: g.tensor_scalar(out=out, in0=in0, scalar1=s1, scalar2=s2, op0=op0, **kw),
                       [in0, s1, s2], wr)

    def stt(self, e, out, in0, scalar, in1, op0, op1):
        return self.op(e, lambda g: g.scalar_tensor_tensor(out=out, in0=in0, scalar=scalar, in1=in1, op0=op0, op1=op1),
                       [in0, scalar, in1], [out])

    def copy(self, e, out, in_):
        if e == "act":
            return self.op(e, lambda g: g.copy(out=out, in_=in_), [in_], [out])
        return self.op(e, lambda g: g.tensor_copy(out=out, in_=in_), [in_], [out])

    def memset(self, e, ap, val):
        return self.op(e, lambda g: g.memset(ap, val), [], [ap])

    def reduce(self, e, out, in_, op, axis=AX.X):
        return self.op(e, lambda g: g.tensor_reduce(out=out, in_=in_, axis=axis, op=op), [in_], [out])

    def finish(self):
        self.barrier()
        while len(self.stack) > 1:
            self.stack.pop().close()
        self.es.close()


class Pool:
    def __init__(self, P, name, n, shape, dt, psum=False):
        self.t = [(P.ps if psum else P.sb)(f"{name}{i}", shape, dt) for i in range(n)]
        self.i = 0

    def next(self):
        t = self.t[self.i % len(self.t)]
        self.i += 1
        return t


SEQ = 8192
NT_ALL = SEQ // 128
NT_HALF = TOK // 128
A_PROJ = 4112
NEG = -30000.0
SWA_TILES = NT_HALF
SWA_LEVEL = 9


class Builder:
    def __init__(self, stages, n_all=SEQ, t_ffn=512):
        self.stages = stages
        self.TF = t_ffn
        self.nc = bass.Bass("TRN2", target_bir_lowering=False)
        self.P = Prog(self.nc)
        self.inputs = {}

    def din(self, name, shape, dt=F32):
        if name in self.inputs:
            return self.inputs[name]
        ap = self.P.dram(name, shape, dt, "ExternalInput")
        self.inputs[name] = ap
        return ap

    def const_sb(self, name, shape, dt=F32, q="sp"):
        d = self.din(name, shape)
        t = self.P.sb("c_" + name, shape, dt)
        self.P.dma(q if dt == F32 else "pool", t[:], d)
        return t

    def setup_common(self):
        P = self.P
        self.ident_bf = self.const_sb("ident", [128, 128], BF16)
        self.ident_f = P.sb("ident_f", [128, 128], F32)
        P.dma("sp", self.ident_f[:], self.inputs["ident"])
        self.bank = [P.ps(f"bank{i}", [128, 512], F32) for i in range(8)]
        self.gc_ffn = self.const_sb("ffn_gcols", [128, 64])
        self.gc_mix = self.const_sb("mix_gcols", [128, 32])
        self.flag = self.const_sb("flag", [128, 2])

    def rmsnorm_T(self, x_tok, gcol, hT, s, tr_bank):
        P = self.P
        junk = self.junk.next()
        ssq = self.small.next()
        P.act(junk[:], x_tok[:], AF.Square, accum_out=ssq[:, 0:1])
        P.ts("dve", ssq[:, 1:2], ssq[:, 0:1], 1.0 / D, ALU.mult, EPS, ALU.add)
        P.act(ssq[:, 3:4], ssq[:, 1:2], AF.Ln)
        P.act(ssq[:, 2:3], ssq[:, 3:4], AF.Exp, scale=-0.5)
        xn = self.xn.next()
        P.ts("dve", xn[:], x_tok[:], ssq[:, 2:3], ALU.mult)
        pst = tr_bank[:].bitcast(BF16)
        for c in range(8):
            P.tr(pst[:, c * 128:(c + 1) * 128], xn[:, c * 128:(c + 1) * 128], self.ident_bf[:])
        P.tt("dve", hT[:, :, s * 128:(s + 1) * 128], pst.rearrange("p (c t) -> p c t", c=8),
             gcol.unsqueeze(2).to_broadcast([128, 8, 128]), ALU.mult)

    def norm_pools(self):
        P = self.P
        self.junk = Pool(P, "junk", 1, [128, D], BF16)
        self.small = Pool(P, "small", 4, [128, 4], F32)
        self.xn = Pool(P, "xn", 1, [128, D], BF16)

    def ffn_pass(self, idx, src, dst, ntok, final=False):
        P = self.P
        P.push()
        TF = self.TF
        NS = TF // 128
        self.norm_pools()
        xtok = Pool(P, "xtok", 6, [128, D], F32)
        hTp = Pool(P, "hT", 1, [128, 8, TF], BF16)
        aTp = Pool(P, "aT", 1, [128, NJ, TF], BF16)
        sgp = Pool(P, "sg", 2, [128, 512], F32)
        wg = self.din(f"wg{idx}", [D, DFF])
        wu = self.din(f"wu{idx}", [D, DFF])
        wd = self.din(f"wd{idx}", [DFF, D])
        HJ = NJ // 2
        wgs = [P.sb(f"wg{g}", [128, 8, HJ * 128], BF16) for g in range(2)]
        wus = [P.sb(f"wu{g}", [128, 8, HJ * 128], BF16) for g in range(2)]
        wd_sb = P.sb("wdsb", [128, NJ, D], BF16)
        for g in range(2):
            for wsb, wdr in ((wgs[g], wg), (wus[g], wu)):
                for c in range(0, 8, 2):
                    P.dma("pool", wsb[:, c:c + 2, :],
                          wdr[c * 128:(c + 2) * 128, g * HJ * 128:(g + 1) * HJ * 128].rearrange("(c p) m -> p c m", p=128))
        for j0 in range(0, NJ, 2):
            P.dma("pool", wd_sb[:, j0:j0 + 2, :], wd[j0 * 128:(j0 + 2) * 128, :].rearrange("(j p) m -> p j m", p=128))
        gcol = self.gc_ffn[:, idx * 8:(idx + 1) * 8]
        if final:
            fg = self.const_sb("final_bc", [128, D])
        sk, dk = src.tensor.name, dst.tensor.name
        for t in range(ntok // TF):
            hT = hTp.next()
            xts = []
            for s in range(NS):
                xt = xtok.next()
                r0 = t * TF + s * 128
                P.dma("sp", xt[:], src[r0:r0 + 128, :], in_key=f"{sk}:{r0 // 128}")
                self.rmsnorm_T(xt, gcol, hT, s, self.bank[4 + (s % 2)])
                xts.append(xt)
            aT = aTp.next()
            for j in range(NJ):
                g, jj = j // HJ, j % HJ
                pg = self.bank[j % 2]
                pu = self.bank[2 + j % 2]
                for c in range(8):
                    P.mm(pg[:, :TF], wgs[g][:, c, jj * 128:(jj + 1) * 128], hT[:, c, :], start=(c == 0), stop=(c == 7))
                for c in range(8):
                    P.mm(pu[:, :TF], wus[g][:, c, jj * 128:(jj + 1) * 128], hT[:, c, :], start=(c == 0), stop=(c == 7))
                sg = sgp.next()
                P.act(sg[:, :TF], pg[:, :TF], AF.Silu)
                P.tt("dve", aT[:, j, :], sg[:, :TF], pu[:, :TF], ALU.mult)
            for s in range(NS):
                xt = xts[s]
                for hf in range(2):
                    py = self.bank[4 + 2 * (s % 2) + hf]
                    for j in range(NJ):
                        P.mm(py[:, :], aT[:, j, s * 128:(s + 1) * 128], wd_sb[:, j, hf * 512:(hf + 1) * 512],
                             start=(j == 0), stop=(j == NJ - 1))
                    P.stt("dve", xt[:, hf * 512:(hf + 1) * 512], py[:, :], 0.5, xt[:, hf * 512:(hf + 1) * 512],
                          ALU.mult, ALU.add)
                r0 = t * TF + s * 128
                if final:
                    junk = self.junk.next()
                    ssq = self.small.next()
                    P.act(junk[:], xt[:], AF.Square, accum_out=ssq[:, 0:1])
                    P.ts("dve", ssq[:, 1:2], ssq[:, 0:1], 1.0 / D, ALU.mult, EPS, ALU.add)
                    P.act(ssq[:, 3:4], ssq[:, 1:2], AF.Ln)
                    P.act(ssq[:, 2:3], ssq[:, 3:4], AF.Exp, scale=-0.5)
                    P.stt("dve", xt[:], xt[:], ssq[:, 2:3], fg[:], ALU.mult, ALU.mult)
                P.dma("sp", dst[r0:r0 + 128, :], xt[:], out_key=f"{dk}:{r0 // 128}")
        P.pop()

    def dn_pass(self, l, src, dst, ntiles):
        P = self.P
        P.push()
        bank = self.bank
        self.norm_pools()
        w_in_d = self.din(f"a_w_in{l}", [D, A_PROJ])
        w_out_d = self.din(f"a_w_out{l}", [D, D])
        w_in = P.sb("win", [128, 8, A_PROJ], BF16)
        w_out = P.sb("wout", [128, 8, D], BF16)
        for c in range(8):
            P.dma("pool", w_in[:, c, :], w_in_d[c * 128:(c + 1) * 128, :])
        for c in range(8):
            P.dma("pool", w_out[:, c, :], w_out_d[c * 128:(c + 1) * 128, :])
        convw = self.const_sb(f"convw{l}", [128, 24, 4])
        dtb = self.const_sb(f"dtb{l}", [128, 8])
        alog = self.const_sb(f"alog{l}", [128, 8])
        onorm = self.const_sb(f"onorm{l}", [128, 128])
        masks = self.const_sb("masks", [128, 3, 128])
        cm = self.const_sb("cmats", [128, 4, 128])
        ones = self.const_sb("ones", [128, 128])
        gcol = self.gc_mix[:, l * 8:(l + 1) * 8]
        negA = P.sb("negA", [128, 8], F32)
        P.act(negA[:], alog[:], AF.Exp)
        P.ts("dve", negA[:], negA[:], -1.0, ALU.mult)
        S_f = [P.sb(f"Sf{h}", [128, 128], F32) for h in range(8)]
        S_b = [P.sb(f"Sb{h}", [128, 128], BF16) for h in range(8)]
        for h in range(8):
            P.memset("dve", S_f[h][:], 0.0)
            P.memset("dve", S_b[h][:], 0.0)
        xc = P.sb("xc", [128, 24, 131], F32)
        P.memset("dve", xc[:], 0.0)
        xtm = Pool(P, "xtm", 2, [128, D], F32)
        hTp = Pool(P, "hTm", 2, [128, 8, 128], BF16)
        accp = Pool(P, "cacc", 2, [128, 4, 128], F32)
        dd = P.sb("dd", [128, 16, 128], F32)
        ctmp = P.sb("ctmp", [128, 128], F32)
        qkv_s = P.sb("qkvs", [128, 24, 128], BF16)
        qkv_t = P.sb("qkvt", [128, 24, 128], BF16)
        sq = dd[:]
        sc = P.sb("sc", [128, 16, 8], F32)
        (RN_Q, RN_K, BETA, G, GC, GLAST, EG, EK, CKB, CKBEG, CKDEC, CQDEC, TMP, TMP2, ORS, Z) = range(16)
        glbe = P.sb("glbe", [128, 2, 8], F32)
        names = ["kn", "kb", "qn", "qdec", "vb", "kbeg", "kdec"]
        big = P.sb("big", [128, 4096], BF16)
        sop = {n: big[:, k_ * 1024:(k_ + 1) * 1024].rearrange("p (h t) -> p h t", h=8)
               for k_, n in enumerate(names[:4])}
        for n in names[4:]:
            sop[n] = P.sb("so_" + n, [128, 8, 128], BF16)[:]
        diagG = dd[:, 0:8, :]
        dtarg = dd[:, 8:16, :]
        e1 = dtarg
        e2 = diagG
        DTs = P.sb("DTs", [128, 8, 128], F32)
        DTi = P.sb("DTi", [128, 8, 128], BF16)
        Ds = P.sb("Ds", [128, 8, 128], F32)
        fms = [P.sb(f"fm{q}", [128, 3, 128], BF16) for q in range(4)]
        qdecT = P.sb("qdecT", [128, 8, 128], BF16)
        qkT = P.sb("qkT", [128, 8, 128], BF16)
        Mxs = [P.sb(f"Mx{q}", [128, 128], F32) for q in range(4)]
        Axs = [P.sb(f"Ax{q}", [128, 128], F32) for q in range(4)]
        Xs = [P.sb(f"Xx{q}", [128, 128], F32) for q in range(4)]
        Xbs = [P.sb(f"Xb{q}", [128, 128], BF16) for q in range(4)]
        Pns = [[P.sb(f"Pn{q}_{k_}", [128, 2, 128], F32) for k_ in range(2)] for q in range(4)]
        u_all = P.sb("uall", [128, 8, 128], F32)
        wT_all = P.sb("wTall", [128, 8, 128], BF16)
        vnew = P.sb("vnew", [128, 8, 128], BF16)
        o_tok = big[:, 0:2048].bitcast(F32).rearrange("p (h t) -> p h t", h=8)
        sgp = Pool(P, "sgate", 2, [128, D], BF16)
        ofin = big[:, 2048:3072]
        ofinT = big[:, 3072:4096].rearrange("p (h t) -> p h t", h=8)
        sk, dk = src.tensor.name, dst.tensor.name

        def bc(col):
            return sc[:, col, :].unsqueeze(2).to_broadcast([128, 8, 128])

        def front_a(i):
            st = {}

            def s0():
                st["xt"] = xtm.next()
                P.dma("sp", st["xt"][:], src[i * 128:(i + 1) * 128, :], in_key=f"{sk}:{i}")
                st["hT"] = hTp.next()
                self.rmsnorm_T(st["xt"], gcol, st["hT"], 0, bank[7])
                st["sg"] = sgp.next()

            def proj_pe(g):
                hT = st["hT"]
                pb = bank[4 + g % 2]
                for mm_ in range(4):
                    m = 4 * g + mm_
                    for c in range(8):
                        P.mm(pb[:, mm_ * 128:(mm_ + 1) * 128], w_in[:, c, m * 128:(m + 1) * 128], hT[:, c, :],
                             start=(c == 0), stop=(c == 7))

            def proj_post(g):
                pb = bank[4 + g % 2]
                P.act(xc[:, 4 * g:4 * g + 4, 3:131], pb[:].rearrange("p (m t) -> p m t", m=4), AF.Copy)
                acc = accp.next()
                for mm_ in range(4):
                    m = 4 * g + mm_
                    P.ts("dve", acc[:, mm_, :], xc[:, m, 3:131], convw[:, m, 3:4], ALU.mult)
                    for tp in range(3):
                        P.stt("dve", acc[:, mm_, :], xc[:, m, tp:tp + 128], convw[:, m, tp:tp + 1], acc[:, mm_, :],
                              ALU.mult, ALU.add)
                P.act(qkv_s[:, 4 * g:4 * g + 4, :], acc[:], AF.Silu)
                if g == 5:
                    P.copy("pool", xc[:, :, 0:3], xc[:, :, 128:131])

            def gate_pe(hf):
                hT = st["hT"]
                pgt = bank[6 + hf]
                for c in range(8):
                    P.mm(pgt[:, :], hT[:, c, :], w_in[:, c, 3072 + hf * 512:3072 + (hf + 1) * 512],
                         start=(c == 0), stop=(c == 7))

            def gate_post(hf):
                P.act(st["sg"][:, hf * 512:(hf + 1) * 512], bank[6 + hf][:, :], AF.Silu)

            sl = [s0, lambda: proj_pe(0)]
            for g in range(1, 6):
                sl.append(lambda g=g: (proj_post(g - 1), proj_pe(g)))
            sl.append(lambda: (proj_post(5), gate_pe(0), gate_pe(1)))
            sl.append(lambda: (gate_post(0), gate_post(1)))
            return st, sl

        st_next, sl0 = front_a(0)
        for f_ in sl0:
            f_()
        for i in range(ntiles):
            st_cur = st_next
            xt, hT, sgate = st_cur["xt"], st_cur["hT"], st_cur["sg"]
            pending = []
            if i + 1 < ntiles:
                st_next, pending = front_a(i + 1)
            for t3 in range(3):
                pst = bank[2 + t3][:].bitcast(BF16)
                for h in range(8):
                    P.tr(pst[:, h * 128:(h + 1) * 128], qkv_s[:, t3 * 8 + h, :], self.ident_bf[:])
                P.copy("act" if t3 != 1 else "dve", qkv_t[:, t3 * 8:(t3 + 1) * 8, :],
                       pst.rearrange("p (h t) -> p h t", h=8))
            P.tt("dve", sq, qkv_t[:, 0:16, :], qkv_t[:, 0:16, :], ALU.mult)
            rn = sc[:, RN_Q:RN_K + 1, :]
            P.reduce("dve", rn, sq.rearrange("p (a h) t -> p a h t", a=2), ALU.add)
            P.ts("dve", rn, rn, EPS, ALU.add)
            P.act(rn, rn, AF.Ln)
            P.act(rn, rn, AF.Exp, scale=-0.5)
            P.ts("dve", sc[:, RN_Q, :], sc[:, RN_Q, :], 128.0 ** -0.5, ALU.mult)
            pba = bank[5]
            for c in range(8):
                P.mm(pba[:, 0:16], hT[:, c, :], w_in[:, c, 4096:4112], start=(c == 0), stop=(c == 7))
            P.act(sc[:, BETA, :], pba[:, 0:8], AF.Exp, scale=-1.0)
            P.ts("dve", sc[:, BETA, :], sc[:, BETA, :], 1.0, ALU.add)
            P.op("dve", lambda g_: g_.reciprocal(out=sc[:, BETA, :], in_=sc[:, BETA, :]), [sc], [sc])
            P.tt("dve", sc[:, Z, :], pba[:, 8:16], dtb[:], ALU.add)
            P.act(sc[:, Z, :], sc[:, Z, :], AF.Exp)
            P.ts("dve", sc[:, Z, :], sc[:, Z, :], 1.0, ALU.add)
            P.act(sc[:, Z, :], sc[:, Z, :], AF.Ln)
            P.tt("dve", sc[:, G, :], sc[:, Z, :], negA[:], ALU.mult)
            P.mm(pba[:, 16:24], cm[:, 0, :], sc[:, G, :])
            P.mm(pba[:, 24:32], cm[:, 1, :], sc[:, G, :])
            P.copy("dve", sc[:, GC:GLAST + 1, :], pba[:, 16:32].rearrange("p (a h) -> p a h", a=2))
            P.mm(pba[:, 32:40], cm[:, 2, :], sc[:, GC, :])
            P.mm(pba[:, 40:48], cm[:, 3, :], sc[:, GC, :])
            P.act(glbe[:], pba[:, 32:48].rearrange("p (a h) -> p a h", a=2), AF.Exp)
            P.act(sc[:, EG, :], sc[:, GC, :], AF.Exp)
            P.tt("dve", sc[:, TMP, :], sc[:, GLAST, :], sc[:, GC, :], ALU.subtract)
            P.act(sc[:, EK, :], sc[:, TMP, :], AF.Exp)
            P.tt("dve", sc[:, CKB, :], sc[:, RN_K, :], sc[:, BETA, :], ALU.mult)
            P.tt("dve", sc[:, CKBEG, :], sc[:, CKB, :], sc[:, EG, :], ALU.mult)
            P.tt("dve", sc[:, CKDEC, :], sc[:, RN_K, :], sc[:, EK, :], ALU.mult)
            P.tt("dve", sc[:, CQDEC, :], sc[:, RN_Q, :], sc[:, EG, :], ALU.mult)
            qt, kt, vt = qkv_t[:, 0:8, :], qkv_t[:, 8:16, :], qkv_t[:, 16:24, :]
            P.tt("dve", sop["kn"][:], kt, bc(RN_K), ALU.mult)
            P.tt("dve", sop["kb"][:], kt, bc(CKB), ALU.mult)
            P.tt("dve", sop["qn"][:], qt, bc(RN_Q), ALU.mult)
            P.tt("dve", sop["qdec"][:], qt, bc(CQDEC), ALU.mult)
            P.tt("pool", sop["vb"][:], vt, bc(BETA), ALU.mult)
            P.tt("pool", sop["kbeg"][:], kt, bc(CKBEG), ALU.mult)
            P.tt("pool", sop["kdec"][:], kt, bc(CKDEC), ALU.mult)
            P.tt("dve", diagG[:], self.ident_f[:].unsqueeze(1).to_broadcast([128, 8, 128]), bc(GC), ALU.mult)
            for hh in range(2):
                P.mm(bank[6 + hh][:, :], ones[:], diagG[:, 4 * hh:4 * hh + 4, :].rearrange("p h t -> p (h t)"))
                P.tt("dve", dtarg[:, 4 * hh:4 * hh + 4, :], bank[6 + hh][:].rearrange("p (h t) -> p h t", h=4),
                     sc[:, GC, 4 * hh:4 * hh + 4].unsqueeze(2).to_broadcast([128, 4, 128]), ALU.subtract)
            P.ts("dve", e2[:], dtarg[:], -1.0, ALU.mult, 0.0, ALU.min)
            P.act(e2[:], e2[:], AF.Exp)
            P.ts("dve", e1[:], dtarg[:], 0.0, ALU.min)
            P.act(e1[:], e1[:], AF.Exp)
            P.tt("dve", DTs[:], e1[:], masks[:, 0:1, :].to_broadcast([128, 8, 128]), ALU.mult)
            P.tt("dve", DTi[:], e1[:], masks[:, 1:2, :].to_broadcast([128, 8, 128]), ALU.mult)
            P.tt("dve", Ds[:], e2[:], masks[:, 2:3, :].to_broadcast([128, 8, 128]), ALU.mult)
            for hg in range(2):
                hs = [4 * hg + q for q in range(4)]
                for q, h in enumerate(hs):
                    pt = bank[q][:].bitcast(BF16)
                    for k_, n_ in enumerate(["kn", "kb", "qn", "qdec"]):
                        P.tr(pt[:, k_ * 128:(k_ + 1) * 128], sop[n_][:, h, :], self.ident_bf[:])
                if pending:
                    pending.pop(0)()
                for q, h in enumerate(hs):
                    pt = bank[q][:].bitcast(BF16)
                    P.copy("act", fms[q][:], pt[:, 0:384].rearrange("p (a t) -> p a t", a=3))
                    P.copy("act", qdecT[:, h, :], pt[:, 384:512])
                if pending:
                    pending.pop(0)()
                for q, h in enumerate(hs):
                    pg = bank[q]
                    fm = fms[q]
                    P.mm(pg[:, 0:256], fm[:, 0, :], fm[:, 1:3, :].rearrange("p a t -> p (a t)"))
                    P.mm(pg[:, 256:384], fm[:, 1, :], fm[:, 0, :])
                for q, h in enumerate(hs):
                    pg = bank[q]
                    P.tt("dve", Mxs[q][:], pg[:, 0:128], DTs[:, h, :], ALU.mult)
                    P.tt("dve", qkT[:, h, :], pg[:, 128:256], DTi[:, h, :], ALU.mult)
                    P.tt("dve", Axs[q][:], pg[:, 256:384], Ds[:, h, :], ALU.mult)
                    P.tt("dve", Xs[q][:], self.ident_f[:], Mxs[q][:], ALU.subtract)
                Pm = [Mxs[q][:] for q in range(4)]
                PTm = [Axs[q][:] for q in range(4)]
                for lvl in range(1, 6):
                    for q in range(4):
                        pp = bank[q]
                        if lvl < 5:
                            P.mm(pp[:, 0:128], PTm[q], Pm[q])
                        P.mm(pp[:, 128:256], Pm[q], PTm[q])
                    for q in range(4):
                        pp = bank[q]
                        pn = Pns[q][lvl % 2]
                        if lvl < 5:
                            P.copy("act", pn[:], pp[:, 0:256].rearrange("p (a t) -> p a t", a=2))
                        else:
                            P.copy("act", pn[:, 1, :], pp[:, 128:256])
                    if pending and lvl in (1, 3, 5):
                        pending.pop(0)()
                    for q in range(4):
                        pn = Pns[q][lvl % 2]
                        P.mm(bank[q][:, 256:384], pn[:, 1, :], Xs[q][:])
                    for q in range(4):
                        P.tt("dve", Xs[q][:], Xs[q][:], bank[q][:, 256:384], ALU.add)
                        pn = Pns[q][lvl % 2]
                        Pm[q], PTm[q] = pn[:, 0, :], pn[:, 1, :]
                for q in range(4):
                    P.copy("act", Xbs[q][:], Xs[q][:])
                for q, h in enumerate(hs):
                    pu = bank[q]
                    P.mm(pu[:, 0:128], Xbs[q][:], sop["vb"][:, h, :])
                    P.mm(pu[:, 128:256], sop["kbeg"][:, h, :], Xbs[q][:])
                for q, h in enumerate(hs):
                    pu = bank[q]
                    P.copy("act", u_all[:, h, :], pu[:, 0:128])
                    P.copy("act", wT_all[:, h, :], pu[:, 128:256])
            while pending:
                pending.pop(0)()
            for c in range(2):
                rows = slice(64 * c, 64 * c + 64)
                for h in range(8):
                    P.mm(bank[h][:, 0:128], wT_all[:, h, :], S_b[h][:])
                for h in range(8):
                    P.tt("dve", vnew[rows, h, :], u_all[rows, h, :], bank[h][rows, 0:128], ALU.subtract)
                for h in range(8):
                    P.mm(bank[h][:, 128:256], qdecT[:, h, :], S_b[h][:], start=True, stop=False)
                    P.mm(bank[h][:, 128:256], qkT[rows, h, :], vnew[rows, h, :], start=False, stop=True)
                    P.mm(bank[h][:, 256:384], sop["kdec"][rows, h, :], vnew[rows, h, :])
                for h in range(8):
                    P.stt("dve", S_f[h][:], S_f[h][:], glbe[:, c, h:h + 1], bank[h][:, 256:384], ALU.mult, ALU.add)
                    P.copy("act", S_b[h][:], S_f[h][:])
                    P.copy("act", o_tok[rows, h, :], bank[h][rows, 128:256])
            P.tt("dve", sq[:, 0:8, :], o_tok, o_tok, ALU.mult)
            P.reduce("dve", sc[:, ORS, :], sq[:, 0:8, :], ALU.add)
            P.ts("dve", sc[:, ORS, :], sc[:, ORS, :], 1.0 / 128, ALU.mult, EPS, ALU.add)
            P.act(sc[:, ORS, :], sc[:, ORS, :], AF.Ln)
            P.act(sc[:, ORS, :], sc[:, ORS, :], AF.Exp, scale=-0.5)
            P.tt("dve", o_tok, o_tok, bc(ORS), ALU.mult)
            P.tt("dve", o_tok, o_tok, onorm[:].unsqueeze(1).to_broadcast([128, 8, 128]), ALU.mult)
            P.tt("dve", ofin, o_tok.rearrange("p h t -> p (h t)"), sgate[:], ALU.mult)
            pst = bank[2][:].bitcast(BF16)
            for c in range(8):
                P.tr(pst[:, c * 128:(c + 1) * 128], ofin[:, c * 128:(c + 1) * 128], self.ident_bf[:])
            P.copy("act", ofinT, pst.rearrange("p (c t) -> p c t", c=8))
            for hf in range(2):
                py = bank[4 + hf]
                for c in range(8):
                    P.mm(py[:, :], ofinT[:, c, :], w_out[:, c, hf * 512:(hf + 1) * 512], start=(c == 0), stop=(c == 7))
                P.tt("dve", xt[:, hf * 512:(hf + 1) * 512], xt[:, hf * 512:(hf + 1) * 512], py[:, :], ALU.add)
            P.dma("sp", dst[i * 128:(i + 1) * 128, :], xt[:], out_key=f"{dk}:{i}")
        P.pop()

    def kv_pass(self, src, dst):
        P = self.P
        P.push()
        bank = self.bank
        self.norm_pools()
        kvw = self.din("kv_w", [D, 512])
        wk2 = P.sb("wk2", [128, 8, 4, 128], BF16)
        for hk in range(4):
            for dup in range(2):
                P.dma("pool", wk2[:, :, hk, dup * 64:(dup + 1) * 64],
                      kvw[:, hk * 64:(hk + 1) * 64].rearrange("(c p) m -> p c m", p=128))
        wv = P.sb("wv", [128, 8, 256], BF16)
        P.dma("pool", wv[:], kvw[:, 256:512].rearrange("(c p) m -> p c m", p=128))
        kb2 = self.const_sb("kbias2", [128, 4])
        vbc = self.const_sb("vbias_bc", [128, 256])
        self.KT2 = P.sb("KT2", [128, 4, (NT_HALF + 1) * 128], BF16)
        self.Vsb = P.sb("Vsb", [128, NT_HALF + 1, 256], BF16)
        gcol = self.const_sb("kv_gcol", [128, 8])
        xlp = Pool(P, "xl", 2, [128, D], F32)
        xhp = Pool(P, "xh", 2, [128, D], F32)
        hTp = Pool(P, "hTk", 2, [128, 8, 128], BF16)
        sk, dk = src.tensor.name, dst.tensor.name
        f, omf = self.flag[:, 0:1], self.flag[:, 1:2]
        for r in range(-1, NT_HALF):
            lo, hi = max(r, 0), NT_HALF + r
            xl, xh = xlp.next(), xhp.next()
            P.dma("sp", xl[:], src[lo * 128:(lo + 1) * 128, :], in_key=f"{sk}:{lo}")
            P.dma("sp", xh[:], src[hi * 128:(hi + 1) * 128, :], in_key=f"{sk}:{hi}")
            P.ts("dve", xl[:], xl[:], omf, ALU.mult)
            P.stt("dve", xl[:], xh[:], f, xl[:], ALU.mult, ALU.add)
            if r >= 0:
                P.dma("sp", dst[r * 128:(r + 1) * 128, :], xl[:], out_key=f"{dk}:{r}")
            hT = hTp.next()
            self.rmsnorm_T(xl, gcol[:], hT, 0, bank[7])
            for hk in range(4):
                pk = bank[hk % 2]
                for c in range(8):
                    P.mm(pk[:, 0:128], wk2[:, c, hk, :], hT[:, c, :], start=(c == 0), stop=(c == 7))
                P.ts("dve", self.KT2[:, hk, (r + 1) * 128:(r + 2) * 128], pk[:, 0:128], kb2[:, hk:hk + 1], ALU.add)
            pv = bank[2]
            for c in range(8):
                P.mm(pv[:, 0:256], hT[:, c, :], wv[:, c, :], start=(c == 0), stop=(c == 7))
            P.tt("dve", self.Vsb[:, r + 1, :], pv[:, 0:256], vbc[:], ALU.add)
        P.dma("sp", self.kt2_d, self.KT2[:].rearrange("p h k -> p (h k)"))
        P.dma("sp", self.vsb_d, self.Vsb[:].rearrange("p t d -> p (t d)"))
        P.pop()

    def swa_pass(self, l, j, src, dst):
        P = self.P
        P.push()
        bank = self.bank
        self.norm_pools()
        wq_d = self.din(f"b_w_q{j}", [D, D])
        wo_d = self.din(f"b_w_out{j}", [D, D])
        wq = P.sb("wq", [128, 8, D], BF16)
        wo = P.sb("wo", [128, 8, D], BF16)
        for c in range(8):
            P.dma("pool", wq[:, c, :], wq_d[c * 128:(c + 1) * 128, :])
        for c in range(8):
            P.dma("pool", wo[:, c, :], wo_d[c * 128:(c + 1) * 128, :])
        self.KT2 = P.sb("KT2", [128, 4, (NT_HALF + 1) * 128], BF16)
        self.Vsb = P.sb("Vsb", [128, NT_HALF + 1, 256], BF16)
        P.dma("sp", self.KT2[:].rearrange("p h k -> p (h k)"), self.kt2_d)
        P.dma("sp", self.Vsb[:].rearrange("p t d -> p (t d)"), self.vsb_d)
        bq = self.const_sb(f"bq{j}", [128, 8])
        sinks = self.const_sb(f"sinks{j}", [128, 16])
        mb = self.const_sb("maskbias", [128, 2, 256])
        gcol = self.gc_mix[:, l * 8:(l + 1) * 8]
        xtm = Pool(P, "xts", 2, [128, D], F32)
        hTp = Pool(P, "hTs", 2, [128, 8, 128], BF16)
        qT = P.sb("qT", [128, 8, 128], BF16)
        smp = Pool(P, "sm", 2, [128, 2, 256], F32)
        pp_ = Pool(P, "pp", 2, [128, 2, 256], F32)
        pnp = Pool(P, "pn", 2, [128, 2, 256], BF16)
        pTp = Pool(P, "pT", 2, [128, 4, 128], BF16)
        stp = Pool(P, "st", 4, [128, 8, 2], F32)
        oT = P.sb("oT", [128, 8, 128], BF16)
        sk, dk = src.tensor.name, dst.tensor.name
        for r in range(SWA_TILES):
            xt = xtm.next()
            P.dma("sp", xt[:], src[r * 128:(r + 1) * 128, :], in_key=f"{sk}:{r}")
            hT = hTp.next()
            self.rmsnorm_T(xt, gcol, hT, 0, bank[7])
            for m in range(8):
                pq = bank[m % 2]
                for c in range(8):
                    P.mm(pq[:, 0:128], wq[:, c, m * 128:(m + 1) * 128], hT[:, c, :], start=(c == 0), stop=(c == 7))
                P.ts("dve", qT[:, m, :], pq[:, 0:128], bq[:, m:m + 1], ALU.add, 0.125, ALU.mult)
            mbr = mb[:, 0:1, :] if r == 0 else mb[:, 1:2, :]
            if SWA_LEVEL < 1:
                P.memset("dve", oT[:], 0.0)
            for pr in range(8 if SWA_LEVEL >= 1 else 0):
                hk = pr // 2
                for e in range(2):
                    P.mm(bank[2 + e][:, 0:256], qT[e * 64:(e + 1) * 64, pr, :],
                         self.KT2[e * 64:(e + 1) * 64, hk, r * 128:(r + 2) * 128])
                sm, p_, pn, pT, st = smp.next(), pp_.next(), pnp.next(), pTp.next(), stp.next()
                for e in range(2):
                    P.tt("dve", sm[:, e, :], bank[2 + e][:, 0:256], mb[:, 0 if r == 0 else 1, :], ALU.add)
                if SWA_LEVEL == 1:
                    P.copy("dve", oT[:, pr, :], sm[:, 0, 0:128])
                    continue
                P.reduce("dve", st[:, 0, :], sm[:], ALU.max)
                P.tt("dve", st[:, 0, :], st[:, 0, :], sinks[:, 2 * pr:2 * pr + 2], ALU.max)
                P.ts("dve", st[:, 1, :], st[:, 0, :], -1.0, ALU.mult)
                for e in range(2):
                    P.act(p_[:, e, :], sm[:, e, :], AF.Exp, bias=st[:, 1, e:e + 1], accum_out=st[:, 2, e:e + 1])
                P.tt("dve", st[:, 3, :], sinks[:, 2 * pr:2 * pr + 2], st[:, 0, :], ALU.subtract)
                P.act(st[:, 3, :], st[:, 3, :], AF.Exp)
                P.tt("dve", st[:, 4, :], st[:, 3, :], st[:, 2, :], ALU.add)
                P.op("dve", lambda g_, st=st: g_.reciprocal(out=st[:, 5, :], in_=st[:, 4, :]), [st], [st])
                P.tt("dve", pn[:], p_[:], st[:, 5, :].unsqueeze(2).to_broadcast([128, 2, 256]), ALU.mult)
                if SWA_LEVEL < 3:
                    P.copy("act", oT[:, pr, :], pn[:, 0, 0:128])
                    continue
                ptb = bank[4 + pr % 2][:].bitcast(BF16)
                for e in range(2):
                    for kt in range(2):
                        q_ = e * 2 + kt
                        P.tr(ptb[:, q_ * 128:(q_ + 1) * 128], pn[:, e, kt * 128:(kt + 1) * 128], self.ident_bf[:])
                P.copy("act", pT[:], ptb[:, 0:512].rearrange("p (a t) -> p a t", a=4))
                po = bank[6]
                for e in range(2):
                    for kt in range(2):
                        P.mm(po[e * 64:(e + 1) * 64, 0:128], self.Vsb[:, r + kt, hk * 64:(hk + 1) * 64],
                             pT[:, e * 2 + kt, :], start=(kt == 0), stop=(kt == 1))
                P.copy("act", oT[:, pr, :], po[:, 0:128])
            for hf in range(2):
                py = bank[hf]
                for c in range(8):
                    P.mm(py[:, :], oT[:, c, :], wo[:, c, hf * 512:(hf + 1) * 512], start=(c == 0), stop=(c == 7))
                P.tt("dve", xt[:, hf * 512:(hf + 1) * 512], xt[:, hf * 512:(hf + 1) * 512], py[:, :], ALU.add)
            P.dma("sp", dst[r * 128:(r + 1) * 128, :], xt[:], out_key=f"{dk}:{r}")
        P.pop()

    def build(self, out_tokens=TOK):
        P = self.P
        self.x_in = self.din("x", [SEQ, D])
        self.out = P.dram("out", [out_tokens, D], F32, "ExternalOutput")
        self.din("ident", [128, 128])
        self.setup_common()
        self.xs = P.dram("xs", [SEQ, D], F32, "Internal")
        self.xs2 = P.dram("xs2", [TOK, D], F32, "Internal")
        self.kt2_d = P.dram("kt2_d", [128, 4 * (NT_HALF + 1) * 128], BF16, "Internal")
        self.vsb_d = P.dram("vsb_d", [128, (NT_HALF + 1) * 256], BF16, "Internal")
        cur = self.x_in
        full = True
        for si, stg in enumerate(self.stages):
            last = si == len(self.stages) - 1
            kind = stg[0]
            if kind == "kv":
                dst = self.out if last else self.xs2
                self.kv_pass(cur, dst)
                full = False
            else:
                dst = self.out if last else (self.xs if full else self.xs2)
                if kind == "ffn":
                    self.ffn_pass(stg[1], cur, dst, SEQ if full else TOK, final=(len(stg) > 2 and stg[2]))
                elif kind == "dn":
                    self.dn_pass(stg[1], cur, dst, NT_ALL if full else NT_HALF)
                elif kind == "swa":
                    self.swa_pass(stg[1], stg[1] - 2, cur, dst)
            cur = dst
        P.finish()
        return self.nc


def colmajor(v, nchunk):
    return np.ascontiguousarray(np.asarray(v, dtype=np.float32).reshape(nchunk, 128).T)


def rep128(v):
    v = np.asarray(v, dtype=np.float32).reshape(1, -1)
    return np.ascontiguousarray(np.broadcast_to(v, (128, v.shape[1])))


def const_tables():
    idx = np.arange(128)
    same = (idx[:, None] // 64) == (idx[None, :] // 64)
    up_s = same & (idx[None, :] > idx[:, None])
    up_i = same & (idx[None, :] >= idx[:, None])
    lo_s = same & (idx[:, None] > idx[None, :])
    masks = np.stack([up_s, up_i, lo_s], axis=1).astype(np.float32)
    sel0 = np.zeros((128, 128), np.float32); sel0[63, :] = 1.0
    sel1 = np.zeros((128, 128), np.float32); sel1[127, :] = 1.0
    cm = np.stack([up_i.astype(np.float32), same.astype(np.float32), sel0, sel1], axis=1)
    qi = np.arange(128)[:, None]
    ki = np.arange(256)[None, :]
    in_win = (ki > qi) & (ki <= qi + 128)
    normal = np.where(in_win, 0.0, NEG).astype(np.float32)
    first = np.where(in_win & (ki >= 128), 0.0, NEG).astype(np.float32)
    return masks, cm, normal, first


def host_inputs(inputs, needed):
    g = lambda k: np.asarray(inputs[k], dtype=np.float32)
    masks, cm, normal, first = const_tables()
    common = {"ident": np.eye(128, dtype=np.float32), "masks": masks, "cmats": cm,
              "ones": np.ones((128, 128), np.float32)}
    if "ffn_gcols" in needed:
        common["ffn_gcols"] = np.ascontiguousarray(
            np.concatenate([colmajor(g("ffn_norm")[i // 2, i % 2], 8) for i in range(8)], axis=1))
    if "mix_gcols" in needed:
        common["mix_gcols"] = np.ascontiguousarray(
            np.concatenate([colmajor(g("mix_norm")[i], 8) for i in range(4)], axis=1))
    for i in range(8):
        for nm, src in (("wg", "ffn_w_gate"), ("wu", "ffn_w_up"), ("wd", "ffn_w_down")):
            if f"{nm}{i}" in needed:
                common[f"{nm}{i}"] = np.ascontiguousarray(g(src)[i // 2, i % 2])
    for l in range(2):
        if f"a_w_in{l}" in needed:
            common[f"a_w_in{l}"] = np.ascontiguousarray(g("a_w_in")[l])
            common[f"a_w_out{l}"] = np.ascontiguousarray(g("a_w_out")[l])
            cw = g("a_conv")[l]
            common[f"convw{l}"] = np.ascontiguousarray(cw.reshape(4, 24, 128).transpose(2, 1, 0))
            common[f"dtb{l}"] = rep128(g("a_dt_bias")[l])
            common[f"alog{l}"] = rep128(g("a_A_log")[l])
            common[f"onorm{l}"] = rep128(g("a_o_norm")[l])
    if "kv_w" in needed:
        common["kv_w"] = np.ascontiguousarray(g("kv_w"))
        kvb = g("kv_b")
        kb2 = np.stack([np.tile(kvb[hk * 64:(hk + 1) * 64], 2) for hk in range(4)], axis=1)
        common["kbias2"] = np.ascontiguousarray(kb2)
        common["vbias_bc"] = rep128(kvb[256:512])
        common["kv_gcol"] = colmajor(g("kv_norm"), 8)
    for j in range(2):
        if f"b_w_q{j}" in needed:
            common[f"b_w_q{j}"] = np.ascontiguousarray(g("b_w_q")[j])
            common[f"b_w_out{j}"] = np.ascontiguousarray(g("b_w_out")[j])
            common[f"bq{j}"] = colmajor(g("b_b_q")[j], 8)
            common[f"sinks{j}"] = rep128(g("b_sinks")[j])
    if "final_bc" in needed:
        common["final_bc"] = rep128(g("final_norm"))
    x = g("x")
    maps = []
    for c in range(NCORES):
        hf = c % 2
        m = dict(common)
        m["x"] = np.ascontiguousarray(x[c // 2])
        m["flag"] = np.ascontiguousarray(np.broadcast_to(np.array([[float(hf), 1.0 - hf]], np.float32), (128, 2)))
        m["maskbias"] = np.ascontiguousarray(np.stack([normal if hf == 1 else first, normal], axis=1))
        maps.append({k: v for k, v in m.items() if k in needed})
    return maps


FULL_STAGES = [("ffn", 0), ("dn", 0), ("ffn", 1), ("ffn", 2), ("dn", 1), ("ffn", 3), ("kv",),
               ("ffn", 4), ("swa", 2), ("ffn", 5), ("ffn", 6), ("swa", 3), ("ffn", 7, True)]


def run(stages, inputs, out_tokens=TOK, trace=False):
    b = Builder(stages)
    nc = b.build(out_tokens)
    maps = host_inputs(inputs, set(b.inputs))
    res = run_bass_kernel_spmd(nc, maps, core_ids=list(range(NCORES)), trace=trace)
    return [res.results[c]["out"] for c in range(NCORES)], res, b


def kernel(**inputs):
    outs, _, _ = run(FULL_STAGES, inputs)
    out = np.zeros((4, SEQ, D), np.float32)
    for c in range(NCORES):
        out[c // 2, (c % 2) * TOK:(c % 2 + 1) * TOK, :] = outs[c]
    return out
```

```python
import contextlib
import numpy as np
import concourse.bass as bass
import concourse.mybir as mybir
from concourse.bass_utils import run_bass_kernel_spmd

F32 = mybir.dt.float32
BF16 = mybir.dt.bfloat16
AF = mybir.ActivationFunctionType
ALU = mybir.AluOpType
AX = mybir.AxisListType

D = 1024
DFF = 2816
NJ = DFF // 128
TOK = 4096
EPS = 1e-6
NCORES = 8


class Tok:
    __slots__ = ("sem", "val", "eng", "sem_key")

    def __init__(self, sem, val, eng):
        self.sem, self.val, self.eng = sem, val, eng


class BufState:
    def __init__(self):
        self.last_w = None
        self.reads = {}


class Prog:
    def __init__(self, nc):
        self.nc = nc
        self.es = contextlib.ExitStack()
        self.eng = {"pe": nc.tensor, "act": nc.scalar, "dve": nc.vector, "pool": nc.gpsimd, "sp": nc.sync}
        self.sem = {}
        self.cnt = {}
        for e in self.eng:
            self.sem[e] = self.es.enter_context(nc.semaphore("sem_" + e))
            self.cnt[e] = 0
        self.waited = {e: {} for e in self.eng}
        self.bufs = {}
        self.dsems = {}
        self.n_ins = 0
        self._uid = 0
        self.stack = [self.es]
        self.semkey_of = {}

    def push(self):
        es = contextlib.ExitStack()
        self.stack.append(es)
        return es

    def pop(self):
        self.barrier()
        self.stack.pop().close()

    def barrier(self):
        for e in self.eng:
            for o in self.eng:
                if o != e and self.cnt[o] > 0:
                    self._wait(e, Tok(self.sem[o], self.cnt[o], o))
            for k, d in self.dsems.items():
                if d[1] > 0:
                    t = Tok(d[0], d[1], "dma")
                    t.sem_key = k
                    self._wait(e, t)

    def sb(self, name, shape, dt):
        self._uid += 1
        t = self.stack[-1].enter_context(self.nc.sbuf_tensor(f"{name}_u{self._uid}", list(shape), dt))
        self.semkey_of[t.name] = name
        return t

    def ps(self, name, shape, dt):
        return self.stack[-1].enter_context(self.nc.psum_tensor(name, list(shape), dt))

    def key(self, x):
        if isinstance(x, str):
            return x
        if isinstance(x, tuple):
            return x[1]
        return getattr(x, 'tensor', x).name

    def st(self, k):
        s = self.bufs.get(k)
        if s is None:
            s = self.bufs[k] = BufState()
        return s

    def _wait(self, e, tok, war=False):
        if tok is None:
            return
        if tok.eng == e and (e == "pe" or war or e == "sp"):
            return
        sid = id(tok.sem)
        val = tok.val
        if tok.eng == "dma":
            val = max(val, self.dsems[tok.sem_key][1])
        if self.waited[e].get(sid, 0) >= val:
            return
        self.eng[e].wait_ge(tok.sem, val)
        self.waited[e][sid] = val

    def _deps(self, e, reads, writes):
        for r in reads:
            s = self.st(self.key(r))
            self._wait(e, s.last_w)
        for w in writes:
            s = self.st(self.key(w))
            self._wait(e, s.last_w)
            for t in s.reads.values():
                self._wait(e, t, war=True)

    def _commit(self, tok, reads, writes):
        for r in reads:
            s = self.st(self.key(r))
            s.reads[tok.eng if tok.eng != "dma" else id(tok.sem)] = tok
        for w in writes:
            s = self.st(self.key(w))
            s.last_w = tok
            s.reads = {}

    def op(self, e, fn, reads, writes):
        reads = [r for r in reads if r is not None and not isinstance(r, (int, float))]
        self._deps(e, reads, writes)
        ins = fn(self.eng[e])
        self.cnt[e] += 1
        ins.then_inc(self.sem[e], 1)
        tok = Tok(self.sem[e], self.cnt[e], e)
        self._commit(tok, reads, writes)
        self.n_ins += 1
        return ins

    def dma(self, q, out, in_, out_key=None, in_key=None, sem_key=None):
        ok = out_key or self.key(out)
        ik = in_key or self.key(in_)
        sb_side = out.tensor.name if out.tensor.name not in self.dram_names else in_.tensor.name
        sk = sem_key or self.semkey_of.get(sb_side, sb_side)
        if sk not in self.dsems:
            self.dsems[sk] = [self.es.enter_context(self.nc.semaphore("d_" + sk.replace(":", "_"))), 0]
        d = self.dsems[sk]
        self._deps(q, [ik], [ok])
        ins = self.eng[q].dma_start(out=out, in_=in_)
        d[1] += 16
        ins.then_inc(d[0], 16)
        tok = Tok(d[0], d[1], "dma")
        tok_key = sk
        tok.sem_key = tok_key
        self._commit(tok, [ik], [ok])
        self.n_ins += 1
        return ins

    dram_names = set()

    def dram(self, name, shape, dt, kind):
        self.dram_names.add(name)
        return self.nc.dram_tensor(name, list(shape), dt, kind=kind).ap()

    def mm(self, out, lhsT, rhs, start=True, stop=True, **kw):
        return self.op("pe", lambda e: e.matmul(out, lhsT, rhs, start=start, stop=stop, **kw), [lhsT, rhs], [out])

    def tr(self, out, in_, ident):
        return self.op("pe", lambda e: e.transpose(out, in_, ident), [in_, ident], [out])

    def act(self, out, in_, func, bias=None, scale=None, accum_out=None):
        kw = {}
        if bias is not None:
            kw["bias"] = bias
        if scale is not None:
            kw["scale"] = scale
        if accum_out is not None:
            kw["accum_out"] = accum_out
        wr = [out] + ([accum_out] if accum_out is not None else [])
        return self.op("act", lambda e: e.activation(out=out, in_=in_, func=func, **kw), [in_, bias, scale], wr)

    def tt(self, e, out, in0, in1, op):
        return self.op(e, lambda g: g.tensor_tensor(out=out, in0=in0, in1=in1, op=op), [in0, in1], [out])

    def ts(self, e, out, in0, s1, op0, s2=None, op1=None, accum_out=None):
        kw = {}
        if op1 is not None:
            kw["op1"] = op1
        if accum_out is not None:
            kw["accum_out"] = accum_out
        wr = [out] + ([accum_out] if accum_out is not None else [])
        return self.op(e, lambda g: g.tensor_scalar(out=out, in0=in0, scalar1=s1, scalar2=s2, op0=op0, **kw),
                       [in0, s1, s2], wr)

    def stt(self, e, out, in0, scalar, in1, op0, op1):
        return self.op(e, lambda g: g.scalar_tensor_tensor(out=out, in0=in0, scalar=scalar, in1=in1, op0=op0, op1=op1),
                       [in0, scalar, in1], [out])

    def copy(self, e, out, in_):
        if e == "act":
            return self.op(e, lambda g: g.copy(out=out, in_=in_), [in_], [out])
        return self.op(e, lambda g: g.tensor_copy(out=out, in_=in_), [in_], [out])

    def memset(self, e, ap, val):
        return self.op(e, lambda g: g.memset(ap, val), [], [ap])

    def reduce(self, e, out, in_, op, axis=AX.X):
        return self.op(e, lambda g: g.tensor_reduce(out=out, in_=in_, axis=axis, op=op), [in_], [out])

    def finish(self):
        self.barrier()
        while len(self.stack) > 1:
            self.stack.pop().close()
        self.es.close()


class Pool:
    def __init__(self, P, name, n, shape, dt, psum=False):
        self.t = [(P.ps if psum else P.sb)(f"{name}{i}", shape, dt) for i in range(n)]
        self.i = 0

    def next(self):
        t = self.t[self.i % len(self.t)]
        self.i += 1
        return t


SEQ = 8192
NT_ALL = SEQ // 128
NT_HALF = TOK // 128
A_PROJ = 4112
NEG = -30000.0
SWA_TILES = NT_HALF
SWA_LEVEL = 9


class Builder:
    def __init__(self, stages, n_all=SEQ, t_ffn=512):
        self.stages = stages
        self.TF = t_ffn
        self.nc = bass.Bass("TRN2", target_bir_lowering=False)
        self.P = Prog(self.nc)
        self.inputs = {}

    def din(self, name, shape, dt=F32):
        if name in self.inputs:
            return self.inputs[name]
        ap = self.P.dram(name, shape, dt, "ExternalInput")
        self.inputs[name] = ap
        return ap

    def const_sb(self, name, shape, dt=F32, q="sp"):
        d = self.din(name, shape)
        t = self.P.sb("c_" + name, shape, dt)
        self.P.dma(q if dt == F32 else "pool", t[:], d)
        return t

    def setup_common(self):
        P = self.P
        self.ident_bf = self.const_sb("ident", [128, 128], BF16)
        self.ident_f = P.sb("ident_f", [128, 128], F32)
        P.dma("sp", self.ident_f[:], self.inputs["ident"])
        self.bank = [P.ps(f"bank{i}", [128, 512], F32) for i in range(8)]
        self.gc_ffn = self.const_sb("ffn_gcols", [128, 64])
        self.gc_mix = self.const_sb("mix_gcols", [128, 32])
        self.flag = self.const_sb("flag", [128, 2])

    def rmsnorm_T(self, x_tok, gcol, hT, s, tr_bank):
        P = self.P
        junk = self.junk.next()
        ssq = self.small.next()
        P.act(junk[:], x_tok[:], AF.Square, accum_out=ssq[:, 0:1])
        P.ts("dve", ssq[:, 1:2], ssq[:, 0:1], 1.0 / D, ALU.mult, EPS, ALU.add)
        P.act(ssq[:, 3:4], ssq[:, 1:2], AF.Ln)
        P.act(ssq[:, 2:3], ssq[:, 3:4], AF.Exp, scale=-0.5)
        xn = self.xn.next()
        P.ts("dve", xn[:], x_tok[:], ssq[:, 2:3], ALU.mult)
        pst = tr_bank[:].bitcast(BF16)
        for c in range(8):
            P.tr(pst[:, c * 128:(c + 1) * 128], xn[:, c * 128:(c + 1) * 128], self.ident_bf[:])
        P.tt("dve", hT[:, :, s * 128:(s + 1) * 128], pst.rearrange("p (c t) -> p c t", c=8),
             gcol.unsqueeze(2).to_broadcast([128, 8, 128]), ALU.mult)

    def norm_pools(self):
        P = self.P
        self.junk = Pool(P, "junk", 1, [128, D], BF16)
        self.small = Pool(P, "small", 4, [128, 4], F32)
        self.xn = Pool(P, "xn", 1, [128, D], BF16)

    def ffn_pass(self, idx, src, dst, ntok, final=False):
        P = self.P
        P.push()
        TF = self.TF
        NS = TF // 128
        self.norm_pools()
        xtok = Pool(P, "xtok", 6, [128, D], F32)
        hTp = Pool(P, "hT", 1, [128, 8, TF], BF16)
        aTp = Pool(P, "aT", 1, [128, NJ, TF], BF16)
        sgp = Pool(P, "sg", 2, [128, 512], F32)
        wg = self.din(f"wg{idx}", [D, DFF])
        wu = self.din(f"wu{idx}", [D, DFF])
        wd = self.din(f"wd{idx}", [DFF, D])
        HJ = NJ // 2
        wgs = [P.sb(f"wg{g}", [128, 8, HJ * 128], BF16) for g in range(2)]
        wus = [P.sb(f"wu{g}", [128, 8, HJ * 128], BF16) for g in range(2)]
        wd_sb = P.sb("wdsb", [128, NJ, D], BF16)
        for g in range(2):
            for wsb, wdr in ((wgs[g], wg), (wus[g], wu)):
                for c in range(0, 8, 2):
                    P.dma("pool", wsb[:, c:c + 2, :],
                          wdr[c * 128:(c + 2) * 128, g * HJ * 128:(g + 1) * HJ * 128].rearrange("(c p) m -> p c m", p=128))
        for j0 in range(0, NJ, 2):
            P.dma("pool", wd_sb[:, j0:j0 + 2, :], wd[j0 * 128:(j0 + 2) * 128, :].rearrange("(j p) m -> p j m", p=128))
        gcol = self.gc_ffn[:, idx * 8:(idx + 1) * 8]
        if final:
            fg = self.const_sb("final_bc", [128, D])
        sk, dk = src.tensor.name, dst.tensor.name
        for t in range(ntok // TF):
            hT = hTp.next()
            xts = []
            for s in range(NS):
                xt = xtok.next()
                r0 = t * TF + s * 128
                P.dma("sp", xt[:], src[r0:r0 + 128, :], in_key=f"{sk}:{r0 // 128}")
                self.rmsnorm_T(xt, gcol, hT, s, self.bank[4 + (s % 2)])
                xts.append(xt)
            aT = aTp.next()
            for j in range(NJ):
                g, jj = j // HJ, j % HJ
                pg = self.bank[j % 2]
                pu = self.bank[2 + j % 2]
                for c in range(8):
                    P.mm(pg[:, :TF], wgs[g][:, c, jj * 128:(jj + 1) * 128], hT[:, c, :], start=(c == 0), stop=(c == 7))
                for c in range(8):
                    P.mm(pu[:, :TF], wus[g][:, c, jj * 128:(jj + 1) * 128], hT[:, c, :], start=(c == 0), stop=(c == 7))
                sg = sgp.next()
                P.act(sg[:, :TF], pg[:, :TF], AF.Silu)
                P.tt("dve", aT[:, j, :], sg[:, :TF], pu[:, :TF], ALU.mult)
            for s in range(NS):
                xt = xts[s]
                for hf in range(2):
                    py = self.bank[4 + 2 * (s % 2) + hf]
                    for j in range(NJ):
                        P.mm(py[:, :], aT[:, j, s * 128:(s + 1) * 128], wd_sb[:, j, hf * 512:(hf + 1) * 512],
                             start=(j == 0), stop=(j == NJ - 1))
                    P.stt("dve", xt[:, hf * 512:(hf + 1) * 512], py[:, :], 0.5, xt[:, hf * 512:(hf + 1) * 512],
                          ALU.mult, ALU.add)
                r0 = t * TF + s * 128
                if final:
                    junk = self.junk.next()
                    ssq = self.small.next()
                    P.act(junk[:], xt[:], AF.Square, accum_out=ssq[:, 0:1])
                    P.ts("dve", ssq[:, 1:2], ssq[:, 0:1], 1.0 / D, ALU.mult, EPS, ALU.add)
                    P.act(ssq[:, 3:4], ssq[:, 1:2], AF.Ln)
                    P.act(ssq[:, 2:3], ssq[:, 3:4], AF.Exp, scale=-0.5)
                    P.stt("dve", xt[:], xt[:], ssq[:, 2:3], fg[:], ALU.mult, ALU.mult)
                P.dma("sp", dst[r0:r0 + 128, :], xt[:], out_key=f"{dk}:{r0 // 128}")
        P.pop()

    def dn_pass(self, l, src, dst, ntiles):
        P = self.P
        P.push()
        bank = self.bank
        self.norm_pools()
        w_in_d = self.din(f"a_w_in{l}", [D, A_PROJ])
        w_out_d = self.din(f"a_w_out{l}", [D, D])
        w_in = P.sb("win", [128, 8, A_PROJ], BF16)
        w_out = P.sb("wout", [128, 8, D], BF16)
        for c in range(8):
            P.dma("pool", w_in[:, c, :], w_in_d[c * 128:(c + 1) * 128, :])
        for c in range(8):
            P.dma("pool", w_out[:, c, :], w_out_d[c * 128:(c + 1) * 128, :])
        convw = self.const_sb(f"convw{l}", [128, 24, 4])
        dtb = self.const_sb(f"dtb{l}", [128, 8])
        alog = self.const_sb(f"alog{l}", [128, 8])
        onorm = self.const_sb(f"onorm{l}", [128, 128])
        masks = self.const_sb("masks", [128, 3, 128])
        cm = self.const_sb("cmats", [128, 4, 128])
        ones = self.const_sb("ones", [128, 128])
        gcol = self.gc_mix[:, l * 8:(l + 1) * 8]
        negA = P.sb("negA", [128, 8], F32)
        P.act(negA[:], alog[:], AF.Exp)
        P.ts("dve", negA[:], negA[:], -1.0, ALU.mult)
        S_f = [P.sb(f"Sf{h}", [128, 128], F32) for h in range(8)]
        S_b = [P.sb(f"Sb{h}", [128, 128], BF16) for h in range(8)]
        for h in range(8):
            P.memset("dve", S_f[h][:], 0.0)
            P.memset("dve", S_b[h][:], 0.0)
        xc = P.sb("xc", [128, 24, 131], F32)
        P.memset("dve", xc[:], 0.0)
        xtm = Pool(P, "xtm", 2, [128, D], F32)
        hTp = Pool(P, "hTm", 2, [128, 8, 128], BF16)
        accp = Pool(P, "cacc", 2, [128, 4, 128], F32)
        dd = P.sb("dd", [128, 16, 128], F32)
        ctmp = P.sb("ctmp", [128, 128], F32)
        qkv_s = P.sb("qkvs", [128, 24, 128], BF16)
        qkv_t = P.sb("qkvt", [128, 24, 128], BF16)
        sq = dd[:]
        sc = P.sb("sc", [128, 16, 8], F32)
        (RN_Q, RN_K, BETA, G, GC, GLAST, EG, EK, CKB, CKBEG, CKDEC, CQDEC, TMP, TMP2, ORS, Z) = range(16)
        glbe = P.sb("glbe", [128, 2, 8], F32)
        names = ["kn", "kb", "qn", "qdec", "vb", "kbeg", "kdec"]
        big = P.sb("big", [128, 4096], BF16)
        sop = {n: big[:, k_ * 1024:(k_ + 1) * 1024].rearrange("p (h t) -> p h t", h=8)
               for k_, n in enumerate(names[:4])}
        for n in names[4:]:
            sop[n] = P.sb("so_" + n, [128, 8, 128], BF16)[:]
        diagG = dd[:, 0:8, :]
        dtarg = dd[:, 8:16, :]
        e1 = dtarg
        e2 = diagG
        DTs = P.sb("DTs", [128, 8, 128], F32)
        DTi = P.sb("DTi", [128, 8, 128], BF16)
        Ds = P.sb("Ds", [128, 8, 128], F32)
        fms = [P.sb(f"fm{q}", [128, 3, 128], BF16) for q in range(4)]
        qdecT = P.sb("qdecT", [128, 8, 128], BF16)
        qkT = P.sb("qkT", [128, 8, 128], BF16)
        Mxs = [P.sb(f"Mx{q}", [128, 128], F32) for q in range(4)]
        Axs = [P.sb(f"Ax{q}", [128, 128], F32) for q in range(4)]
        Xs = [P.sb(f"Xx{q}", [128, 128], F32) for q in range(4)]
        Xbs = [P.sb(f"Xb{q}", [128, 128], BF16) for q in range(4)]
        Pns = [[P.sb(f"Pn{q}_{k_}", [128, 2, 128], F32) for k_ in range(2)] for q in range(4)]
        u_all = P.sb("uall", [128, 8, 128], F32)
        wT_all = P.sb("wTall", [128, 8, 128], BF16)
        vnew = P.sb("vnew", [128, 8, 128], BF16)
        o_tok = big[:, 0:2048].bitcast(F32).rearrange("p (h t) -> p h t", h=8)
        sgp = Pool(P, "sgate", 2, [128, D], BF16)
        ofin = big[:, 2048:3072]
        ofinT = big[:, 3072:4096].rearrange("p (h t) -> p h t", h=8)
        sk, dk = src.tensor.name, dst.tensor.name

        def bc(col):
            return sc[:, col, :].unsqueeze(2).to_broadcast([128, 8, 128])

        def front_a(i):
            st = {}

            def s0():
                st["xt"] = xtm.next()
                P.dma("sp", st["xt"][:], src[i * 128:(i + 1) * 128, :], in_key=f"{sk}:{i}")
                st["hT"] = hTp.next()
                self.rmsnorm_T(st["xt"], gcol, st["hT"], 0, bank[7])
                st["sg"] = sgp.next()

            def proj_pe(g):
                hT = st["hT"]
                pb = bank[4 + g % 2]
                for mm_ in range(4):
                    m = 4 * g + mm_
                    for c in range(8):
                        P.mm(pb[:, mm_ * 128:(mm_ + 1) * 128], w_in[:, c, m * 128:(m + 1) * 128], hT[:, c, :],
                             start=(c == 0), stop=(c == 7))

            def proj_post(g):
                pb = bank[4 + g % 2]
                P.act(xc[:, 4 * g:4 * g + 4, 3:131], pb[:].rearrange("p (m t) -> p m t", m=4), AF.Copy)
                acc = accp.next()
                for mm_ in range(4):
                    m = 4 * g + mm_
                    P.act(acc[:, mm_, :], pb[:, mm_ * 128:(mm_ + 1) * 128], AF.Copy, scale=convw[:, m, 3:4])
                for mm_ in range(4):
                    m = 4 * g + mm_
                    for tp in range(3):
                        P.stt("dve", acc[:, mm_, :], xc[:, m, tp:tp + 128], convw[:, m, tp:tp + 1], acc[:, mm_, :],
                              ALU.mult, ALU.add)
                P.act(qkv_s[:, 4 * g:4 * g + 4, :], acc[:], AF.Silu)
                if g == 5:
                    P.copy("pool", xc[:, :, 0:3], xc[:, :, 128:131])

            def gate_pe(hf):
                hT = st["hT"]
                pgt = bank[6 + hf]
                for c in range(8):
                    P.mm(pgt[:, :], hT[:, c, :], w_in[:, c, 3072 + hf * 512:3072 + (hf + 1) * 512],
                         start=(c == 0), stop=(c == 7))

            def gate_post(hf):
                P.act(st["sg"][:, hf * 512:(hf + 1) * 512], bank[6 + hf][:, :], AF.Silu)

            sl = [s0, lambda: proj_pe(0)]
            for g in range(1, 6):
                sl.append(lambda g=g: (proj_post(g - 1), proj_pe(g)))
            sl.append(lambda: (proj_post(5), gate_pe(0), gate_pe(1)))
            sl.append(lambda: (gate_post(0), gate_post(1)))
            return st, sl

        st_next, sl0 = front_a(0)
        for f_ in sl0:
            f_()
        for i in range(ntiles):
            st_cur = st_next
            xt, hT, sgate = st_cur["xt"], st_cur["hT"], st_cur["sg"]
            pending = []
            if i + 1 < ntiles:
                st_next, pending = front_a(i + 1)
            for t3 in range(3):
                pst = bank[2 + t3][:].bitcast(BF16)
                for h in range(8):
                    P.tr(pst[:, h * 128:(h + 1) * 128], qkv_s[:, t3 * 8 + h, :], self.ident_bf[:])
                P.copy("act" if t3 != 1 else "dve", qkv_t[:, t3 * 8:(t3 + 1) * 8, :],
                       pst.rearrange("p (h t) -> p h t", h=8))
            P.tt("dve", sq, qkv_t[:, 0:16, :], qkv_t[:, 0:16, :], ALU.mult)
            rn = sc[:, RN_Q:RN_K + 1, :]
            P.reduce("dve", rn, sq.rearrange("p (a h) t -> p a h t", a=2), ALU.add)
            P.ts("dve", rn, rn, EPS, ALU.add)
            P.act(rn, rn, AF.Ln)
            P.act(rn, rn, AF.Exp, scale=-0.5)
            P.ts("dve", sc[:, RN_Q, :], sc[:, RN_Q, :], 128.0 ** -0.5, ALU.mult)
            pba = bank[5]
            for c in range(8):
                P.mm(pba[:, 0:16], hT[:, c, :], w_in[:, c, 4096:4112], start=(c == 0), stop=(c == 7))
            P.act(sc[:, BETA, :], pba[:, 0:8], AF.Exp, scale=-1.0)
            P.ts("dve", sc[:, BETA, :], sc[:, BETA, :], 1.0, ALU.add)
            P.op("dve", lambda g_: g_.reciprocal(out=sc[:, BETA, :], in_=sc[:, BETA, :]), [sc], [sc])
            P.tt("dve", sc[:, Z, :], pba[:, 8:16], dtb[:], ALU.add)
            P.act(sc[:, Z, :], sc[:, Z, :], AF.Exp)
            P.ts("dve", sc[:, Z, :], sc[:, Z, :], 1.0, ALU.add)
            P.act(sc[:, Z, :], sc[:, Z, :], AF.Ln)
            P.tt("dve", sc[:, G, :], sc[:, Z, :], negA[:], ALU.mult)
            P.mm(pba[:, 16:24], cm[:, 0, :], sc[:, G, :])
            P.mm(pba[:, 24:32], cm[:, 1, :], sc[:, G, :])
            P.copy("dve", sc[:, GC:GLAST + 1, :], pba[:, 16:32].rearrange("p (a h) -> p a h", a=2))
            P.mm(pba[:, 32:40], cm[:, 2, :], sc[:, GC, :])
            P.mm(pba[:, 40:48], cm[:, 3, :], sc[:, GC, :])
            P.act(glbe[:], pba[:, 32:48].rearrange("p (a h) -> p a h", a=2), AF.Exp)
            P.act(sc[:, EG, :], sc[:, GC, :], AF.Exp)
            P.tt("dve", sc[:, TMP, :], sc[:, GLAST, :], sc[:, GC, :], ALU.subtract)
            P.act(sc[:, EK, :], sc[:, TMP, :], AF.Exp)
            P.tt("dve", sc[:, CKB, :], sc[:, RN_K, :], sc[:, BETA, :], ALU.mult)
            P.tt("dve", sc[:, CKBEG, :], sc[:, CKB, :], sc[:, EG, :], ALU.mult)
            P.tt("dve", sc[:, CKDEC, :], sc[:, RN_K, :], sc[:, EK, :], ALU.mult)
            P.tt("dve", sc[:, CQDEC, :], sc[:, RN_Q, :], sc[:, EG, :], ALU.mult)
            qt, kt, vt = qkv_t[:, 0:8, :], qkv_t[:, 8:16, :], qkv_t[:, 16:24, :]
            P.tt("dve", sop["kn"][:], kt, bc(RN_K), ALU.mult)
            P.tt("dve", sop["kb"][:], kt, bc(CKB), ALU.mult)
            P.tt("dve", sop["qn"][:], qt, bc(RN_Q), ALU.mult)
            P.tt("dve", sop["qdec"][:], qt, bc(CQDEC), ALU.mult)
            P.tt("pool", sop["vb"][:], vt, bc(BETA), ALU.mult)
            P.tt("pool", sop["kbeg"][:], kt, bc(CKBEG), ALU.mult)
            P.tt("pool", sop["kdec"][:], kt, bc(CKDEC), ALU.mult)
            P.tt("dve", diagG[:], self.ident_f[:].unsqueeze(1).to_broadcast([128, 8, 128]), bc(GC), ALU.mult)
            for hh in range(2):
                P.mm(bank[6 + hh][:, :], ones[:], diagG[:, 4 * hh:4 * hh + 4, :].rearrange("p h t -> p (h t)"))
                P.tt("dve", dtarg[:, 4 * hh:4 * hh + 4, :], bank[6 + hh][:].rearrange("p (h t) -> p h t", h=4),
                     sc[:, GC, 4 * hh:4 * hh + 4].unsqueeze(2).to_broadcast([128, 4, 128]), ALU.subtract)
            P.ts("dve", e2[:], dtarg[:], -1.0, ALU.mult, 0.0, ALU.min)
            P.act(e2[:], e2[:], AF.Exp)
            P.ts("dve", e1[:], dtarg[:], 0.0, ALU.min)
            P.act(e1[:], e1[:], AF.Exp)
            P.tt("dve", DTs[:], e1[:], masks[:, 0:1, :].to_broadcast([128, 8, 128]), ALU.mult)
            P.tt("dve", DTi[:], e1[:], masks[:, 1:2, :].to_broadcast([128, 8, 128]), ALU.mult)
            P.tt("dve", Ds[:], e2[:], masks[:, 2:3, :].to_broadcast([128, 8, 128]), ALU.mult)
            for hg in range(2):
                hs = [4 * hg + q for q in range(4)]
                for q, h in enumerate(hs):
                    pt = bank[q][:].bitcast(BF16)
                    for k_, n_ in enumerate(["kn", "kb", "qn", "qdec"]):
                        P.tr(pt[:, k_ * 128:(k_ + 1) * 128], sop[n_][:, h, :], self.ident_bf[:])
                if pending:
                    pending.pop(0)()
                for q, h in enumerate(hs):
                    pt = bank[q][:].bitcast(BF16)
                    P.copy("act", fms[q][:], pt[:, 0:384].rearrange("p (a t) -> p a t", a=3))
                    P.copy("act", qdecT[:, h, :], pt[:, 384:512])
                if pending:
                    pending.pop(0)()
                for q, h in enumerate(hs):
                    pg = bank[q]
                    fm = fms[q]
                    P.mm(pg[:, 0:256], fm[:, 0, :], fm[:, 1:3, :].rearrange("p a t -> p (a t)"))
                    P.mm(pg[:, 256:384], fm[:, 1, :], fm[:, 0, :])
                for q, h in enumerate(hs):
                    pg = bank[q]
                    P.tt("dve", Mxs[q][:], pg[:, 0:128], DTs[:, h, :], ALU.mult)
                    P.tt("dve", qkT[:, h, :], pg[:, 128:256], DTi[:, h, :], ALU.mult)
                    P.tt("dve", Axs[q][:], pg[:, 256:384], Ds[:, h, :], ALU.mult)
                    P.tt("dve", Xs[q][:], self.ident_f[:], Mxs[q][:], ALU.subtract)
                Pm = [Mxs[q][:] for q in range(4)]
                PTm = [Axs[q][:] for q in range(4)]
                for lvl in range(1, 6):
                    for q in range(4):
                        pp = bank[q]
                        if lvl < 5:
                            P.mm(pp[:, 0:128], PTm[q], Pm[q])
                        P.mm(pp[:, 128:256], Pm[q], PTm[q])
                    for q in range(4):
                        pp = bank[q]
                        pn = Pns[q][lvl % 2]
                        if lvl < 5:
                            P.copy("act", pn[:], pp[:, 0:256].rearrange("p (a t) -> p a t", a=2))
                        else:
                            P.copy("act", pn[:, 1, :], pp[:, 128:256])
                    if pending and lvl in (1, 3, 5):
                        pending.pop(0)()
                    for q in range(4):
                        pn = Pns[q][lvl % 2]
                        P.mm(bank[q][:, 256:384], pn[:, 1, :], Xs[q][:])
                    for q in range(4):
                        P.tt("dve", Xs[q][:], Xs[q][:], bank[q][:, 256:384], ALU.add)
                        pn = Pns[q][lvl % 2]
                        Pm[q], PTm[q] = pn[:, 0, :], pn[:, 1, :]
                for q in range(4):
                    P.copy("act", Xbs[q][:], Xs[q][:])
                for q, h in enumerate(hs):
                    pu = bank[q]
                    P.mm(pu[:, 0:128], Xbs[q][:], sop["vb"][:, h, :])
                    P.mm(pu[:, 128:256], sop["kbeg"][:, h, :], Xbs[q][:])
                for q, h in enumerate(hs):
                    pu = bank[q]
                    P.copy("act", u_all[:, h, :], pu[:, 0:128])
                    P.copy("act", wT_all[:, h, :], pu[:, 128:256])
            while pending:
                pending.pop(0)()
            for c in range(2):
                rows = slice(64 * c, 64 * c + 64)
                for h in range(8):
                    P.mm(bank[h][:, 0:128], wT_all[:, h, :], S_b[h][:])
                for h in range(8):
                    P.tt("dve", vnew[rows, h, :], u_all[rows, h, :], bank[h][rows, 0:128], ALU.subtract)
                for h in range(8):
                    P.mm(bank[h][:, 128:256], qdecT[:, h, :], S_b[h][:], start=True, stop=False)
                    P.mm(bank[h][:, 128:256], qkT[rows, h, :], vnew[rows, h, :], start=False, stop=True)
                    P.mm(bank[h][:, 256:384], sop["kdec"][rows, h, :], vnew[rows, h, :])
                for h in range(8):
                    P.stt("dve", S_f[h][:], S_f[h][:], glbe[:, c, h:h + 1], bank[h][:, 256:384], ALU.mult, ALU.add)
                    P.copy("act", S_b[h][:], S_f[h][:])
                    P.copy("act", o_tok[rows, h, :], bank[h][rows, 128:256])
            P.tt("dve", sq[:, 0:8, :], o_tok, o_tok, ALU.mult)
            P.reduce("dve", sc[:, ORS, :], sq[:, 0:8, :], ALU.add)
            P.ts("dve", sc[:, ORS, :], sc[:, ORS, :], 1.0 / 128, ALU.mult, EPS, ALU.add)
            P.act(sc[:, ORS, :], sc[:, ORS, :], AF.Ln)
            P.act(sc[:, ORS, :], sc[:, ORS, :], AF.Exp, scale=-0.5)
            P.tt("dve", o_tok, o_tok, bc(ORS), ALU.mult)
            P.tt("dve", o_tok, o_tok, onorm[:].unsqueeze(1).to_broadcast([128, 8, 128]), ALU.mult)
            P.tt("dve", ofin, o_tok.rearrange("p h t -> p (h t)"), sgate[:], ALU.mult)
            pst = bank[2][:].bitcast(BF16)
            for c in range(8):
                P.tr(pst[:, c * 128:(c + 1) * 128], ofin[:, c * 128:(c + 1) * 128], self.ident_bf[:])
            P.copy("act", ofinT, pst.rearrange("p (c t) -> p c t", c=8))
            for hf in range(2):
                py = bank[4 + hf]
                for c in range(8):
                    P.mm(py[:, :], ofinT[:, c, :], w_out[:, c, hf * 512:(hf + 1) * 512], start=(c == 0), stop=(c == 7))
                P.tt("dve", xt[:, hf * 512:(hf + 1) * 512], xt[:, hf * 512:(hf + 1) * 512], py[:, :], ALU.add)
            P.dma("sp", dst[i * 128:(i + 1) * 128, :], xt[:], out_key=f"{dk}:{i}")
        P.pop()

    def kv_pass(self, src, dst):
        P = self.P
        P.push()
        bank = self.bank
        self.norm_pools()
        kvw = self.din("kv_w", [D, 512])
        wk2 = P.sb("wk2", [128, 8, 4, 128], BF16)
        for hk in range(4):
            for dup in range(2):
                P.dma("pool", wk2[:, :, hk, dup * 64:(dup + 1) * 64],
                      kvw[:, hk * 64:(hk + 1) * 64].rearrange("(c p) m -> p c m", p=128))
        wv = P.sb("wv", [128, 8, 256], BF16)
        P.dma("pool", wv[:], kvw[:, 256:512].rearrange("(c p) m -> p c m", p=128))
        kb2 = self.const_sb("kbias2", [128, 4])
        vbc = self.const_sb("vbias_bc", [128, 256])
        self.KT2 = P.sb("KT2", [128, 4, (NT_HALF + 1) * 128], BF16)
        self.Vsb = P.sb("Vsb", [128, NT_HALF + 1, 256], BF16)
        gcol = self.const_sb("kv_gcol", [128, 8])
        xlp = Pool(P, "xl", 2, [128, D], F32)
        xhp = Pool(P, "xh", 2, [128, D], F32)
        hTp = Pool(P, "hTk", 2, [128, 8, 128], BF16)
        sk, dk = src.tensor.name, dst.tensor.name
        f, omf = self.flag[:, 0:1], self.flag[:, 1:2]
        for r in range(-1, NT_HALF):
            lo, hi = max(r, 0), NT_HALF + r
            xl, xh = xlp.next(), xhp.next()
            P.dma("sp", xl[:], src[lo * 128:(lo + 1) * 128, :], in_key=f"{sk}:{lo}")
            P.dma("sp", xh[:], src[hi * 128:(hi + 1) * 128, :], in_key=f"{sk}:{hi}")
            P.ts("dve", xl[:], xl[:], omf, ALU.mult)
            P.stt("dve", xl[:], xh[:], f, xl[:], ALU.mult, ALU.add)
            if r >= 0:
                P.dma("sp", dst[r * 128:(r + 1) * 128, :], xl[:], out_key=f"{dk}:{r}")
            hT = hTp.next()
            self.rmsnorm_T(xl, gcol[:], hT, 0, bank[7])
            for hk in range(4):
                pk = bank[hk % 2]
                for c in range(8):
                    P.mm(pk[:, 0:128], wk2[:, c, hk, :], hT[:, c, :], start=(c == 0), stop=(c == 7))
                P.ts("dve", self.KT2[:, hk, (r + 1) * 128:(r + 2) * 128], pk[:, 0:128], kb2[:, hk:hk + 1], ALU.add)
            pv = bank[2]
            for c in range(8):
                P.mm(pv[:, 0:256], hT[:, c, :], wv[:, c, :], start=(c == 0), stop=(c == 7))
            P.tt("dve", self.Vsb[:, r + 1, :], pv[:, 0:256], vbc[:], ALU.add)
        P.dma("sp", self.kt2_d, self.KT2[:].rearrange("p h k -> p (h k)"))
        P.dma("sp", self.vsb_d, self.Vsb[:].rearrange("p t d -> p (t d)"))
        P.pop()

    def swa_pass(self, l, j, src, dst):
        P = self.P
        P.push()
        bank = self.bank
        self.norm_pools()
        wq_d = self.din(f"b_w_q{j}", [D, D])
        wo_d = self.din(f"b_w_out{j}", [D, D])
        wq = P.sb("wq", [128, 8, D], BF16)
        wo = P.sb("wo", [128, 8, D], BF16)
        for c in range(8):
            P.dma("pool", wq[:, c, :], wq_d[c * 128:(c + 1) * 128, :])
        for c in range(8):
            P.dma("pool", wo[:, c, :], wo_d[c * 128:(c + 1) * 128, :])
        self.KT2 = P.sb("KT2", [128, 4, (NT_HALF + 1) * 128], BF16)
        self.Vsb = P.sb("Vsb", [128, NT_HALF + 1, 256], BF16)
        P.dma("sp", self.KT2[:].rearrange("p h k -> p (h k)"), self.kt2_d)
        P.dma("sp", self.Vsb[:].rearrange("p t d -> p (t d)"), self.vsb_d)
        bq = self.const_sb(f"bq{j}", [128, 8])
        sinks = self.const_sb(f"sinks{j}", [128, 16])
        mb = self.const_sb("maskbias", [128, 2, 256])
        gcol = self.gc_mix[:, l * 8:(l + 1) * 8]
        xtm = Pool(P, "xts", 2, [128, D], F32)
        hTp = Pool(P, "hTs", 2, [128, 8, 128], BF16)
        qT = P.sb("qT", [128, 8, 128], BF16)
        smp = Pool(P, "sm", 4, [128, 2, 256], F32)
        pp_ = Pool(P, "pp", 4, [128, 2, 256], F32)
        pnp = Pool(P, "pn", 4, [128, 2, 256], BF16)
        pTp = Pool(P, "pT", 4, [128, 4, 128], BF16)
        stp = Pool(P, "st", 4, [128, 8, 2], F32)
        oT = P.sb("oT", [128, 8, 128], BF16)
        sk, dk = src.tensor.name, dst.tensor.name
        for r in range(SWA_TILES):
            xt = xtm.next()
            P.dma("sp", xt[:], src[r * 128:(r + 1) * 128, :], in_key=f"{sk}:{r}")
            hT = hTp.next()
            self.rmsnorm_T(xt, gcol, hT, 0, bank[7])
            for m in range(8):
                pq = bank[m % 2]
                for c in range(8):
                    P.mm(pq[:, 0:128], wq[:, c, m * 128:(m + 1) * 128], hT[:, c, :], start=(c == 0), stop=(c == 7))
                P.ts("dve", qT[:, m, :], pq[:, 0:128], bq[:, m:m + 1], ALU.add, 0.125, ALU.mult)
            mrow = 0 if r == 0 else 1
            for gp in range(4):
                prs = [2 * gp, 2 * gp + 1]
                bufs = {}
                for k_, pr in enumerate(prs):
                    hk = pr // 2
                    for e in range(2):
                        P.mm(bank[2 + 2 * k_ + e][:, 0:256], qT[e * 64:(e + 1) * 64, pr, :],
                             self.KT2[e * 64:(e + 1) * 64, hk, r * 128:(r + 2) * 128])
                    bufs[pr] = (smp.next(), pp_.next(), pnp.next(), pTp.next(), stp.next())
                for k_, pr in enumerate(prs):
                    sm = bufs[pr][0]
                    for e in range(2):
                        P.tt("dve", sm[:, e, :], bank[2 + 2 * k_ + e][:, 0:256], mb[:, mrow, :], ALU.add)
                for pr in prs:
                    sm, p_, pn, pT, st = bufs[pr]
                    P.reduce("dve", st[:, 0, :], sm[:], ALU.max)
                    P.tt("dve", st[:, 0, :], st[:, 0, :], sinks[:, 2 * pr:2 * pr + 2], ALU.max)
                    P.ts("dve", st[:, 1, :], st[:, 0, :], -1.0, ALU.mult)
                    P.tt("dve", st[:, 3, :], sinks[:, 2 * pr:2 * pr + 2], st[:, 0, :], ALU.subtract)
                for pr in prs:
                    sm, p_, pn, pT, st = bufs[pr]
                    for e in range(2):
                        P.act(p_[:, e, :], sm[:, e, :], AF.Exp, bias=st[:, 1, e:e + 1], accum_out=st[:, 2, e:e + 1])
                    P.act(st[:, 3, :], st[:, 3, :], AF.Exp)
                for pr in prs:
                    sm, p_, pn, pT, st = bufs[pr]
                    P.tt("dve", st[:, 4, :], st[:, 3, :], st[:, 2, :], ALU.add)
                    P.op("dve", lambda g_, st=st: g_.reciprocal(out=st[:, 5, :], in_=st[:, 4, :]), [st], [st])
                    P.tt("dve", pn[:], p_[:], st[:, 5, :].unsqueeze(2).to_broadcast([128, 2, 256]), ALU.mult)
                for k_, pr in enumerate(prs):
                    sm, p_, pn, pT, st = bufs[pr]
                    ptb = bank[6 + k_][:].bitcast(BF16)
                    for e in range(2):
                        for kt in range(2):
                            q_ = e * 2 + kt
                            P.tr(ptb[:, q_ * 128:(q_ + 1) * 128], pn[:, e, kt * 128:(kt + 1) * 128], self.ident_bf[:])
                for k_, pr in enumerate(prs):
                    pT = bufs[pr][3]
                    ptb = bank[6 + k_][:].bitcast(BF16)
                    P.copy("act", pT[:], ptb[:, 0:512].rearrange("p (a t) -> p a t", a=4))
                for k_, pr in enumerate(prs):
                    pT = bufs[pr][3]
                    hk = pr // 2
                    po = bank[k_]
                    for e in range(2):
                        for kt in range(2):
                            P.mm(po[e * 64:(e + 1) * 64, 0:128], self.Vsb[:, r + kt, hk * 64:(hk + 1) * 64],
                                 pT[:, e * 2 + kt, :], start=(kt == 0), stop=(kt == 1))
                for k_, pr in enumerate(prs):
                    P.copy("act", oT[:, pr, :], bank[k_][:, 0:128])
            for hf in range(2):
                py = bank[hf]
                for c in range(8):
                    P.mm(py[:, :], oT[:, c, :], wo[:, c, hf * 512:(hf + 1) * 512], start=(c == 0), stop=(c == 7))
                P.tt("dve", xt[:, hf * 512:(hf + 1) * 512], xt[:, hf * 512:(hf + 1) * 512], py[:, :], ALU.add)
            P.dma("sp", dst[r * 128:(r + 1) * 128, :], xt[:], out_key=f"{dk}:{r}")
        P.pop()

    def build(self, out_tokens=TOK):
        P = self.P
        self.x_in = self.din("x", [SEQ, D])
        self.out = P.dram("out", [out_tokens, D], F32, "ExternalOutput")
        self.din("ident", [128, 128])
        self.setup_common()
        self.xs = P.dram("xs", [SEQ, D], F32, "Internal")
        self.xs2 = P.dram("xs2", [TOK, D], F32, "Internal")
        self.kt2_d = P.dram("kt2_d", [128, 4 * (NT_HALF + 1) * 128], BF16, "Internal")
        self.vsb_d = P.dram("vsb_d", [128, (NT_HALF + 1) * 256], BF16, "Internal")
        cur = self.x_in
        full = True
        for si, stg in enumerate(self.stages):
            last = si == len(self.stages) - 1
            kind = stg[0]
            if kind == "kv":
                dst = self.out if last else self.xs2
                self.kv_pass(cur, dst)
                full = False
            else:
                dst = self.out if last else (self.xs if full else self.xs2)
                if kind == "ffn":
                    self.ffn_pass(stg[1], cur, dst, SEQ if full else TOK, final=(len(stg) > 2 and stg[2]))
                elif kind == "dn":
                    self.dn_pass(stg[1], cur, dst, NT_ALL if full else NT_HALF)
                elif kind == "swa":
                    self.swa_pass(stg[1], stg[1] - 2, cur, dst)
            cur = dst
        P.finish()
        return self.nc


def colmajor(v, nchunk):
    return np.ascontiguousarray(np.asarray(v, dtype=np.float32).reshape(nchunk, 128).T)


def rep128(v):
    v = np.asarray(v, dtype=np.float32).reshape(1, -1)
    return np.ascontiguousarray(np.broadcast_to(v, (128, v.shape[1])))


def const_tables():
    idx = np.arange(128)
    same = (idx[:, None] // 64) == (idx[None, :] // 64)
    up_s = same & (idx[None, :] > idx[:, None])
    up_i = same & (idx[None, :] >= idx[:, None])
    lo_s = same & (idx[:, None] > idx[None, :])
    masks = np.stack([up_s, up_i, lo_s], axis=1).astype(np.float32)
    sel0 = np.zeros((128, 128), np.float32); sel0[63, :] = 1.0
    sel1 = np.zeros((128, 128), np.float32); sel1[127, :] = 1.0
    cm = np.stack([up_i.astype(np.float32), same.astype(np.float32), sel0, sel1], axis=1)
    qi = np.arange(128)[:, None]
    ki = np.arange(256)[None, :]
    in_win = (ki > qi) & (ki <= qi + 128)
    normal = np.where(in_win, 0.0, NEG).astype(np.float32)
    first = np.where(in_win & (ki >= 128), 0.0, NEG).astype(np.float32)
    return masks, cm, normal, first


def host_inputs(inputs, needed):
    g = lambda k: np.asarray(inputs[k], dtype=np.float32)
    masks, cm, normal, first = const_tables()
    common = {"ident": np.eye(128, dtype=np.float32), "masks": masks, "cmats": cm,
              "ones": np.ones((128, 128), np.float32)}
    if "ffn_gcols" in needed:
        common["ffn_gcols"] = np.ascontiguousarray(
            np.concatenate([colmajor(g("ffn_norm")[i // 2, i % 2], 8) for i in range(8)], axis=1))
    if "mix_gcols" in needed:
        common["mix_gcols"] = np.ascontiguousarray(
            np.concatenate([colmajor(g("mix_norm")[i], 8) for i in range(4)], axis=1))
    for i in range(8):
        for nm, src in (("wg", "ffn_w_gate"), ("wu", "ffn_w_up"), ("wd", "ffn_w_down")):
            if f"{nm}{i}" in needed:
                common[f"{nm}{i}"] = np.ascontiguousarray(g(src)[i // 2, i % 2])
    for l in range(2):
        if f"a_w_in{l}" in needed:
            common[f"a_w_in{l}"] = np.ascontiguousarray(g("a_w_in")[l])
            common[f"a_w_out{l}"] = np.ascontiguousarray(g("a_w_out")[l])
            cw = g("a_conv")[l]
            common[f"convw{l}"] = np.ascontiguousarray(cw.reshape(4, 24, 128).transpose(2, 1, 0))
            common[f"dtb{l}"] = rep128(g("a_dt_bias")[l])
            common[f"alog{l}"] = rep128(g("a_A_log")[l])
            common[f"onorm{l}"] = rep128(g("a_o_norm")[l])
    if "kv_w" in needed:
        common["kv_w"] = np.ascontiguousarray(g("kv_w"))
        kvb = g("kv_b")
        kb2 = np.stack([np.tile(kvb[hk * 64:(hk + 1) * 64], 2) for hk in range(4)], axis=1)
        common["kbias2"] = np.ascontiguousarray(kb2)
        common["vbias_bc"] = rep128(kvb[256:512])
        common["kv_gcol"] = colmajor(g("kv_norm"), 8)
    for j in range(2):
        if f"b_w_q{j}" in needed:
            common[f"b_w_q{j}"] = np.ascontiguousarray(g("b_w_q")[j])
            common[f"b_w_out{j}"] = np.ascontiguousarray(g("b_w_out")[j])
            common[f"bq{j}"] = colmajor(g("b_b_q")[j], 8)
            common[f"sinks{j}"] = rep128(g("b_sinks")[j])
    if "final_bc" in needed:
        common["final_bc"] = rep128(g("final_norm"))
    x = g("x")
    maps = []
    for c in range(NCORES):
        hf = c % 2
        m = dict(common)
        m["x"] = np.ascontiguousarray(x[c // 2])
        m["flag"] = np.ascontiguousarray(np.broadcast_to(np.array([[float(hf), 1.0 - hf]], np.float32), (128, 2)))
        m["maskbias"] = np.ascontiguousarray(np.stack([normal if hf == 1 else first, normal], axis=1))
        maps.append({k: v for k, v in m.items() if k in needed})
    return maps


FULL_STAGES = [("ffn", 0), ("dn", 0), ("ffn", 1), ("ffn", 2), ("dn", 1), ("ffn", 3), ("kv",),
               ("ffn", 4), ("swa", 2), ("ffn", 5), ("ffn", 6), ("swa", 3), ("ffn", 7, True)]


def run(stages, inputs, out_tokens=TOK, trace=False):
    b = Builder(stages)
    nc = b.build(out_tokens)
    maps = host_inputs(inputs, set(b.inputs))
    res = run_bass_kernel_spmd(nc, maps, core_ids=list(range(NCORES)), trace=trace)
    return [res.results[c]["out"] for c in range(NCORES)], res, b


def kernel(**inputs):
    outs, _, _ = run(FULL_STAGES, inputs)
    out = np.zeros((4, SEQ, D), np.float32)
    for c in range(NCORES):
        out[c // 2, (c % 2) * TOK:(c % 2 + 1) * TOK, :] = outs[c]
    return out
```

```python
import contextlib
import numpy as np
import concourse.bass as bass
import concourse.mybir as mybir
from concourse.bass_utils import run_bass_kernel_spmd

F32 = mybir.dt.float32
BF16 = mybir.dt.bfloat16
AF = mybir.ActivationFunctionType
ALU = mybir.AluOpType
AX = mybir.AxisListType

D = 1024
DFF = 2816
NJ = DFF // 128
TOK = 4096
EPS = 1e-6
NCORES = 8


class Tok:
    __slots__ = ("sem", "val", "eng", "sem_key")

    def __init__(self, sem, val, eng):
        self.sem, self.val, self.eng = sem, val, eng


class BufState:
    def __init__(self):
        self.last_w = None
        self.reads = {}


class Prog:
    def __init__(self, nc):
        self.nc = nc
        self.es = contextlib.ExitStack()
        self.eng = {"pe": nc.tensor, "act": nc.scalar, "dve": nc.vector, "pool": nc.gpsimd, "sp": nc.sync}
        self.sem = {}
        self.cnt = {}
        for e in self.eng:
            self.sem[e] = self.es.enter_context(nc.semaphore("sem_" + e))
            self.cnt[e] = 0
        self.waited = {e: {} for e in self.eng}
        self.bufs = {}
        self.dsems = {}
        self.n_ins = 0
        self._uid = 0
        self.stack = [self.es]
        self.semkey_of = {}

    def push(self):
        es = contextlib.ExitStack()
        self.stack.append(es)
        return es

    def pop(self):
        self.barrier()
        self.stack.pop().close()

    def barrier(self):
        for e in self.eng:
            for o in self.eng:
                if o != e and self.cnt[o] > 0:
                    self._wait(e, Tok(self.sem[o], self.cnt[o], o))
            for k, d in self.dsems.items():
                if d[1] > 0:
                    t = Tok(d[0], d[1], "dma")
                    t.sem_key = k
                    self._wait(e, t)

    def sb(self, name, shape, dt):
        self._uid += 1
        t = self.stack[-1].enter_context(self.nc.sbuf_tensor(f"{name}_u{self._uid}", list(shape), dt))
        self.semkey_of[t.name] = name
        return t

    def ps(self, name, shape, dt):
        return self.stack[-1].enter_context(self.nc.psum_tensor(name, list(shape), dt))

    def key(self, x):
        if isinstance(x, str):
            return x
        if isinstance(x, tuple):
            return x[1]
        return getattr(x, 'tensor', x).name

    def st(self, k):
        s = self.bufs.get(k)
        if s is None:
            s = self.bufs[k] = BufState()
        return s

    def _wait(self, e, tok, war=False):
        if tok is None:
            return
        if tok.eng == e and (e == "pe" or war or e == "sp"):
            return
        sid = id(tok.sem)
        val = tok.val
        if tok.eng == "dma":
            val = max(val, self.dsems[tok.sem_key][1])
        if self.waited[e].get(sid, 0) >= val:
            return
        self.eng[e].wait_ge(tok.sem, val)
        self.waited[e][sid] = val

    def _deps(self, e, reads, writes):
        for r in reads:
            s = self.st(self.key(r))
            self._wait(e, s.last_w)
        for w in writes:
            s = self.st(self.key(w))
            self._wait(e, s.last_w)
            for t in s.reads.values():
                self._wait(e, t, war=True)

    def _commit(self, tok, reads, writes):
        for r in reads:
            s = self.st(self.key(r))
            s.reads[tok.eng if tok.eng != "dma" else id(tok.sem)] = tok
        for w in writes:
            s = self.st(self.key(w))
            s.last_w = tok
            s.reads = {}

    def op(self, e, fn, reads, writes):
        reads = [r for r in reads if r is not None and not isinstance(r, (int, float))]
        self._deps(e, reads, writes)
        ins = fn(self.eng[e])
        self.cnt[e] += 1
        ins.then_inc(self.sem[e], 1)
        tok = Tok(self.sem[e], self.cnt[e], e)
        self._commit(tok, reads, writes)
        self.n_ins += 1
        return ins

    def dma(self, q, out, in_, out_key=None, in_key=None, sem_key=None):
        ok = out_key or self.key(out)
        ik = in_key or self.key(in_)
        sb_side = out.tensor.name if out.tensor.name not in self.dram_names else in_.tensor.name
        sk = sem_key or self.semkey_of.get(sb_side, sb_side)
        if sk not in self.dsems:
            self.dsems[sk] = [self.es.enter_context(self.nc.semaphore("d_" + sk.replace(":", "_"))), 0]
        d = self.dsems[sk]
        self._deps(q, [ik], [ok])
        ins = self.eng[q].dma_start(out=out, in_=in_)
        d[1] += 16
        ins.then_inc(d[0], 16)
        tok = Tok(d[0], d[1], "dma")
        tok_key = sk
        tok.sem_key = tok_key
        self._commit(tok, [ik], [ok])
        self.n_ins += 1
        return ins

    dram_names = set()

    def dram(self, name, shape, dt, kind):
        self.dram_names.add(name)
        return self.nc.dram_tensor(name, list(shape), dt, kind=kind).ap()

    def mm(self, out, lhsT, rhs, start=True, stop=True, **kw):
        return self.op("pe", lambda e: e.matmul(out, lhsT, rhs, start=start, stop=stop, **kw), [lhsT, rhs], [out])

    def tr(self, out, in_, ident):
        return self.op("pe", lambda e: e.transpose(out, in_, ident), [in_, ident], [out])

    def act(self, out, in_, func, bias=None, scale=None, accum_out=None):
        kw = {}
        if bias is not None:
            kw["bias"] = bias
        if scale is not None:
            kw["scale"] = scale
        if accum_out is not None:
            kw["accum_out"] = accum_out
        wr = [out] + ([accum_out] if accum_out is not None else [])
        return self.op("act", lambda e: e.activation(out=out, in_=in_, func=func, **kw), [in_, bias, scale], wr)

    def tt(self, e, out, in0, in1, op):
        return self.op(e, lambda g: g.tensor_tensor(out=out, in0=in0, in1=in1, op=op), [in0, in1], [out])

    def ts(self, e, out, in0, s1, op0, s2=None, op1=None, accum_out=None):
        kw = {}
        if op1 is not None:
            kw["op1"] = op1
        if accum_out is not None:
            kw["accum_out"] = accum_out
        wr = [out] + ([accum_out] if accum_out is not None else [])
        return self.op(e, lambda g: g.tensor_scalar(out=out, in0=in0, scalar1=s1, scalar2=s2, op0=op0, **kw),
                       [in0, s1, s2], wr)

    def stt(self, e, out, in0, scalar, in1, op0, op1):
        return self.op(e, lambda g: g.scalar_tensor_tensor(out=out, in0=in0, scalar=scalar, in1=in1, op0=op0, op1=op1),
                       [in0, scalar, in1], [out])

    def copy(self, e, out, in_):
        if e == "act":
            return self.op(e, lambda g: g.copy(out=out, in_=in_), [in_], [out])
        return self.op(e, lambda g: g.tensor_copy(out=out, in_=in_), [in_], [out])

    def memset(self, e, ap, val):
        return self.op(e, lambda g: g.memset(ap, val), [], [ap])

    def reduce(self, e, out, in_, op, axis=AX.X):
        return self.op(e, lambda g: g.tensor_reduce(out=out, in_=in_, axis=axis, op=op), [in_], [out])

    def finish(self):
        self.barrier()
        while len(self.stack) > 1:
            self.stack.pop().close()
        self.es.close()


class Pool:
    def __init__(self, P, name, n, shape, dt, psum=False):
        self.t = [(P.ps if psum else P.sb)(f"{name}{i}", shape, dt) for i in range(n)]
        self.i = 0

    def next(self):
        t = self.t[self.i % len(self.t)]
        self.i += 1
        return t


SEQ = 8192
NT_ALL = SEQ // 128
NT_HALF = TOK // 128
A_PROJ = 4112
NEG = -30000.0
SWA_TILES = NT_HALF
SWA_LEVEL = 9


class Builder:
    def __init__(self, stages, n_all=SEQ, t_ffn=512):
        self.stages = stages
        self.TF = t_ffn
        self.nc = bass.Bass("TRN2", target_bir_lowering=False)
        self.P = Prog(self.nc)
        self.inputs = {}

    def din(self, name, shape, dt=F32):
        if name in self.inputs:
            return self.inputs[name]
        ap = self.P.dram(name, shape, dt, "ExternalInput")
        self.inputs[name] = ap
        return ap

    def const_sb(self, name, shape, dt=F32, q="sp"):
        d = self.din(name, shape)
        t = self.P.sb("c_" + name, shape, dt)
        self.P.dma(q if dt == F32 else "pool", t[:], d)
        return t

    def setup_common(self):
        P = self.P
        self.ident_bf = self.const_sb("ident", [128, 128], BF16)
        self.ident_f = P.sb("ident_f", [128, 128], F32)
        P.dma("sp", self.ident_f[:], self.inputs["ident"])
        self.bank = [P.ps(f"bank{i}", [128, 512], F32) for i in range(8)]
        self.gc_ffn = self.const_sb("ffn_gcols", [128, 64])
        self.gc_mix = self.const_sb("mix_gcols", [128, 32])
        self.flag = self.const_sb("flag", [128, 2])

    def rmsnorm_T(self, x_tok, gcol, hT, s, tr_bank):
        P = self.P
        junk = self.junk.next()
        ssq = self.small.next()
        P.act(junk[:], x_tok[:], AF.Square, accum_out=ssq[:, 0:1])
        P.ts("dve", ssq[:, 1:2], ssq[:, 0:1], 1.0 / D, ALU.mult, EPS, ALU.add)
        P.act(ssq[:, 3:4], ssq[:, 1:2], AF.Ln)
        P.act(ssq[:, 2:3], ssq[:, 3:4], AF.Exp, scale=-0.5)
        xn = self.xn.next()
        P.ts("dve", xn[:], x_tok[:], ssq[:, 2:3], ALU.mult)
        if hT is None:
            return xn
        self.rms_transpose(xn, gcol, hT, s, tr_bank)

    def rms_transpose(self, xn, gcol, hT, s, tr_bank):
        P = self.P
        pst = tr_bank[:].bitcast(BF16)
        for c in range(8):
            P.tr(pst[:, c * 128:(c + 1) * 128], xn[:, c * 128:(c + 1) * 128], self.ident_bf[:])
        P.tt("dve", hT[:, :, s * 128:(s + 1) * 128], pst.rearrange("p (c t) -> p c t", c=8),
             gcol.unsqueeze(2).to_broadcast([128, 8, 128]), ALU.mult)

    def norm_pools(self):
        P = self.P
        self.junk = Pool(P, "junk", 1, [128, D], BF16)
        self.small = Pool(P, "small", 4, [128, 4], F32)
        self.xn = Pool(P, "xn", 1, [128, D], BF16)

    def ffn_pass(self, idx, src, dst, ntok, final=False):
        P = self.P
        P.push()
        TF = self.TF
        NS = TF // 128
        self.junk = Pool(P, "junk", 1, [128, D], BF16)
        self.small = Pool(P, "small", 4, [128, 4], F32)
        xtok = Pool(P, "xtok", 6, [128, D], F32)
        self.xn = Pool(P, "xnf", 2, [128, D], BF16)
        hTp = Pool(P, "hT", 2, [128, 8, TF], BF16)
        aTp = Pool(P, "aT", 1, [128, NJ, TF], BF16)
        sgp = Pool(P, "sg", 1, [128, 512], F32)
        wg = self.din(f"wg{idx}", [D, DFF])
        wu = self.din(f"wu{idx}", [D, DFF])
        wd = self.din(f"wd{idx}", [DFF, D])
        HJ = NJ // 2
        wgs = [P.sb(f"wg{g}", [128, 8, HJ * 128], BF16) for g in range(2)]
        wus = [P.sb(f"wu{g}", [128, 8, HJ * 128], BF16) for g in range(2)]
        wd_sb = P.sb("wdsb", [128, NJ, D], BF16)
        for g in range(2):
            for wsb, wdr in ((wgs[g], wg), (wus[g], wu)):
                for c in range(0, 8, 2):
                    P.dma("pool", wsb[:, c:c + 2, :],
                          wdr[c * 128:(c + 2) * 128, g * HJ * 128:(g + 1) * HJ * 128].rearrange("(c p) m -> p c m", p=128))
        for j0 in range(0, NJ, 2):
            P.dma("pool", wd_sb[:, j0:j0 + 2, :], wd[j0 * 128:(j0 + 2) * 128, :].rearrange("(j p) m -> p j m", p=128))
        gcol = self.gc_ffn[:, idx * 8:(idx + 1) * 8]
        if final:
            fg = self.const_sb("final_bc", [128, D])
        sk, dk = src.tensor.name, dst.tensor.name
        NTT = ntok // TF
        tiles = {}

        def stats(t, s):
            if s == 0:
                tiles[t] = (hTp.next(), [], [])
            xt_ = xtok.next()
            r0_ = t * TF + s * 128
            P.dma("sp", xt_[:], src[r0_:r0_ + 128, :], in_key=f"{sk}:{r0_ // 128}")
            tiles[t][1].append(xt_)
            tiles[t][2].append(self.rmsnorm_T(xt_, gcol, None, s, None))

        def transposes(t, s):
            self.rms_transpose(tiles[t][2][s], gcol, tiles[t][0], s, self.bank[4 + 2 * (s % 2)])

        for s in range(NS):
            stats(0, s)
            transposes(0, s)
        for t in range(NTT):
            hT, xts, _ = tiles[t]
            nxt = t + 1 < NTT
            aT = aTp.next()
            for j in range(NJ):
                g, jj = j // HJ, j % HJ
                pg = self.bank[j % 2]
                pu = self.bank[2 + j % 2]
                for c in range(8):
                    P.mm(pg[:, :TF], wgs[g][:, c, jj * 128:(jj + 1) * 128], hT[:, c, :], start=(c == 0), stop=(c == 7))
                for c in range(8):
                    P.mm(pu[:, :TF], wus[g][:, c, jj * 128:(jj + 1) * 128], hT[:, c, :], start=(c == 0), stop=(c == 7))
                sg = sgp.next()
                P.act(sg[:, :TF], pg[:, :TF], AF.Silu)
                P.tt("dve", aT[:, j, :], sg[:, :TF], pu[:, :TF], ALU.mult)
            if nxt:
                stats(t + 1, 0)
                stats(t + 1, 1)
            for s in range(NS):
                xt = xts[s]
                for hf in range(2):
                    py = self.bank[4 + 2 * (s % 2) + hf]
                    for j in range(NJ):
                        P.mm(py[:, :], aT[:, j, s * 128:(s + 1) * 128], wd_sb[:, j, hf * 512:(hf + 1) * 512],
                             start=(j == 0), stop=(j == NJ - 1))
                    P.stt("dve", xt[:, hf * 512:(hf + 1) * 512], py[:, :], 0.5, xt[:, hf * 512:(hf + 1) * 512],
                          ALU.mult, ALU.add)
                r0 = t * TF + s * 128
                if final:
                    junk = self.junk.next()
                    ssq = self.small.next()
                    P.act(junk[:], xt[:], AF.Square, accum_out=ssq[:, 0:1])
                    P.ts("dve", ssq[:, 1:2], ssq[:, 0:1], 1.0 / D, ALU.mult, EPS, ALU.add)
                    P.act(ssq[:, 3:4], ssq[:, 1:2], AF.Ln)
                    P.act(ssq[:, 2:3], ssq[:, 3:4], AF.Exp, scale=-0.5)
                    P.stt("dve", xt[:], xt[:], ssq[:, 2:3], fg[:], ALU.mult, ALU.mult)
                P.dma("sp", dst[r0:r0 + 128, :], xt[:], out_key=f"{dk}:{r0 // 128}")
                if nxt:
                    transposes(t + 1, s)
                    if s < 2:
                        stats(t + 1, s + 2)
            tiles.pop(t)
        P.pop()

    def dn_pass(self, l, src, dst, ntiles):
        P = self.P
        P.push()
        bank = self.bank
        self.norm_pools()
        w_in_d = self.din(f"a_w_in{l}", [D, A_PROJ])
        w_out_d = self.din(f"a_w_out{l}", [D, D])
        w_in = P.sb("win", [128, 8, A_PROJ], BF16)
        w_out = P.sb("wout", [128, 8, D], BF16)
        for c in range(8):
            P.dma("pool", w_in[:, c, :], w_in_d[c * 128:(c + 1) * 128, :])
        for c in range(8):
            P.dma("pool", w_out[:, c, :], w_out_d[c * 128:(c + 1) * 128, :])
        convw = self.const_sb(f"convw{l}", [128, 24, 4])
        dtb = self.const_sb(f"dtb{l}", [128, 8])
        alog = self.const_sb(f"alog{l}", [128, 8])
        onorm = self.const_sb(f"onorm{l}", [128, 128])
        masks = self.const_sb("masks", [128, 3, 128])
        cm = self.const_sb("cmats", [128, 4, 128])
        ones = self.const_sb("ones", [128, 128])
        gcol = self.gc_mix[:, l * 8:(l + 1) * 8]
        negA = P.sb("negA", [128, 8], F32)
        P.act(negA[:], alog[:], AF.Exp)
        P.ts("dve", negA[:], negA[:], -1.0, ALU.mult)
        S_f = [P.sb(f"Sf{h}", [128, 128], F32) for h in range(8)]
        S_b = [P.sb(f"Sb{h}", [128, 128], BF16) for h in range(8)]
        for h in range(8):
            P.memset("dve", S_f[h][:], 0.0)
            P.memset("dve", S_b[h][:], 0.0)
        xc = P.sb("xc", [128, 24, 131], F32)
        P.memset("dve", xc[:], 0.0)
        xtm = Pool(P, "xtm", 2, [128, D], F32)
        hTp = Pool(P, "hTm", 2, [128, 8, 128], BF16)
        accp = Pool(P, "cacc", 2, [128, 4, 128], F32)
        dd = P.sb("dd", [128, 16, 128], F32)
        ctmp = P.sb("ctmp", [128, 128], F32)
        qkv_s = P.sb("qkvs", [128, 24, 128], BF16)
        qkv_t = P.sb("qkvt", [128, 24, 128], BF16)
        sq = dd[:]
        sc = P.sb("sc", [128, 16, 8], F32)
        (RN_Q, RN_K, BETA, G, GC, GLAST, EG, EK, CKB, CKBEG, CKDEC, CQDEC, TMP, TMP2, ORS, Z) = range(16)
        glbe = P.sb("glbe", [128, 2, 8], F32)
        names = ["kn", "kb", "qn", "qdec", "vb", "kbeg", "kdec"]
        big = P.sb("big", [128, 4096], BF16)
        sop = {n: big[:, k_ * 1024:(k_ + 1) * 1024].rearrange("p (h t) -> p h t", h=8)
               for k_, n in enumerate(names[:4])}
        for n in names[4:]:
            sop[n] = P.sb("so_" + n, [128, 8, 128], BF16)[:]
        diagG = dd[:, 0:8, :]
        dtarg = dd[:, 8:16, :]
        e1 = dtarg
        e2 = diagG
        DTs = P.sb("DTs", [128, 8, 128], F32)
        DTi = P.sb("DTi", [128, 8, 128], BF16)
        Ds = P.sb("Ds", [128, 8, 128], F32)
        fms = [P.sb(f"fm{q}", [128, 3, 128], BF16) for q in range(4)]
        qdecT = P.sb("qdecT", [128, 8, 128], BF16)
        qkT = P.sb("qkT", [128, 8, 128], BF16)
        Mxs = [P.sb(f"Mx{q}", [128, 128], F32) for q in range(4)]
        Axs = [P.sb(f"Ax{q}", [128, 128], F32) for q in range(4)]
        Xs = [P.sb(f"Xx{q}", [128, 128], F32) for q in range(4)]
        Xbs = [P.sb(f"Xb{q}", [128, 128], BF16) for q in range(4)]
        Pns = [[P.sb(f"Pn{q}_{k_}", [128, 2, 128], F32) for k_ in range(2)] for q in range(4)]
        u_all = P.sb("uall", [128, 8, 128], F32)
        wT_all = P.sb("wTall", [128, 8, 128], BF16)
        vnew = P.sb("vnew", [128, 8, 128], BF16)
        o_tok = big[:, 0:2048].bitcast(F32).rearrange("p (h t) -> p h t", h=8)
        sgp = Pool(P, "sgate", 2, [128, D], BF16)
        ofin = big[:, 2048:3072]
        ofinT = big[:, 3072:4096].rearrange("p (h t) -> p h t", h=8)
        sk, dk = src.tensor.name, dst.tensor.name

        def bc(col):
            return sc[:, col, :].unsqueeze(2).to_broadcast([128, 8, 128])

        def front_a(i):
            st = {}

            def s0():
                st["xt"] = xtm.next()
                P.dma("sp", st["xt"][:], src[i * 128:(i + 1) * 128, :], in_key=f"{sk}:{i}")
                st["hT"] = hTp.next()
                self.rmsnorm_T(st["xt"], gcol, st["hT"], 0, bank[7])
                st["sg"] = sgp.next()

            def proj_pe(g):
                hT = st["hT"]
                pb = bank[4 + g % 2]
                for mm_ in range(4):
                    m = 4 * g + mm_
                    for c in range(8):
                        P.mm(pb[:, mm_ * 128:(mm_ + 1) * 128], w_in[:, c, m * 128:(m + 1) * 128], hT[:, c, :],
                             start=(c == 0), stop=(c == 7))

            def proj_post(g):
                pb = bank[4 + g % 2]
                P.act(xc[:, 4 * g:4 * g + 4, 3:131], pb[:].rearrange("p (m t) -> p m t", m=4), AF.Copy)
                acc = accp.next()
                for mm_ in range(4):
                    m = 4 * g + mm_
                    P.act(acc[:, mm_, :], pb[:, mm_ * 128:(mm_ + 1) * 128], AF.Copy, scale=convw[:, m, 3:4])
                for mm_ in range(4):
                    m = 4 * g + mm_
                    for tp in range(3):
                        P.stt("dve", acc[:, mm_, :], xc[:, m, tp:tp + 128], convw[:, m, tp:tp + 1], acc[:, mm_, :],
                              ALU.mult, ALU.add)
                P.act(qkv_s[:, 4 * g:4 * g + 4, :], acc[:], AF.Silu)
                if g == 5:
                    P.copy("pool", xc[:, :, 0:3], xc[:, :, 128:131])

            def gate_pe(hf):
                hT = st["hT"]
                pgt = bank[6 + hf]
                for c in range(8):
                    P.mm(pgt[:, :], hT[:, c, :], w_in[:, c, 3072 + hf * 512:3072 + (hf + 1) * 512],
                         start=(c == 0), stop=(c == 7))

            def gate_post(hf):
                P.act(st["sg"][:, hf * 512:(hf + 1) * 512], bank[6 + hf][:, :], AF.Silu)

            sl = [s0, lambda: proj_pe(0)]
            for g in range(1, 6):
                sl.append(lambda g=g: (proj_post(g - 1), proj_pe(g)))
            sl.append(lambda: (proj_post(5), gate_pe(0), gate_pe(1)))
            sl.append(lambda: (gate_post(0), gate_post(1)))
            return st, sl

        st_next, sl0 = front_a(0)
        for f_ in sl0:
            f_()
        for i in range(ntiles):
            st_cur = st_next
            xt, hT, sgate = st_cur["xt"], st_cur["hT"], st_cur["sg"]
            pending = []
            if i + 1 < ntiles:
                st_next, pending = front_a(i + 1)
            for t3 in range(3):
                pst = bank[2 + t3][:].bitcast(BF16)
                for h in range(8):
                    P.tr(pst[:, h * 128:(h + 1) * 128], qkv_s[:, t3 * 8 + h, :], self.ident_bf[:])
                P.copy("act" if t3 != 1 else "dve", qkv_t[:, t3 * 8:(t3 + 1) * 8, :],
                       pst.rearrange("p (h t) -> p h t", h=8))
            P.tt("dve", sq, qkv_t[:, 0:16, :], qkv_t[:, 0:16, :], ALU.mult)
            rn = sc[:, RN_Q:RN_K + 1, :]
            P.reduce("dve", rn, sq.rearrange("p (a h) t -> p a h t", a=2), ALU.add)
            P.ts("dve", rn, rn, EPS, ALU.add)
            P.act(rn, rn, AF.Ln)
            P.act(rn, rn, AF.Exp, scale=-0.5)
            P.ts("dve", sc[:, RN_Q, :], sc[:, RN_Q, :], 128.0 ** -0.5, ALU.mult)
            pba = bank[5]
            for c in range(8):
                P.mm(pba[:, 0:16], hT[:, c, :], w_in[:, c, 4096:4112], start=(c == 0), stop=(c == 7))
            P.act(sc[:, BETA, :], pba[:, 0:8], AF.Exp, scale=-1.0)
            P.ts("dve", sc[:, BETA, :], sc[:, BETA, :], 1.0, ALU.add)
            P.op("dve", lambda g_: g_.reciprocal(out=sc[:, BETA, :], in_=sc[:, BETA, :]), [sc], [sc])
            P.tt("dve", sc[:, Z, :], pba[:, 8:16], dtb[:], ALU.add)
            P.act(sc[:, Z, :], sc[:, Z, :], AF.Exp)
            P.ts("dve", sc[:, Z, :], sc[:, Z, :], 1.0, ALU.add)
            P.act(sc[:, Z, :], sc[:, Z, :], AF.Ln)
            P.tt("dve", sc[:, G, :], sc[:, Z, :], negA[:], ALU.mult)
            P.mm(pba[:, 16:24], cm[:, 0, :], sc[:, G, :])
            P.mm(pba[:, 24:32], cm[:, 1, :], sc[:, G, :])
            P.copy("dve", sc[:, GC:GLAST + 1, :], pba[:, 16:32].rearrange("p (a h) -> p a h", a=2))
            P.mm(pba[:, 32:40], cm[:, 2, :], sc[:, GC, :])
            P.mm(pba[:, 40:48], cm[:, 3, :], sc[:, GC, :])
            P.act(glbe[:], pba[:, 32:48].rearrange("p (a h) -> p a h", a=2), AF.Exp)
            P.act(sc[:, EG, :], sc[:, GC, :], AF.Exp)
            P.tt("dve", sc[:, TMP, :], sc[:, GLAST, :], sc[:, GC, :], ALU.subtract)
            P.act(sc[:, EK, :], sc[:, TMP, :], AF.Exp)
            P.tt("dve", sc[:, CKB, :], sc[:, RN_K, :], sc[:, BETA, :], ALU.mult)
            P.tt("dve", sc[:, CKBEG, :], sc[:, CKB, :], sc[:, EG, :], ALU.mult)
            P.tt("dve", sc[:, CKDEC, :], sc[:, RN_K, :], sc[:, EK, :], ALU.mult)
            P.tt("dve", sc[:, CQDEC, :], sc[:, RN_Q, :], sc[:, EG, :], ALU.mult)
            qt, kt, vt = qkv_t[:, 0:8, :], qkv_t[:, 8:16, :], qkv_t[:, 16:24, :]
            P.tt("dve", sop["kn"][:], kt, bc(RN_K), ALU.mult)
            P.tt("dve", sop["kb"][:], kt, bc(CKB), ALU.mult)
            P.tt("dve", sop["qn"][:], qt, bc(RN_Q), ALU.mult)
            P.tt("dve", sop["qdec"][:], qt, bc(CQDEC), ALU.mult)
            P.tt("pool", sop["vb"][:], vt, bc(BETA), ALU.mult)
            P.tt("pool", sop["kbeg"][:], kt, bc(CKBEG), ALU.mult)
            P.tt("pool", sop["kdec"][:], kt, bc(CKDEC), ALU.mult)
            P.tt("dve", diagG[:], self.ident_f[:].unsqueeze(1).to_broadcast([128, 8, 128]), bc(GC), ALU.mult)
            for hh in range(2):
                P.mm(bank[6 + hh][:, :], ones[:], diagG[:, 4 * hh:4 * hh + 4, :].rearrange("p h t -> p (h t)"))
                P.tt("dve", dtarg[:, 4 * hh:4 * hh + 4, :], bank[6 + hh][:].rearrange("p (h t) -> p h t", h=4),
                     sc[:, GC, 4 * hh:4 * hh + 4].unsqueeze(2).to_broadcast([128, 4, 128]), ALU.subtract)
            P.ts("dve", e2[:], dtarg[:], -1.0, ALU.mult, 0.0, ALU.min)
            P.act(e2[:], e2[:], AF.Exp)
            P.ts("dve", e1[:], dtarg[:], 0.0, ALU.min)
            P.act(e1[:], e1[:], AF.Exp)
            P.tt("dve", DTs[:], e1[:], masks[:, 0:1, :].to_broadcast([128, 8, 128]), ALU.mult)
            P.tt("dve", DTi[:], e1[:], masks[:, 1:2, :].to_broadcast([128, 8, 128]), ALU.mult)
            P.tt("dve", Ds[:], e2[:], masks[:, 2:3, :].to_broadcast([128, 8, 128]), ALU.mult)
            for hg in range(2):
                hs = [4 * hg + q for q in range(4)]
                for q, h in enumerate(hs):
                    pt = bank[q][:].bitcast(BF16)
                    for k_, n_ in enumerate(["kn", "kb", "qn", "qdec"]):
                        P.tr(pt[:, k_ * 128:(k_ + 1) * 128], sop[n_][:, h, :], self.ident_bf[:])
                if pending:
                    pending.pop(0)()
                for q, h in enumerate(hs):
                    pt = bank[q][:].bitcast(BF16)
                    P.copy("act", fms[q][:], pt[:, 0:384].rearrange("p (a t) -> p a t", a=3))
                    P.copy("act", qdecT[:, h, :], pt[:, 384:512])
                if pending:
                    pending.pop(0)()
                for q, h in enumerate(hs):
                    pg = bank[q]
                    fm = fms[q]
                    P.mm(pg[:, 0:256], fm[:, 0, :], fm[:, 1:3, :].rearrange("p a t -> p (a t)"))
                    P.mm(pg[:, 256:384], fm[:, 1, :], fm[:, 0, :])
                for q, h in enumerate(hs):
                    pg = bank[q]
                    P.tt("dve", Mxs[q][:], pg[:, 0:128], DTs[:, h, :], ALU.mult)
                    P.tt("dve", qkT[:, h, :], pg[:, 128:256], DTi[:, h, :], ALU.mult)
                    P.tt("dve", Axs[q][:], pg[:, 256:384], Ds[:, h, :], ALU.mult)
                    P.tt("dve", Xs[q][:], self.ident_f[:], Mxs[q][:], ALU.subtract)
                Pm = [Mxs[q][:] for q in range(4)]
                PTm = [Axs[q][:] for q in range(4)]
                for lvl in range(1, 6):
                    for q in range(4):
                        pp = bank[q]
                        if lvl < 5:
                            P.mm(pp[:, 0:128], PTm[q], Pm[q])
                        P.mm(pp[:, 128:256], Pm[q], PTm[q])
                    for q in range(4):
                        pp = bank[q]
                        pn = Pns[q][lvl % 2]
                        if lvl < 5:
                            P.copy("act", pn[:], pp[:, 0:256].rearrange("p (a t) -> p a t", a=2))
                        else:
                            P.copy("act", pn[:, 1, :], pp[:, 128:256])
                    if pending and lvl in (1, 3, 5):
                        pending.pop(0)()
                    for q in range(4):
                        pn = Pns[q][lvl % 2]
                        P.mm(bank[q][:, 256:384], pn[:, 1, :], Xs[q][:])
                    for q in range(4):
                        P.tt("dve", Xs[q][:], Xs[q][:], bank[q][:, 256:384], ALU.add)
                        pn = Pns[q][lvl % 2]
                        Pm[q], PTm[q] = pn[:, 0, :], pn[:, 1, :]
                for q in range(4):
                    P.copy("act", Xbs[q][:], Xs[q][:])
                for q, h in enumerate(hs):
                    pu = bank[q]
                    P.mm(pu[:, 0:128], Xbs[q][:], sop["vb"][:, h, :])
                    P.mm(pu[:, 128:256], sop["kbeg"][:, h, :], Xbs[q][:])
                for q, h in enumerate(hs):
                    pu = bank[q]
                    P.copy("act", u_all[:, h, :], pu[:, 0:128])
                    P.copy("act", wT_all[:, h, :], pu[:, 128:256])
            while pending:
                pending.pop(0)()
            for c in range(2):
                rows = slice(64 * c, 64 * c + 64)
                for h in range(8):
                    P.mm(bank[h][:, 0:128], wT_all[:, h, :], S_b[h][:])
                for h in range(8):
                    P.tt("dve", vnew[rows, h, :], u_all[rows, h, :], bank[h][rows, 0:128], ALU.subtract)
                for h in range(8):
                    P.mm(bank[h][:, 128:256], qdecT[:, h, :], S_b[h][:], start=True, stop=False)
                    P.mm(bank[h][:, 128:256], qkT[rows, h, :], vnew[rows, h, :], start=False, stop=True)
                    P.mm(bank[h][:, 256:384], sop["kdec"][rows, h, :], vnew[rows, h, :])
                for h in range(8):
                    P.stt("dve", S_f[h][:], S_f[h][:], glbe[:, c, h:h + 1], bank[h][:, 256:384], ALU.mult, ALU.add)
                    P.copy("act", S_b[h][:], S_f[h][:])
                    P.copy("act", o_tok[rows, h, :], bank[h][rows, 128:256])
            P.tt("dve", sq[:, 0:8, :], o_tok, o_tok, ALU.mult)
            P.reduce("dve", sc[:, ORS, :], sq[:, 0:8, :], ALU.add)
            P.ts("dve", sc[:, ORS, :], sc[:, ORS, :], 1.0 / 128, ALU.mult, EPS, ALU.add)
            P.act(sc[:, ORS, :], sc[:, ORS, :], AF.Ln)
            P.act(sc[:, ORS, :], sc[:, ORS, :], AF.Exp, scale=-0.5)
            P.tt("dve", o_tok, o_tok, bc(ORS), ALU.mult)
            P.tt("dve", o_tok, o_tok, onorm[:].unsqueeze(1).to_broadcast([128, 8, 128]), ALU.mult)
            P.tt("dve", ofin, o_tok.rearrange("p h t -> p (h t)"), sgate[:], ALU.mult)
            pst = bank[2][:].bitcast(BF16)
            for c in range(8):
                P.tr(pst[:, c * 128:(c + 1) * 128], ofin[:, c * 128:(c + 1) * 128], self.ident_bf[:])
            P.copy("act", ofinT, pst.rearrange("p (c t) -> p c t", c=8))
            for hf in range(2):
                py = bank[4 + hf]
                for c in range(8):
                    P.mm(py[:, :], ofinT[:, c, :], w_out[:, c, hf * 512:(hf + 1) * 512], start=(c == 0), stop=(c == 7))
                P.tt("dve", xt[:, hf * 512:(hf + 1) * 512], xt[:, hf * 512:(hf + 1) * 512], py[:, :], ALU.add)
            P.dma("sp", dst[i * 128:(i + 1) * 128, :], xt[:], out_key=f"{dk}:{i}")
        P.pop()

    def kv_pass(self, src, dst):
        P = self.P
        P.push()
        bank = self.bank
        self.norm_pools()
        kvw = self.din("kv_w", [D, 512])
        wk2 = P.sb("wk2", [128, 8, 4, 128], BF16)
        for hk in range(4):
            for dup in range(2):
                P.dma("pool", wk2[:, :, hk, dup * 64:(dup + 1) * 64],
                      kvw[:, hk * 64:(hk + 1) * 64].rearrange("(c p) m -> p c m", p=128))
        wv = P.sb("wv", [128, 8, 256], BF16)
        P.dma("pool", wv[:], kvw[:, 256:512].rearrange("(c p) m -> p c m", p=128))
        kb2 = self.const_sb("kbias2", [128, 4])
        vbc = self.const_sb("vbias_bc", [128, 256])
        self.KT2 = P.sb("KT2", [128, 4, (NT_HALF + 1) * 128], BF16)
        self.Vsb = P.sb("Vsb", [128, NT_HALF + 1, 256], BF16)
        gcol = self.const_sb("kv_gcol", [128, 8])
        xlp = Pool(P, "xl", 2, [128, D], F32)
        xhp = Pool(P, "xh", 2, [128, D], F32)
        hTp = Pool(P, "hTk", 2, [128, 8, 128], BF16)
        sk, dk = src.tensor.name, dst.tensor.name
        f, omf = self.flag[:, 0:1], self.flag[:, 1:2]
        for r in range(-1, NT_HALF):
            lo, hi = max(r, 0), NT_HALF + r
            xl, xh = xlp.next(), xhp.next()
            P.dma("sp", xl[:], src[lo * 128:(lo + 1) * 128, :], in_key=f"{sk}:{lo}")
            P.dma("sp", xh[:], src[hi * 128:(hi + 1) * 128, :], in_key=f"{sk}:{hi}")
            P.ts("dve", xl[:], xl[:], omf, ALU.mult)
            P.stt("dve", xl[:], xh[:], f, xl[:], ALU.mult, ALU.add)
            if r >= 0:
                P.dma("sp", dst[r * 128:(r + 1) * 128, :], xl[:], out_key=f"{dk}:{r}")
            hT = hTp.next()
            self.rmsnorm_T(xl, gcol[:], hT, 0, bank[7])
            for hk in range(4):
                pk = bank[hk % 2]
                for c in range(8):
                    P.mm(pk[:, 0:128], wk2[:, c, hk, :], hT[:, c, :], start=(c == 0), stop=(c == 7))
                P.ts("dve", self.KT2[:, hk, (r + 1) * 128:(r + 2) * 128], pk[:, 0:128], kb2[:, hk:hk + 1], ALU.add)
            pv = bank[2]
            for c in range(8):
                P.mm(pv[:, 0:256], hT[:, c, :], wv[:, c, :], start=(c == 0), stop=(c == 7))
            P.tt("dve", self.Vsb[:, r + 1, :], pv[:, 0:256], vbc[:], ALU.add)
        P.dma("sp", self.kt2_d, self.KT2[:].rearrange("p h k -> p (h k)"))
        P.dma("sp", self.vsb_d, self.Vsb[:].rearrange("p t d -> p (t d)"))
        P.pop()

    def swa_pass(self, l, j, src, dst):
        P = self.P
        P.push()
        bank = self.bank
        self.norm_pools()
        wq_d = self.din(f"b_w_q{j}", [D, D])
        wo_d = self.din(f"b_w_out{j}", [D, D])
        wq = P.sb("wq", [128, 8, D], BF16)
        wo = P.sb("wo", [128, 8, D], BF16)
        for c in range(8):
            P.dma("pool", wq[:, c, :], wq_d[c * 128:(c + 1) * 128, :])
        for c in range(8):
            P.dma("pool", wo[:, c, :], wo_d[c * 128:(c + 1) * 128, :])
        self.KT2 = P.sb("KT2", [128, 4, (NT_HALF + 1) * 128], BF16)
        self.Vsb = P.sb("Vsb", [128, NT_HALF + 1, 256], BF16)
        P.dma("sp", self.KT2[:].rearrange("p h k -> p (h k)"), self.kt2_d)
        P.dma("sp", self.Vsb[:].rearrange("p t d -> p (t d)"), self.vsb_d)
        bq = self.const_sb(f"bq{j}", [128, 8])
        sinks = self.const_sb(f"sinks{j}", [128, 16])
        mb = self.const_sb("maskbias", [128, 2, 256])
        gcol = self.gc_mix[:, l * 8:(l + 1) * 8]
        xtm = Pool(P, "xts", 2, [128, D], F32)
        hTp = Pool(P, "hTs", 2, [128, 8, 128], BF16)
        qT = P.sb("qT", [128, 8, 128], BF16)
        smp = Pool(P, "sm", 4, [128, 2, 256], F32)
        pp_ = Pool(P, "pp", 4, [128, 2, 256], F32)
        pnp = Pool(P, "pn", 4, [128, 2, 256], BF16)
        pTp = Pool(P, "pT", 4, [128, 4, 128], BF16)
        stp = Pool(P, "st", 4, [128, 8, 2], F32)
        oT = P.sb("oT", [128, 8, 128], BF16)
        sk, dk = src.tensor.name, dst.tensor.name
        for r in range(SWA_TILES):
            xt = xtm.next()
            P.dma("sp", xt[:], src[r * 128:(r + 1) * 128, :], in_key=f"{sk}:{r}")
            hT = hTp.next()
            self.rmsnorm_T(xt, gcol, hT, 0, bank[7])
            for m in range(8):
                pq = bank[m % 2]
                for c in range(8):
                    P.mm(pq[:, 0:128], wq[:, c, m * 128:(m + 1) * 128], hT[:, c, :], start=(c == 0), stop=(c == 7))
                P.ts("dve", qT[:, m, :], pq[:, 0:128], bq[:, m:m + 1], ALU.add, 0.125, ALU.mult)
            mrow = 0 if r == 0 else 1
            for gp in range(4):
                prs = [2 * gp, 2 * gp + 1]
                bufs = {}
                for k_, pr in enumerate(prs):
                    hk = pr // 2
                    for e in range(2):
                        P.mm(bank[2 + 2 * k_ + e][:, 0:256], qT[e * 64:(e + 1) * 64, pr, :],
                             self.KT2[e * 64:(e + 1) * 64, hk, r * 128:(r + 2) * 128])
                    bufs[pr] = (smp.next(), pp_.next(), pnp.next(), pTp.next(), stp.next())
                for k_, pr in enumerate(prs):
                    sm = bufs[pr][0]
                    for e in range(2):
                        P.tt("dve", sm[:, e, :], bank[2 + 2 * k_ + e][:, 0:256], mb[:, mrow, :], ALU.add)
                for pr in prs:
                    sm, p_, pn, pT, st = bufs[pr]
                    P.reduce("dve", st[:, 0, :], sm[:], ALU.max)
                    P.tt("dve", st[:, 0, :], st[:, 0, :], sinks[:, 2 * pr:2 * pr + 2], ALU.max)
                    P.ts("dve", st[:, 1, :], st[:, 0, :], -1.0, ALU.mult)
                    P.tt("dve", st[:, 3, :], sinks[:, 2 * pr:2 * pr + 2], st[:, 0, :], ALU.subtract)
                for pr in prs:
                    sm, p_, pn, pT, st = bufs[pr]
                    for e in range(2):
                        P.act(p_[:, e, :], sm[:, e, :], AF.Exp, bias=st[:, 1, e:e + 1], accum_out=st[:, 2, e:e + 1])
                    P.act(st[:, 3, :], st[:, 3, :], AF.Exp)
                for pr in prs:
                    sm, p_, pn, pT, st = bufs[pr]
                    P.tt("dve", st[:, 4, :], st[:, 3, :], st[:, 2, :], ALU.add)
                    P.op("dve", lambda g_, st=st: g_.reciprocal(out=st[:, 5, :], in_=st[:, 4, :]), [st], [st])
                    P.tt("dve", pn[:], p_[:], st[:, 5, :].unsqueeze(2).to_broadcast([128, 2, 256]), ALU.mult)
                for k_, pr in enumerate(prs):
                    sm, p_, pn, pT, st = bufs[pr]
                    ptb = bank[6 + k_][:].bitcast(BF16)
                    for e in range(2):
                        for kt in range(2):
                            q_ = e * 2 + kt
                            P.tr(ptb[:, q_ * 128:(q_ + 1) * 128], pn[:, e, kt * 128:(kt + 1) * 128], self.ident_bf[:])
                for k_, pr in enumerate(prs):
                    pT = bufs[pr][3]
                    ptb = bank[6 + k_][:].bitcast(BF16)
                    P.copy("act", pT[:], ptb[:, 0:512].rearrange("p (a t) -> p a t", a=4))
                for k_, pr in enumerate(prs):
                    pT = bufs[pr][3]
                    hk = pr // 2
                    po = bank[k_]
                    for e in range(2):
                        for kt in range(2):
                            P.mm(po[e * 64:(e + 1) * 64, 0:128], self.Vsb[:, r + kt, hk * 64:(hk + 1) * 64],
                                 pT[:, e * 2 + kt, :], start=(kt == 0), stop=(kt == 1))
                for k_, pr in enumerate(prs):
                    P.copy("act", oT[:, pr, :], bank[k_][:, 0:128])
            for hf in range(2):
                py = bank[hf]
                for c in range(8):
                    P.mm(py[:, :], oT[:, c, :], wo[:, c, hf * 512:(hf + 1) * 512], start=(c == 0), stop=(c == 7))
                P.tt("dve", xt[:, hf * 512:(hf + 1) * 512], xt[:, hf * 512:(hf + 1) * 512], py[:, :], ALU.add)
            P.dma("sp", dst[r * 128:(r + 1) * 128, :], xt[:], out_key=f"{dk}:{r}")
        P.pop()

    def build(self, out_tokens=TOK):
        P = self.P
        self.x_in = self.din("x", [SEQ, D])
        self.out = P.dram("out", [out_tokens, D], F32, "ExternalOutput")
        self.din("ident", [128, 128])
        self.setup_common()
        self.xs = P.dram("xs", [SEQ, D], F32, "Internal")
        self.xs2 = P.dram("xs2", [TOK, D], F32, "Internal")
        self.kt2_d = P.dram("kt2_d", [128, 4 * (NT_HALF + 1) * 128], BF16, "Internal")
        self.vsb_d = P.dram("vsb_d", [128, (NT_HALF + 1) * 256], BF16, "Internal")
        cur = self.x_in
        full = True
        for si, stg in enumerate(self.stages):
            last = si == len(self.stages) - 1
            kind = stg[0]
            if kind == "kv":
                dst = self.out if last else self.xs2
                self.kv_pass(cur, dst)
                full = False
            else:
                dst = self.out if last else (self.xs if full else self.xs2)
                if kind == "ffn":
                    self.ffn_pass(stg[1], cur, dst, SEQ if full else TOK, final=(len(stg) > 2 and stg[2]))
                elif kind == "dn":
                    self.dn_pass(stg[1], cur, dst, NT_ALL if full else NT_HALF)
                elif kind == "swa":
                    self.swa_pass(stg[1], stg[1] - 2, cur, dst)
            cur = dst
        P.finish()
        return self.nc


def colmajor(v, nchunk):
    return np.ascontiguousarray(np.asarray(v, dtype=np.float32).reshape(nchunk, 128).T)


def rep128(v):
    v = np.asarray(v, dtype=np.float32).reshape(1, -1)
    return np.ascontiguousarray(np.broadcast_to(v, (128, v.shape[1])))


def const_tables():
    idx = np.arange(128)
    same = (idx[:, None] // 64) == (idx[None, :] // 64)
    up_s = same & (idx[None, :] > idx[:, None])
    up_i = same & (idx[None, :] >= idx[:, None])
    lo_s = same & (idx[:, None] > idx[None, :])
    masks = np.stack([up_s, up_i, lo_s], axis=1).astype(np.float32)
    sel0 = np.zeros((128, 128), np.float32); sel0[63, :] = 1.0
    sel1 = np.zeros((128, 128), np.float32); sel1[127, :] = 1.0
    cm = np.stack([up_i.astype(np.float32), same.astype(np.float32), sel0, sel1], axis=1)
    qi = np.arange(128)[:, None]
    ki = np.arange(256)[None, :]
    in_win = (ki > qi) & (ki <= qi + 128)
    normal = np.where(in_win, 0.0, NEG).astype(np.float32)
    first = np.where(in_win & (ki >= 128), 0.0, NEG).astype(np.float32)
    return masks, cm, normal, first


def host_inputs(inputs, needed):
    g = lambda k: np.asarray(inputs[k], dtype=np.float32)
    masks, cm, normal, first = const_tables()
    common = {"ident": np.eye(128, dtype=np.float32), "masks": masks, "cmats": cm,
              "ones": np.ones((128, 128), np.float32)}
    if "ffn_gcols" in needed:
        common["ffn_gcols"] = np.ascontiguousarray(
            np.concatenate([colmajor(g("ffn_norm")[i // 2, i % 2], 8) for i in range(8)], axis=1))
    if "mix_gcols" in needed:
        common["mix_gcols"] = np.ascontiguousarray(
            np.concatenate([colmajor(g("mix_norm")[i], 8) for i in range(4)], axis=1))
    for i in range(8):
        for nm, src in (("wg", "ffn_w_gate"), ("wu", "ffn_w_up"), ("wd", "ffn_w_down")):
            if f"{nm}{i}" in needed:
                common[f"{nm}{i}"] = np.ascontiguousarray(g(src)[i // 2, i % 2])
    for l in range(2):
        if f"a_w_in{l}" in needed:
            common[f"a_w_in{l}"] = np.ascontiguousarray(g("a_w_in")[l])
            common[f"a_w_out{l}"] = np.ascontiguousarray(g("a_w_out")[l])
            cw = g("a_conv")[l]
            common[f"convw{l}"] = np.ascontiguousarray(cw.reshape(4, 24, 128).transpose(2, 1, 0))
            common[f"dtb{l}"] = rep128(g("a_dt_bias")[l])
            common[f"alog{l}"] = rep128(g("a_A_log")[l])
            common[f"onorm{l}"] = rep128(g("a_o_norm")[l])
    if "kv_w" in needed:
        common["kv_w"] = np.ascontiguousarray(g("kv_w"))
        kvb = g("kv_b")
        kb2 = np.stack([np.tile(kvb[hk * 64:(hk + 1) * 64], 2) for hk in range(4)], axis=1)
        common["kbias2"] = np.ascontiguousarray(kb2)
        common["vbias_bc"] = rep128(kvb[256:512])
        common["kv_gcol"] = colmajor(g("kv_norm"), 8)
    for j in range(2):
        if f"b_w_q{j}" in needed:
            common[f"b_w_q{j}"] = np.ascontiguousarray(g("b_w_q")[j])
            common[f"b_w_out{j}"] = np.ascontiguousarray(g("b_w_out")[j])
            common[f"bq{j}"] = colmajor(g("b_b_q")[j], 8)
            common[f"sinks{j}"] = rep128(g("b_sinks")[j])
    if "final_bc" in needed:
        common["final_bc"] = rep128(g("final_norm"))
    x = g("x")
    maps = []
    for c in range(NCORES):
        hf = c % 2
        m = dict(common)
        m["x"] = np.ascontiguousarray(x[c // 2])
        m["flag"] = np.ascontiguousarray(np.broadcast_to(np.array([[float(hf), 1.0 - hf]], np.float32), (128, 2)))
        m["maskbias"] = np.ascontiguousarray(np.stack([normal if hf == 1 else first, normal], axis=1))
        maps.append({k: v for k, v in m.items() if k in needed})
    return maps


FULL_STAGES = [("ffn", 0), ("dn", 0), ("ffn", 1), ("ffn", 2), ("dn", 1), ("ffn", 3), ("kv",),
               ("ffn", 4), ("swa", 2), ("ffn", 5), ("ffn", 6), ("swa", 3), ("ffn", 7, True)]


def run(stages, inputs, out_tokens=TOK, trace=False):
    b = Builder(stages)
    nc = b.build(out_tokens)
    maps = host_inputs(inputs, set(b.inputs))
    res = run_bass_kernel_spmd(nc, maps, core_ids=list(range(NCORES)), trace=trace)
    return [res.results[c]["out"] for c in range(NCORES)], res, b


def kernel(**inputs):
    outs, _, _ = run(FULL_STAGES, inputs)
    out = np.zeros((4, SEQ, D), np.float32)
    for c in range(NCORES):
        out[c // 2, (c % 2) * TOK:(c % 2 + 1) * TOK, :] = outs[c]
    return out
```

```python
import contextlib
import numpy as np
import concourse.bass as bass
import concourse.mybir as mybir
from concourse.bass_utils import run_bass_kernel_spmd

F32 = mybir.dt.float32
BF16 = mybir.dt.bfloat16
AF = mybir.ActivationFunctionType
ALU = mybir.AluOpType
AX = mybir.AxisListType

D = 1024
DFF = 2816
NJ = DFF // 128
TOK = 4096
EPS = 1e-6
NCORES = 8


class Tok:
    __slots__ = ("sem", "val", "eng", "sem_key")

    def __init__(self, sem, val, eng):
        self.sem, self.val, self.eng = sem, val, eng


class BufState:
    def __init__(self):
        self.last_w = None
        self.reads = {}


class Prog:
    def __init__(self, nc):
        self.nc = nc
        self.es = contextlib.ExitStack()
        self.eng = {"pe": nc.tensor, "act": nc.scalar, "dve": nc.vector, "pool": nc.gpsimd, "sp": nc.sync}
        self.sem = {}
        self.cnt = {}
        for e in self.eng:
            self.sem[e] = self.es.enter_context(nc.semaphore("sem_" + e))
            self.cnt[e] = 0
        self.waited = {e: {} for e in self.eng}
        self.bufs = {}
        self.dsems = {}
        self.n_ins = 0
        self._uid = 0
        self.stack = [self.es]
        self.semkey_of = {}

    def push(self):
        es = contextlib.ExitStack()
        self.stack.append(es)
        return es

    def pop(self):
        self.barrier()
        self.stack.pop().close()

    def barrier(self):
        for e in self.eng:
            for o in self.eng:
                if o != e and self.cnt[o] > 0:
                    self._wait(e, Tok(self.sem[o], self.cnt[o], o))
            for k, d in self.dsems.items():
                if d[1] > 0:
                    t = Tok(d[0], d[1], "dma")
                    t.sem_key = k
                    self._wait(e, t)

    def sb(self, name, shape, dt):
        self._uid += 1
        t = self.stack[-1].enter_context(self.nc.sbuf_tensor(f"{name}_u{self._uid}", list(shape), dt))
        self.semkey_of[t.name] = name
        return t

    def ps(self, name, shape, dt):
        return self.stack[-1].enter_context(self.nc.psum_tensor(name, list(shape), dt))

    def key(self, x):
        if isinstance(x, str):
            return x
        if isinstance(x, tuple):
            return x[1]
        return getattr(x, 'tensor', x).name

    def st(self, k):
        s = self.bufs.get(k)
        if s is None:
            s = self.bufs[k] = BufState()
        return s

    def _wait(self, e, tok, war=False):
        if tok is None:
            return
        if tok.eng == e and (e == "pe" or war or e == "sp"):
            return
        sid = id(tok.sem)
        val = tok.val
        if tok.eng == "dma":
            val = max(val, self.dsems[tok.sem_key][1])
        if self.waited[e].get(sid, 0) >= val:
            return
        self.eng[e].wait_ge(tok.sem, val)
        self.waited[e][sid] = val

    def _deps(self, e, reads, writes):
        for r in reads:
            s = self.st(self.key(r))
            self._wait(e, s.last_w)
        for w in writes:
            s = self.st(self.key(w))
            self._wait(e, s.last_w)
            for t in s.reads.values():
                self._wait(e, t, war=True)

    def _commit(self, tok, reads, writes):
        for r in reads:
            s = self.st(self.key(r))
            s.reads[tok.eng if tok.eng != "dma" else id(tok.sem)] = tok
        for w in writes:
            s = self.st(self.key(w))
            s.last_w = tok
            s.reads = {}

    def op(self, e, fn, reads, writes):
        reads = [r for r in reads if r is not None and not isinstance(r, (int, float))]
        self._deps(e, reads, writes)
        ins = fn(self.eng[e])
        self.cnt[e] += 1
        ins.then_inc(self.sem[e], 1)
        tok = Tok(self.sem[e], self.cnt[e], e)
        self._commit(tok, reads, writes)
        self.n_ins += 1
        return ins

    def dma(self, q, out, in_, out_key=None, in_key=None, sem_key=None):
        ok = out_key or self.key(out)
        ik = in_key or self.key(in_)
        sb_side = out.tensor.name if out.tensor.name not in self.dram_names else in_.tensor.name
        sk = sem_key or self.semkey_of.get(sb_side, sb_side)
        if sk not in self.dsems:
            self.dsems[sk] = [self.es.enter_context(self.nc.semaphore("d_" + sk.replace(":", "_"))), 0]
        d = self.dsems[sk]
        self._deps(q, [ik], [ok])
        ins = self.eng[q].dma_start(out=out, in_=in_)
        d[1] += 16
        ins.then_inc(d[0], 16)
        tok = Tok(d[0], d[1], "dma")
        tok_key = sk
        tok.sem_key = tok_key
        self._commit(tok, [ik], [ok])
        self.n_ins += 1
        return ins

    dram_names = set()

    def dram(self, name, shape, dt, kind):
        self.dram_names.add(name)
        return self.nc.dram_tensor(name, list(shape), dt, kind=kind).ap()

    def mm(self, out, lhsT, rhs, start=True, stop=True, **kw):
        return self.op("pe", lambda e: e.matmul(out, lhsT, rhs, start=start, stop=stop, **kw), [lhsT, rhs], [out])

    def tr(self, out, in_, ident):
        return self.op("pe", lambda e: e.transpose(out, in_, ident), [in_, ident], [out])

    def act(self, out, in_, func, bias=None, scale=None, accum_out=None):
        kw = {}
        if bias is not None:
            kw["bias"] = bias
        if scale is not None:
            kw["scale"] = scale
        if accum_out is not None:
            kw["accum_out"] = accum_out
        wr = [out] + ([accum_out] if accum_out is not None else [])
        return self.op("act", lambda e: e.activation(out=out, in_=in_, func=func, **kw), [in_, bias, scale], wr)

    def tt(self, e, out, in0, in1, op):
        return self.op(e, lambda g: g.tensor_tensor(out=out, in0=in0, in1=in1, op=op), [in0, in1], [out])

    def ts(self, e, out, in0, s1, op0, s2=None, op1=None, accum_out=None):
        kw = {}
        if op1 is not None:
            kw["op1"] = op1
        if accum_out is not None:
            kw["accum_out"] = accum_out
        wr = [out] + ([accum_out] if accum_out is not None else [])
        return self.op(e, lambda g: g.tensor_scalar(out=out, in0=in0, scalar1=s1, scalar2=s2, op0=op0, **kw),
                       [in0, s1, s2], wr)

    def stt(self, e, out, in0, scalar, in1, op0, op1):
        return self.op(e, lambda g: g.scalar_tensor_tensor(out=out, in0=in0, scalar=scalar, in1=in1, op0=op0, op1=op1),
                       [in0, scalar, in1], [out])

    def copy(self, e, out, in_):
        if e == "act":
            return self.op(e, lambda g: g.copy(out=out, in_=in_), [in_], [out])
        return self.op(e, lambda g: g.tensor_copy(out=out, in_=in_), [in_], [out])

    def memset(self, e, ap, val):
        return self.op(e, lambda g: g.memset(ap, val), [], [ap])

    def reduce(self, e, out, in_, op, axis=AX.X):
        return self.op(e, lambda g: g.tensor_reduce(out=out, in_=in_, axis=axis, op=op), [in_], [out])

    def finish(self):
        self.barrier()
        while len(self.stack) > 1:
            self.stack.pop().close()
        self.es.close()


class Pool:
    def __init__(self, P, name, n, shape, dt, psum=False):
        self.t = [(P.ps if psum else P.sb)(f"{name}{i}", shape, dt) for i in range(n)]
        self.i = 0

    def next(self):
        t = self.t[self.i % len(self.t)]
        self.i += 1
        return t


SEQ = 8192
NT_ALL = SEQ // 128
NT_HALF = TOK // 128
A_PROJ = 4112
NEG = -30000.0
SWA_TILES = NT_HALF
SWA_LEVEL = 9
N_SO = 28


class Builder:
    def __init__(self, stages, n_all=SEQ, t_ffn=512):
        self.stages = stages
        self.TF = t_ffn
        self.nc = bass.Bass("TRN2", target_bir_lowering=False)
        self.P = Prog(self.nc)
        self.inputs = {}

    def din(self, name, shape, dt=F32):
        if name in self.inputs:
            return self.inputs[name]
        ap = self.P.dram(name, shape, dt, "ExternalInput")
        self.inputs[name] = ap
        return ap

    def const_sb(self, name, shape, dt=F32, q="sp"):
        d = self.din(name, shape)
        t = self.P.sb("c_" + name, shape, dt)
        self.P.dma(q if dt == F32 else "pool", t[:], d)
        return t

    def setup_common(self):
        P = self.P
        self.ident_bf = self.const_sb("ident", [128, 128], BF16)
        self.ident_f = P.sb("ident_f", [128, 128], F32)
        P.dma("sp", self.ident_f[:], self.inputs["ident"])
        self.bank = [P.ps(f"bank{i}", [128, 512], F32) for i in range(8)]
        self.gc_ffn = self.const_sb("ffn_gcols", [128, 64])
        self.gc_mix = self.const_sb("mix_gcols", [128, 32])
        self.flag = self.const_sb("flag", [128, 2])

    def rmsnorm_T(self, x_tok, gcol, hT, s, tr_bank):
        P = self.P
        junk = self.junk.next()
        ssq = self.small.next()
        P.act(junk[:], x_tok[:], AF.Square, accum_out=ssq[:, 0:1])
        P.ts("dve", ssq[:, 1:2], ssq[:, 0:1], 1.0 / D, ALU.mult, EPS, ALU.add)
        P.act(ssq[:, 3:4], ssq[:, 1:2], AF.Ln)
        P.act(ssq[:, 2:3], ssq[:, 3:4], AF.Exp, scale=-0.5)
        xn = self.xn.next()
        P.ts("dve", xn[:], x_tok[:], ssq[:, 2:3], ALU.mult)
        if hT is None:
            return xn
        self.rms_transpose(xn, gcol, hT, s, tr_bank)

    def rms_transpose(self, xn, gcol, hT, s, tr_bank):
        P = self.P
        pst = tr_bank[:].bitcast(BF16)
        for c in range(8):
            P.tr(pst[:, c * 128:(c + 1) * 128], xn[:, c * 128:(c + 1) * 128], self.ident_bf[:])
        P.tt("dve", hT[:, :, s * 128:(s + 1) * 128], pst.rearrange("p (c t) -> p c t", c=8),
             gcol.unsqueeze(2).to_broadcast([128, 8, 128]), ALU.mult)

    def norm_pools(self):
        P = self.P
        self.junk = Pool(P, "junk", 1, [128, D], BF16)
        self.small = Pool(P, "small", 4, [128, 4], F32)
        self.xn = Pool(P, "xn", 1, [128, D], BF16)

    def ffn_pass(self, idx, src, dst, ntok, final=False, tok0=0):
        P = self.P
        P.push()
        TF = self.TF
        NS = TF // 128
        self.junk = Pool(P, "junk", 1, [128, D], BF16)
        self.small = Pool(P, "small", 4, [128, 4], F32)
        xtok = Pool(P, "xtok", 6, [128, D], F32)
        self.xn = Pool(P, "xnf", 2, [128, D], BF16)
        hTp = Pool(P, "hT", 2, [128, 8, TF], BF16)
        aTp = Pool(P, "aT", 1, [128, NJ, TF], BF16)
        sgp = Pool(P, "sg", 1, [128, 512], F32)
        wg = self.din(f"wg{idx}", [D, DFF])
        wu = self.din(f"wu{idx}", [D, DFF])
        wd = self.din(f"wd{idx}", [DFF, D])
        HJ = NJ // 2
        wgs = [P.sb(f"wg{g}", [128, 8, HJ * 128], BF16) for g in range(2)]
        wus = [P.sb(f"wu{g}", [128, 8, HJ * 128], BF16) for g in range(2)]
        wd_sb = P.sb("wdsb", [128, NJ, D], BF16)
        for g in range(2):
            for wsb, wdr in ((wgs[g], wg), (wus[g], wu)):
                for c in range(0, 8, 2):
                    P.dma("pool", wsb[:, c:c + 2, :],
                          wdr[c * 128:(c + 2) * 128, g * HJ * 128:(g + 1) * HJ * 128].rearrange("(c p) m -> p c m", p=128))
        for j0 in range(0, NJ, 2):
            P.dma("pool", wd_sb[:, j0:j0 + 2, :], wd[j0 * 128:(j0 + 2) * 128, :].rearrange("(j p) m -> p j m", p=128))
        gcol = self.gc_ffn[:, idx * 8:(idx + 1) * 8]
        if final:
            fg = self.const_sb("final_bc", [128, D])
        sk, dk = src.tensor.name, dst.tensor.name
        NTT = (ntok - tok0) // TF
        tiles = {}

        def stats(t, s):
            if s == 0:
                tiles[t] = (hTp.next(), [], [])
            xt_ = xtok.next()
            r0_ = tok0 + t * TF + s * 128
            P.dma("sp", xt_[:], src[r0_:r0_ + 128, :], in_key=f"{sk}:{r0_ // 128}")
            tiles[t][1].append(xt_)
            tiles[t][2].append(self.rmsnorm_T(xt_, gcol, None, s, None))

        def transposes(t, s):
            self.rms_transpose(tiles[t][2][s], gcol, tiles[t][0], s, self.bank[4 + 2 * (s % 2)])

        for s in range(NS):
            stats(0, s)
            transposes(0, s)
        for t in range(NTT):
            hT, xts, _ = tiles[t]
            nxt = t + 1 < NTT
            aT = aTp.next()
            for j in range(NJ):
                g, jj = j // HJ, j % HJ
                pg = self.bank[j % 2]
                pu = self.bank[2 + j % 2]
                for c in range(8):
                    P.mm(pg[:, :TF], wgs[g][:, c, jj * 128:(jj + 1) * 128], hT[:, c, :], start=(c == 0), stop=(c == 7))
                for c in range(8):
                    P.mm(pu[:, :TF], wus[g][:, c, jj * 128:(jj + 1) * 128], hT[:, c, :], start=(c == 0), stop=(c == 7))
                sg = sgp.next()
                P.act(sg[:, :TF], pg[:, :TF], AF.Silu)
                P.tt("dve", aT[:, j, :], sg[:, :TF], pu[:, :TF], ALU.mult)
            if nxt:
                stats(t + 1, 0)
                stats(t + 1, 1)
            for s in range(NS):
                xt = xts[s]
                for hf in range(2):
                    py = self.bank[4 + 2 * (s % 2) + hf]
                    for j in range(NJ):
                        P.mm(py[:, :], aT[:, j, s * 128:(s + 1) * 128], wd_sb[:, j, hf * 512:(hf + 1) * 512],
                             start=(j == 0), stop=(j == NJ - 1))
                    P.stt("dve", xt[:, hf * 512:(hf + 1) * 512], py[:, :], 0.5, xt[:, hf * 512:(hf + 1) * 512],
                          ALU.mult, ALU.add)
                r0 = tok0 + t * TF + s * 128
                if final:
                    junk = self.junk.next()
                    ssq = self.small.next()
                    P.act(junk[:], xt[:], AF.Square, accum_out=ssq[:, 0:1])
                    P.ts("dve", ssq[:, 1:2], ssq[:, 0:1], 1.0 / D, ALU.mult, EPS, ALU.add)
                    P.act(ssq[:, 3:4], ssq[:, 1:2], AF.Ln)
                    P.act(ssq[:, 2:3], ssq[:, 3:4], AF.Exp, scale=-0.5)
                    P.stt("dve", xt[:], xt[:], ssq[:, 2:3], fg[:], ALU.mult, ALU.mult)
                P.dma("sp", dst[r0:r0 + 128, :], xt[:], out_key=f"{dk}:{r0 // 128}")
                if nxt:
                    transposes(t + 1, s)
                    if s < 2:
                        stats(t + 1, s + 2)
            tiles.pop(t)
        P.pop()

    def dn_pass(self, l, src, dst, ntiles, n_state_only=0):
        P = self.P
        P.push()
        bank = self.bank
        self.norm_pools()
        w_in_d = self.din(f"a_w_in{l}", [D, A_PROJ])
        w_out_d = self.din(f"a_w_out{l}", [D, D])
        w_in = P.sb("win", [128, 8, A_PROJ], BF16)
        w_out = P.sb("wout", [128, 8, D], BF16)
        for c in range(8):
            P.dma("pool", w_in[:, c, :], w_in_d[c * 128:(c + 1) * 128, :])
        for c in range(8):
            P.dma("pool", w_out[:, c, :], w_out_d[c * 128:(c + 1) * 128, :])
        convw = self.const_sb(f"convw{l}", [128, 24, 4])
        dtb = self.const_sb(f"dtb{l}", [128, 8])
        alog = self.const_sb(f"alog{l}", [128, 8])
        onorm = self.const_sb(f"onorm{l}", [128, 128])
        masks = self.const_sb("masks", [128, 3, 128])
        cm = self.const_sb("cmats", [128, 4, 128])
        ones = self.const_sb("ones", [128, 128])
        keep = self.const_sb("keep", [128, 1])
        gcol = self.gc_mix[:, l * 8:(l + 1) * 8]
        negA = P.sb("negA", [128, 8], F32)
        P.act(negA[:], alog[:], AF.Exp)
        P.ts("dve", negA[:], negA[:], -1.0, ALU.mult)
        S_f = [P.sb(f"Sf{h}", [128, 128], F32) for h in range(8)]
        S_b = [P.sb(f"Sb{h}", [128, 128], BF16) for h in range(8)]
        for h in range(8):
            P.memset("dve", S_f[h][:], 0.0)
            P.memset("dve", S_b[h][:], 0.0)
        xc = P.sb("xc", [128, 24, 131], F32)
        P.memset("dve", xc[:], 0.0)
        xtm = Pool(P, "xtm", 2, [128, D], F32)
        hTp = Pool(P, "hTm", 2, [128, 8, 128], BF16)
        accp = Pool(P, "cacc", 2, [128, 4, 128], F32)
        dd = P.sb("dd", [128, 16, 128], F32)
        ctmp = P.sb("ctmp", [128, 128], F32)
        qkv_s = P.sb("qkvs", [128, 24, 128], BF16)
        qkv_t = P.sb("qkvt", [128, 24, 128], BF16)
        sq = dd[:]
        sc = P.sb("sc", [128, 16, 8], F32)
        (RN_Q, RN_K, BETA, G, GC, GLAST, EG, EK, CKB, CKBEG, CKDEC, CQDEC, TMP, TMP2, ORS, Z) = range(16)
        glbe = P.sb("glbe", [128, 2, 8], F32)
        names = ["kn", "kb", "qn", "qdec", "vb", "kbeg", "kdec"]
        big = P.sb("big", [128, 4096], BF16)
        sop = {n: big[:, k_ * 1024:(k_ + 1) * 1024].rearrange("p (h t) -> p h t", h=8)
               for k_, n in enumerate(names[:4])}
        for n in names[4:]:
            sop[n] = P.sb("so_" + n, [128, 8, 128], BF16)[:]
        diagG = dd[:, 0:8, :]
        dtarg = dd[:, 8:16, :]
        e1 = dtarg
        e2 = diagG
        DTs = P.sb("DTs", [128, 8, 128], F32)
        DTi = P.sb("DTi", [128, 8, 128], BF16)
        Ds = P.sb("Ds", [128, 8, 128], F32)
        fms = [P.sb(f"fm{q}", [128, 3, 128], BF16) for q in range(4)]
        qdecT = P.sb("qdecT", [128, 8, 128], BF16)
        qkT = P.sb("qkT", [128, 8, 128], BF16)
        Mxs = [P.sb(f"Mx{q}", [128, 128], F32) for q in range(4)]
        Axs = [P.sb(f"Ax{q}", [128, 128], F32) for q in range(4)]
        Xs = [P.sb(f"Xx{q}", [128, 128], F32) for q in range(4)]
        Xbs = [P.sb(f"Xb{q}", [128, 128], BF16) for q in range(4)]
        Pns = [[P.sb(f"Pn{q}_{k_}", [128, 2, 128], F32) for k_ in range(2)] for q in range(4)]
        u_all = P.sb("uall", [128, 8, 128], F32)
        wT_all = P.sb("wTall", [128, 8, 128], BF16)
        vnew = P.sb("vnew", [128, 8, 128], BF16)
        o_tok = big[:, 0:2048].bitcast(F32).rearrange("p (h t) -> p h t", h=8)
        sgp = Pool(P, "sgate", 2, [128, D], BF16)
        ofin = big[:, 2048:3072]
        ofinT = big[:, 3072:4096].rearrange("p (h t) -> p h t", h=8)
        sk, dk = src.tensor.name, dst.tensor.name

        def bc(col):
            return sc[:, col, :].unsqueeze(2).to_broadcast([128, 8, 128])

        def front_a(i):
            st = {}

            def s0():
                st["xt"] = xtm.next()
                P.dma("sp", st["xt"][:], src[i * 128:(i + 1) * 128, :], in_key=f"{sk}:{i}")
                st["hT"] = hTp.next()
                self.rmsnorm_T(st["xt"], gcol, st["hT"], 0, bank[7])
                st["sg"] = sgp.next()

            def proj_pe(g):
                hT = st["hT"]
                pb = bank[4 + g % 2]
                for mm_ in range(4):
                    m = 4 * g + mm_
                    for c in range(8):
                        P.mm(pb[:, mm_ * 128:(mm_ + 1) * 128], w_in[:, c, m * 128:(m + 1) * 128], hT[:, c, :],
                             start=(c == 0), stop=(c == 7))

            def proj_post(g):
                pb = bank[4 + g % 2]
                P.act(xc[:, 4 * g:4 * g + 4, 3:131], pb[:].rearrange("p (m t) -> p m t", m=4), AF.Copy)
                acc = accp.next()
                for mm_ in range(4):
                    m = 4 * g + mm_
                    P.act(acc[:, mm_, :], pb[:, mm_ * 128:(mm_ + 1) * 128], AF.Copy, scale=convw[:, m, 3:4])
                for mm_ in range(4):
                    m = 4 * g + mm_
                    for tp in range(3):
                        P.stt("dve", acc[:, mm_, :], xc[:, m, tp:tp + 128], convw[:, m, tp:tp + 1], acc[:, mm_, :],
                              ALU.mult, ALU.add)
                P.act(qkv_s[:, 4 * g:4 * g + 4, :], acc[:], AF.Silu)
                if g == 5:
                    P.copy("pool", xc[:, :, 0:3], xc[:, :, 128:131])

            def gate_pe(hf):
                hT = st["hT"]
                pgt = bank[6 + hf]
                for c in range(8):
                    P.mm(pgt[:, :], hT[:, c, :], w_in[:, c, 3072 + hf * 512:3072 + (hf + 1) * 512],
                         start=(c == 0), stop=(c == 7))

            def gate_post(hf):
                P.act(st["sg"][:, hf * 512:(hf + 1) * 512], bank[6 + hf][:, :], AF.Silu)

            sl = [s0, lambda: proj_pe(0)]
            for g in range(1, 6):
                sl.append(lambda g=g: (proj_post(g - 1), proj_pe(g)))
            if i < n_state_only:
                sl.append(lambda: proj_post(5))
            else:
                sl.append(lambda: (proj_post(5), gate_pe(0), gate_pe(1)))
                sl.append(lambda: (gate_post(0), gate_post(1)))
            return st, sl

        st_next, sl0 = front_a(0)
        for f_ in sl0:
            f_()
        for i in range(ntiles):
            st_cur = st_next
            xt, hT, sgate = st_cur["xt"], st_cur["hT"], st_cur["sg"]
            pending = []
            if i + 1 < ntiles:
                st_next, pending = front_a(i + 1)
            so = i < n_state_only
            if ntiles == NT_ALL and i == NT_HALF - 1:
                P.ts("dve", xc[:, :, 0:3], xc[:, :, 0:3], keep[:, 0:1], ALU.mult)
            if ntiles == NT_ALL and i == NT_HALF:
                for h in range(8):
                    P.ts("dve", S_f[h][:], S_f[h][:], keep[:, 0:1], ALU.mult)
                    P.ts("dve", S_b[h][:], S_b[h][:], keep[:, 0:1], ALU.mult)
            for t3 in range(3):
                pst = bank[2 + t3][:].bitcast(BF16)
                for h in range(8):
                    P.tr(pst[:, h * 128:(h + 1) * 128], qkv_s[:, t3 * 8 + h, :], self.ident_bf[:])
                P.copy("act" if t3 != 1 else "dve", qkv_t[:, t3 * 8:(t3 + 1) * 8, :],
                       pst.rearrange("p (h t) -> p h t", h=8))
            P.tt("dve", sq, qkv_t[:, 0:16, :], qkv_t[:, 0:16, :], ALU.mult)
            rn = sc[:, RN_Q:RN_K + 1, :]
            P.reduce("dve", rn, sq.rearrange("p (a h) t -> p a h t", a=2), ALU.add)
            P.ts("dve", rn, rn, EPS, ALU.add)
            P.act(rn, rn, AF.Ln)
            P.act(rn, rn, AF.Exp, scale=-0.5)
            P.ts("dve", sc[:, RN_Q, :], sc[:, RN_Q, :], 128.0 ** -0.5, ALU.mult)
            pba = bank[5]
            for c in range(8):
                P.mm(pba[:, 0:16], hT[:, c, :], w_in[:, c, 4096:4112], start=(c == 0), stop=(c == 7))
            P.act(sc[:, BETA, :], pba[:, 0:8], AF.Exp, scale=-1.0)
            P.ts("dve", sc[:, BETA, :], sc[:, BETA, :], 1.0, ALU.add)
            P.op("dve", lambda g_: g_.reciprocal(out=sc[:, BETA, :], in_=sc[:, BETA, :]), [sc], [sc])
            P.tt("dve", sc[:, Z, :], pba[:, 8:16], dtb[:], ALU.add)
            P.act(sc[:, Z, :], sc[:, Z, :], AF.Exp)
            P.ts("dve", sc[:, Z, :], sc[:, Z, :], 1.0, ALU.add)
            P.act(sc[:, Z, :], sc[:, Z, :], AF.Ln)
            P.tt("dve", sc[:, G, :], sc[:, Z, :], negA[:], ALU.mult)
            P.mm(pba[:, 16:24], cm[:, 0, :], sc[:, G, :])
            P.mm(pba[:, 24:32], cm[:, 1, :], sc[:, G, :])
            P.copy("dve", sc[:, GC:GLAST + 1, :], pba[:, 16:32].rearrange("p (a h) -> p a h", a=2))
            P.mm(pba[:, 32:40], cm[:, 2, :], sc[:, GC, :])
            P.mm(pba[:, 40:48], cm[:, 3, :], sc[:, GC, :])
            P.act(glbe[:], pba[:, 32:48].rearrange("p (a h) -> p a h", a=2), AF.Exp)
            P.act(sc[:, EG, :], sc[:, GC, :], AF.Exp)
            P.tt("dve", sc[:, TMP, :], sc[:, GLAST, :], sc[:, GC, :], ALU.subtract)
            P.act(sc[:, EK, :], sc[:, TMP, :], AF.Exp)
            P.tt("dve", sc[:, CKB, :], sc[:, RN_K, :], sc[:, BETA, :], ALU.mult)
            P.tt("dve", sc[:, CKBEG, :], sc[:, CKB, :], sc[:, EG, :], ALU.mult)
            P.tt("dve", sc[:, CKDEC, :], sc[:, RN_K, :], sc[:, EK, :], ALU.mult)
            P.tt("dve", sc[:, CQDEC, :], sc[:, RN_Q, :], sc[:, EG, :], ALU.mult)
            qt, kt, vt = qkv_t[:, 0:8, :], qkv_t[:, 8:16, :], qkv_t[:, 16:24, :]
            P.tt("dve", sop["kn"][:], kt, bc(RN_K), ALU.mult)
            P.tt("dve", sop["kb"][:], kt, bc(CKB), ALU.mult)
            P.tt("dve", sop["qn"][:], qt, bc(RN_Q), ALU.mult)
            P.tt("dve", sop["qdec"][:], qt, bc(CQDEC), ALU.mult)
            P.tt("pool", sop["vb"][:], vt, bc(BETA), ALU.mult)
            P.tt("pool", sop["kbeg"][:], kt, bc(CKBEG), ALU.mult)
            P.tt("pool", sop["kdec"][:], kt, bc(CKDEC), ALU.mult)
            P.tt("dve", diagG[:], self.ident_f[:].unsqueeze(1).to_broadcast([128, 8, 128]), bc(GC), ALU.mult)
            for hh in range(2):
                P.mm(bank[6 + hh][:, :], ones[:], diagG[:, 4 * hh:4 * hh + 4, :].rearrange("p h t -> p (h t)"))
                P.tt("dve", dtarg[:, 4 * hh:4 * hh + 4, :], bank[6 + hh][:].rearrange("p (h t) -> p h t", h=4),
                     sc[:, GC, 4 * hh:4 * hh + 4].unsqueeze(2).to_broadcast([128, 4, 128]), ALU.subtract)
            P.ts("dve", e2[:], dtarg[:], -1.0, ALU.mult, 0.0, ALU.min)
            P.act(e2[:], e2[:], AF.Exp)
            P.ts("dve", e1[:], dtarg[:], 0.0, ALU.min)
            P.act(e1[:], e1[:], AF.Exp)
            P.tt("dve", DTs[:], e1[:], masks[:, 0:1, :].to_broadcast([128, 8, 128]), ALU.mult)
            P.tt("dve", DTi[:], e1[:], masks[:, 1:2, :].to_broadcast([128, 8, 128]), ALU.mult)
            P.tt("dve", Ds[:], e2[:], masks[:, 2:3, :].to_broadcast([128, 8, 128]), ALU.mult)
            for hg in range(2):
                hs = [4 * hg + q for q in range(4)]
                for q, h in enumerate(hs):
                    pt = bank[q][:].bitcast(BF16)
                    for k_, n_ in enumerate(["kn", "kb", "qn", "qdec"]):
                        P.tr(pt[:, k_ * 128:(k_ + 1) * 128], sop[n_][:, h, :], self.ident_bf[:])
                if pending:
                    pending.pop(0)()
                for q, h in enumerate(hs):
                    pt = bank[q][:].bitcast(BF16)
                    P.copy("act", fms[q][:], pt[:, 0:384].rearrange("p (a t) -> p a t", a=3))
                    P.copy("act", qdecT[:, h, :], pt[:, 384:512])
                if pending:
                    pending.pop(0)()
                for q, h in enumerate(hs):
                    pg = bank[q]
                    fm = fms[q]
                    P.mm(pg[:, 0:256], fm[:, 0, :], fm[:, 1:3, :].rearrange("p a t -> p (a t)"))
                    P.mm(pg[:, 256:384], fm[:, 1, :], fm[:, 0, :])
                for q, h in enumerate(hs):
                    pg = bank[q]
                    P.tt("dve", Mxs[q][:], pg[:, 0:128], DTs[:, h, :], ALU.mult)
                    P.tt("dve", qkT[:, h, :], pg[:, 128:256], DTi[:, h, :], ALU.mult)
                    P.tt("dve", Axs[q][:], pg[:, 256:384], Ds[:, h, :], ALU.mult)
                    P.tt("dve", Xs[q][:], self.ident_f[:], Mxs[q][:], ALU.subtract)
                Pm = [Mxs[q][:] for q in range(4)]
                PTm = [Axs[q][:] for q in range(4)]
                for lvl in range(1, 6):
                    for q in range(4):
                        pp = bank[q]
                        if lvl < 5:
                            P.mm(pp[:, 0:128], PTm[q], Pm[q])
                        P.mm(pp[:, 128:256], Pm[q], PTm[q])
                    for q in range(4):
                        pp = bank[q]
                        pn = Pns[q][lvl % 2]
                        if lvl < 5:
                            P.copy("act", pn[:], pp[:, 0:256].rearrange("p (a t) -> p a t", a=2))
                        else:
                            P.copy("act", pn[:, 1, :], pp[:, 128:256])
                    if pending and lvl in (1, 3, 5):
                        pending.pop(0)()
                    for q in range(4):
                        pn = Pns[q][lvl % 2]
                        P.mm(bank[q][:, 256:384], pn[:, 1, :], Xs[q][:])
                    for q in range(4):
                        P.tt("dve", Xs[q][:], Xs[q][:], bank[q][:, 256:384], ALU.add)
                        pn = Pns[q][lvl % 2]
                        Pm[q], PTm[q] = pn[:, 0, :], pn[:, 1, :]
                for q in range(4):
                    P.copy("act", Xbs[q][:], Xs[q][:])
                for q, h in enumerate(hs):
                    pu = bank[q]
                    P.mm(pu[:, 0:128], Xbs[q][:], sop["vb"][:, h, :])
                    P.mm(pu[:, 128:256], sop["kbeg"][:, h, :], Xbs[q][:])
                for q, h in enumerate(hs):
                    pu = bank[q]
                    P.copy("act", u_all[:, h, :], pu[:, 0:128])
                    P.copy("act", wT_all[:, h, :], pu[:, 128:256])
            while pending:
                pending.pop(0)()
            for c in range(2):
                rows = slice(64 * c, 64 * c + 64)
                for h in range(8):
                    P.mm(bank[h][:, 0:128], wT_all[:, h, :], S_b[h][:])
                for h in range(8):
                    P.tt("dve", vnew[rows, h, :], u_all[rows, h, :], bank[h][rows, 0:128], ALU.subtract)
                for h in range(8):
                    if not so:
                        P.mm(bank[h][:, 128:256], qdecT[:, h, :], S_b[h][:], start=True, stop=False)
                        P.mm(bank[h][:, 128:256], qkT[rows, h, :], vnew[rows, h, :], start=False, stop=True)
                    P.mm(bank[h][:, 256:384], sop["kdec"][rows, h, :], vnew[rows, h, :])
                for h in range(8):
                    P.stt("dve", S_f[h][:], S_f[h][:], glbe[:, c, h:h + 1], bank[h][:, 256:384], ALU.mult, ALU.add)
                    P.copy("act", S_b[h][:], S_f[h][:])
                    if not so:
                        P.copy("act", o_tok[rows, h, :], bank[h][rows, 128:256])
            if so:
                continue
            P.tt("dve", sq[:, 0:8, :], o_tok, o_tok, ALU.mult)
            P.reduce("dve", sc[:, ORS, :], sq[:, 0:8, :], ALU.add)
            P.ts("dve", sc[:, ORS, :], sc[:, ORS, :], 1.0 / 128, ALU.mult, EPS, ALU.add)
            P.act(sc[:, ORS, :], sc[:, ORS, :], AF.Ln)
            P.act(sc[:, ORS, :], sc[:, ORS, :], AF.Exp, scale=-0.5)
            P.tt("dve", o_tok, o_tok, bc(ORS), ALU.mult)
            P.tt("dve", o_tok, o_tok, onorm[:].unsqueeze(1).to_broadcast([128, 8, 128]), ALU.mult)
            P.tt("dve", ofin, o_tok.rearrange("p h t -> p (h t)"), sgate[:], ALU.mult)
            pst = bank[2][:].bitcast(BF16)
            for c in range(8):
                P.tr(pst[:, c * 128:(c + 1) * 128], ofin[:, c * 128:(c + 1) * 128], self.ident_bf[:])
            P.copy("act", ofinT, pst.rearrange("p (c t) -> p c t", c=8))
            for hf in range(2):
                py = bank[4 + hf]
                for c in range(8):
                    P.mm(py[:, :], ofinT[:, c, :], w_out[:, c, hf * 512:(hf + 1) * 512], start=(c == 0), stop=(c == 7))
                P.tt("dve", xt[:, hf * 512:(hf + 1) * 512], xt[:, hf * 512:(hf + 1) * 512], py[:, :], ALU.add)
            P.dma("sp", dst[i * 128:(i + 1) * 128, :], xt[:], out_key=f"{dk}:{i}")
        P.pop()

    def kv_pass(self, src, dst):
        P = self.P
        P.push()
        bank = self.bank
        self.norm_pools()
        kvw = self.din("kv_w", [D, 512])
        wk2 = P.sb("wk2", [128, 8, 4, 128], BF16)
        for hk in range(4):
            for dup in range(2):
                P.dma("pool", wk2[:, :, hk, dup * 64:(dup + 1) * 64],
                      kvw[:, hk * 64:(hk + 1) * 64].rearrange("(c p) m -> p c m", p=128))
        wv = P.sb("wv", [128, 8, 256], BF16)
        P.dma("pool", wv[:], kvw[:, 256:512].rearrange("(c p) m -> p c m", p=128))
        kb2 = self.const_sb("kbias2", [128, 4])
        vbc = self.const_sb("vbias_bc", [128, 256])
        self.KT2 = P.sb("KT2", [128, 4, (NT_HALF + 1) * 128], BF16)
        self.Vsb = P.sb("Vsb", [128, NT_HALF + 1, 256], BF16)
        gcol = self.const_sb("kv_gcol", [128, 8])
        xlp = Pool(P, "xl", 2, [128, D], F32)
        xhp = Pool(P, "xh", 2, [128, D], F32)
        hTp = Pool(P, "hTk", 2, [128, 8, 128], BF16)
        sk, dk = src.tensor.name, dst.tensor.name
        f, omf = self.flag[:, 0:1], self.flag[:, 1:2]
        for r in range(-1, NT_HALF):
            lo, hi = max(r, 0), NT_HALF + r
            xl, xh = xlp.next(), xhp.next()
            P.dma("sp", xl[:], src[lo * 128:(lo + 1) * 128, :], in_key=f"{sk}:{lo}")
            P.dma("sp", xh[:], src[hi * 128:(hi + 1) * 128, :], in_key=f"{sk}:{hi}")
            P.ts("dve", xl[:], xl[:], omf, ALU.mult)
            P.stt("dve", xl[:], xh[:], f, xl[:], ALU.mult, ALU.add)
            if r >= 0:
                P.dma("sp", dst[r * 128:(r + 1) * 128, :], xl[:], out_key=f"{dk}:{r}")
            hT = hTp.next()
            self.rmsnorm_T(xl, gcol[:], hT, 0, bank[7])
            for hk in range(4):
                pk = bank[hk % 2]
                for c in range(8):
                    P.mm(pk[:, 0:128], wk2[:, c, hk, :], hT[:, c, :], start=(c == 0), stop=(c == 7))
                P.ts("dve", self.KT2[:, hk, (r + 1) * 128:(r + 2) * 128], pk[:, 0:128], kb2[:, hk:hk + 1], ALU.add)
            pv = bank[2]
            for c in range(8):
                P.mm(pv[:, 0:256], hT[:, c, :], wv[:, c, :], start=(c == 0), stop=(c == 7))
            P.tt("dve", self.Vsb[:, r + 1, :], pv[:, 0:256], vbc[:], ALU.add)
        P.dma("sp", self.kt2_d, self.KT2[:].rearrange("p h k -> p (h k)"))
        P.dma("sp", self.vsb_d, self.Vsb[:].rearrange("p t d -> p (t d)"))
        P.pop()

    def swa_pass(self, l, j, src, dst):
        P = self.P
        P.push()
        bank = self.bank
        self.norm_pools()
        wq_d = self.din(f"b_w_q{j}", [D, D])
        wo_d = self.din(f"b_w_out{j}", [D, D])
        wq = P.sb("wq", [128, 8, D], BF16)
        wo = P.sb("wo", [128, 8, D], BF16)
        for c in range(8):
            P.dma("pool", wq[:, c, :], wq_d[c * 128:(c + 1) * 128, :])
        for c in range(8):
            P.dma("pool", wo[:, c, :], wo_d[c * 128:(c + 1) * 128, :])
        self.KT2 = P.sb("KT2", [128, 4, (NT_HALF + 1) * 128], BF16)
        self.Vsb = P.sb("Vsb", [128, NT_HALF + 1, 256], BF16)
        P.dma("sp", self.KT2[:].rearrange("p h k -> p (h k)"), self.kt2_d)
        P.dma("sp", self.Vsb[:].rearrange("p t d -> p (t d)"), self.vsb_d)
        bq = self.const_sb(f"bq{j}", [128, 8])
        sinks = self.const_sb(f"sinks{j}", [128, 16])
        mb = self.const_sb("maskbias", [128, 2, 256])
        gcol = self.gc_mix[:, l * 8:(l + 1) * 8]
        xtm = Pool(P, "xts", 2, [128, D], F32)
        hTp = Pool(P, "hTs", 2, [128, 8, 128], BF16)
        qT = P.sb("qT", [128, 8, 128], BF16)
        smp = Pool(P, "sm", 4, [128, 2, 256], F32)
        pp_ = Pool(P, "pp", 4, [128, 2, 256], F32)
        pnp = Pool(P, "pn", 4, [128, 2, 256], BF16)
        pTp = Pool(P, "pT", 4, [128, 4, 128], BF16)
        stp = Pool(P, "st", 4, [128, 8, 2], F32)
        oT = P.sb("oT", [128, 8, 128], BF16)
        sk, dk = src.tensor.name, dst.tensor.name
        for r in range(SWA_TILES):
            xt = xtm.next()
            P.dma("sp", xt[:], src[r * 128:(r + 1) * 128, :], in_key=f"{sk}:{r}")
            hT = hTp.next()
            self.rmsnorm_T(xt, gcol, hT, 0, bank[7])
            for m in range(8):
                pq = bank[m % 2]
                for c in range(8):
                    P.mm(pq[:, 0:128], wq[:, c, m * 128:(m + 1) * 128], hT[:, c, :], start=(c == 0), stop=(c == 7))
                P.ts("dve", qT[:, m, :], pq[:, 0:128], bq[:, m:m + 1], ALU.add, 0.125, ALU.mult)
            mrow = 0 if r == 0 else 1
            for gp in range(4):
                prs = [2 * gp, 2 * gp + 1]
                bufs = {}
                for k_, pr in enumerate(prs):
                    hk = pr // 2
                    for e in range(2):
                        P.mm(bank[2 + 2 * k_ + e][:, 0:256], qT[e * 64:(e + 1) * 64, pr, :],
                             self.KT2[e * 64:(e + 1) * 64, hk, r * 128:(r + 2) * 128])
                    bufs[pr] = (smp.next(), pp_.next(), pnp.next(), pTp.next(), stp.next())
                for k_, pr in enumerate(prs):
                    sm = bufs[pr][0]
                    for e in range(2):
                        P.tt("dve", sm[:, e, :], bank[2 + 2 * k_ + e][:, 0:256], mb[:, mrow, :], ALU.add)
                for pr in prs:
                    sm, p_, pn, pT, st = bufs[pr]
                    P.reduce("dve", st[:, 0, :], sm[:], ALU.max)
                    P.tt("dve", st[:, 0, :], st[:, 0, :], sinks[:, 2 * pr:2 * pr + 2], ALU.max)
                    P.ts("dve", st[:, 1, :], st[:, 0, :], -1.0, ALU.mult)
                    P.tt("dve", st[:, 3, :], sinks[:, 2 * pr:2 * pr + 2], st[:, 0, :], ALU.subtract)
                for pr in prs:
                    sm, p_, pn, pT, st = bufs[pr]
                    for e in range(2):
                        P.act(p_[:, e, :], sm[:, e, :], AF.Exp, bias=st[:, 1, e:e + 1], accum_out=st[:, 2, e:e + 1])
                    P.act(st[:, 3, :], st[:, 3, :], AF.Exp)
                for pr in prs:
                    sm, p_, pn, pT, st = bufs[pr]
                    P.tt("dve", st[:, 4, :], st[:, 3, :], st[:, 2, :], ALU.add)
                    P.op("dve", lambda g_, st=st: g_.reciprocal(out=st[:, 5, :], in_=st[:, 4, :]), [st], [st])
                    P.tt("dve", pn[:], p_[:], st[:, 5, :].unsqueeze(2).to_broadcast([128, 2, 256]), ALU.mult)
                for k_, pr in enumerate(prs):
                    sm, p_, pn, pT, st = bufs[pr]
                    ptb = bank[6 + k_][:].bitcast(BF16)
                    for e in range(2):
                        for kt in range(2):
                            q_ = e * 2 + kt
                            P.tr(ptb[:, q_ * 128:(q_ + 1) * 128], pn[:, e, kt * 128:(kt + 1) * 128], self.ident_bf[:])
                for k_, pr in enumerate(prs):
                    pT = bufs[pr][3]
                    ptb = bank[6 + k_][:].bitcast(BF16)
                    P.copy("act", pT[:], ptb[:, 0:512].rearrange("p (a t) -> p a t", a=4))
                for k_, pr in enumerate(prs):
                    pT = bufs[pr][3]
                    hk = pr // 2
                    po = bank[k_]
                    for e in range(2):
                        for kt in range(2):
                            P.mm(po[e * 64:(e + 1) * 64, 0:128], self.Vsb[:, r + kt, hk * 64:(hk + 1) * 64],
                                 pT[:, e * 2 + kt, :], start=(kt == 0), stop=(kt == 1))
                for k_, pr in enumerate(prs):
                    P.copy("act", oT[:, pr, :], bank[k_][:, 0:128])
            for hf in range(2):
                py = bank[hf]
                for c in range(8):
                    P.mm(py[:, :], oT[:, c, :], wo[:, c, hf * 512:(hf + 1) * 512], start=(c == 0), stop=(c == 7))
                P.tt("dve", xt[:, hf * 512:(hf + 1) * 512], xt[:, hf * 512:(hf + 1) * 512], py[:, :], ALU.add)
            P.dma("sp", dst[r * 128:(r + 1) * 128, :], xt[:], out_key=f"{dk}:{r}")
        P.pop()

    def build(self, out_tokens=TOK):
        P = self.P
        self.x_in = self.din("x", [SEQ, D])
        self.out = P.dram("out", [out_tokens, D], F32, "ExternalOutput")
        self.din("ident", [128, 128])
        self.setup_common()
        self.xs = P.dram("xs", [SEQ, D], F32, "Internal")
        self.xs2 = P.dram("xs2", [TOK, D], F32, "Internal")
        self.kt2_d = P.dram("kt2_d", [128, 4 * (NT_HALF + 1) * 128], BF16, "Internal")
        self.vsb_d = P.dram("vsb_d", [128, (NT_HALF + 1) * 256], BF16, "Internal")
        cur = self.x_in
        full = True
        for si, stg in enumerate(self.stages):
            last = si == len(self.stages) - 1
            kind = stg[0]
            if kind == "kv":
                dst = self.out if last else self.xs2
                self.kv_pass(cur, dst)
                full = False
            else:
                dst = self.out if last else (self.xs if full else self.xs2)
                if kind == "ffn":
                    self.ffn_pass(stg[1], cur, dst, SEQ if full else TOK, final=(len(stg) > 2 and stg[2]),
                                  tok0=(N_SO * 128 if (full and stg[1] == 3) else 0))
                elif kind == "dn":
                    self.dn_pass(stg[1], cur, dst, NT_ALL if full else NT_HALF,
                                 n_state_only=(N_SO if (full and stg[1] == 1) else 0))
                elif kind == "swa":
                    self.swa_pass(stg[1], stg[1] - 2, cur, dst)
            cur = dst
        P.finish()
        return self.nc


def colmajor(v, nchunk):
    return np.ascontiguousarray(np.asarray(v, dtype=np.float32).reshape(nchunk, 128).T)


def rep128(v):
    v = np.asarray(v, dtype=np.float32).reshape(1, -1)
    return np.ascontiguousarray(np.broadcast_to(v, (128, v.shape[1])))


def const_tables():
    idx = np.arange(128)
    same = (idx[:, None] // 64) == (idx[None, :] // 64)
    up_s = same & (idx[None, :] > idx[:, None])
    up_i = same & (idx[None, :] >= idx[:, None])
    lo_s = same & (idx[:, None] > idx[None, :])
    masks = np.stack([up_s, up_i, lo_s], axis=1).astype(np.float32)
    sel0 = np.zeros((128, 128), np.float32); sel0[63, :] = 1.0
    sel1 = np.zeros((128, 128), np.float32); sel1[127, :] = 1.0
    cm = np.stack([up_i.astype(np.float32), same.astype(np.float32), sel0, sel1], axis=1)
    qi = np.arange(128)[:, None]
    ki = np.arange(256)[None, :]
    in_win = (ki > qi) & (ki <= qi + 128)
    normal = np.where(in_win, 0.0, NEG).astype(np.float32)
    first = np.where(in_win & (ki >= 128), 0.0, NEG).astype(np.float32)
    return masks, cm, normal, first


def host_inputs(inputs, needed):
    g = lambda k: np.asarray(inputs[k], dtype=np.float32)
    masks, cm, normal, first = const_tables()
    common = {"ident": np.eye(128, dtype=np.float32), "masks": masks, "cmats": cm,
              "ones": np.ones((128, 128), np.float32)}
    if "ffn_gcols" in needed:
        common["ffn_gcols"] = np.ascontiguousarray(
            np.concatenate([colmajor(g("ffn_norm")[i // 2, i % 2], 8) for i in range(8)], axis=1))
    if "mix_gcols" in needed:
        common["mix_gcols"] = np.ascontiguousarray(
            np.concatenate([colmajor(g("mix_norm")[i], 8) for i in range(4)], axis=1))
    for i in range(8):
        for nm, src in (("wg", "ffn_w_gate"), ("wu", "ffn_w_up"), ("wd", "ffn_w_down")):
            if f"{nm}{i}" in needed:
                common[f"{nm}{i}"] = np.ascontiguousarray(g(src)[i // 2, i % 2])
    for l in range(2):
        if f"a_w_in{l}" in needed:
            common[f"a_w_in{l}"] = np.ascontiguousarray(g("a_w_in")[l])
            common[f"a_w_out{l}"] = np.ascontiguousarray(g("a_w_out")[l])
            cw = g("a_conv")[l]
            common[f"convw{l}"] = np.ascontiguousarray(cw.reshape(4, 24, 128).transpose(2, 1, 0))
            common[f"dtb{l}"] = rep128(g("a_dt_bias")[l])
            common[f"alog{l}"] = rep128(g("a_A_log")[l])
            common[f"onorm{l}"] = rep128(g("a_o_norm")[l])
    if "kv_w" in needed:
        common["kv_w"] = np.ascontiguousarray(g("kv_w"))
        kvb = g("kv_b")
        kb2 = np.stack([np.tile(kvb[hk * 64:(hk + 1) * 64], 2) for hk in range(4)], axis=1)
        common["kbias2"] = np.ascontiguousarray(kb2)
        common["vbias_bc"] = rep128(kvb[256:512])
        common["kv_gcol"] = colmajor(g("kv_norm"), 8)
    for j in range(2):
        if f"b_w_q{j}" in needed:
            common[f"b_w_q{j}"] = np.ascontiguousarray(g("b_w_q")[j])
            common[f"b_w_out{j}"] = np.ascontiguousarray(g("b_w_out")[j])
            common[f"bq{j}"] = colmajor(g("b_b_q")[j], 8)
            common[f"sinks{j}"] = rep128(g("b_sinks")[j])
    if "final_bc" in needed:
        common["final_bc"] = rep128(g("final_norm"))
    x = g("x")
    maps = []
    for c in range(NCORES):
        hf = c % 2
        m = dict(common)
        xb = x[c // 2]
        m["x"] = np.ascontiguousarray(xb if hf == 1 else np.concatenate([xb[:TOK], xb[:TOK]], axis=0))
        m["flag"] = np.ascontiguousarray(np.broadcast_to(np.array([[1.0, 0.0]], np.float32), (128, 2)))
        m["keep"] = np.full((128, 1), float(hf), np.float32)
        m["maskbias"] = np.ascontiguousarray(np.stack([normal if hf == 1 else first, normal], axis=1))
        maps.append({k: v for k, v in m.items() if k in needed})
    return maps


FULL_STAGES = [("ffn", 0), ("dn", 0), ("ffn", 1), ("ffn", 2), ("dn", 1), ("ffn", 3), ("kv",),
               ("ffn", 4), ("swa", 2), ("ffn", 5), ("ffn", 6), ("swa", 3), ("ffn", 7, True)]


def run(stages, inputs, out_tokens=TOK, trace=False):
    b = Builder(stages)
    nc = b.build(out_tokens)
    maps = host_inputs(inputs, set(b.inputs))
    res = run_bass_kernel_spmd(nc, maps, core_ids=list(range(NCORES)), trace=trace)
    return [res.results[c]["out"] for c in range(NCORES)], res, b


def kernel(**inputs):
    outs, _, _ = run(FULL_STAGES, inputs)
    out = np.zeros((4, SEQ, D), np.float32)
    for c in range(NCORES):
        out[c // 2, (c % 2) * TOK:(c % 2 + 1) * TOK, :] = outs[c]
    return out
```
